# Optimizing a Trainium2 kernel written in Bass

```python
import math
import jax
import jax.numpy as jnp
from jax import lax
import numpy as np

D_MODEL = 4096
BATCH = 2
SEQ = 8192
DEPTH = 1

GRID_W = 64
CTX_LEN = 256
MIX_W = D_MODEL
ATT_W = MIX_W // 2
CONV_W = MIX_W - ATT_W
HEAD_DIM = 64
V_HEAD_DIM = 2 * HEAD_DIM
N_HEADS = ATT_W // V_HEAD_DIM
QK_W = N_HEADS * 2 * HEAD_DIM
V_W = N_HEADS * V_HEAD_DIM
PROJ_W = 2 * QK_W + V_W + 2 * CONV_W
CONV_K = 31
FFN_CONV_K = 3
D_FF = ((8 * D_MODEL // 3) + 255) // 256 * 256
Q_BLOCK = 128
ROPE_BASE = 10000.0
N_MOD = 6

kernel_name = "hybrid_diffattn_conformer_convffn_dit_block"


def rms_norm(x, g, eps=1e-6):
    xf = x.astype(jnp.float32)
    y = xf * lax.rsqrt(jnp.mean(xf * xf, axis=-1, keepdims=True) + eps)
    return (y * g.astype(jnp.float32)).astype(x.dtype)


def layer_norm(x, g, b, eps=1e-5):
    xf = x.astype(jnp.float32)
    mu = jnp.mean(xf, axis=-1, keepdims=True)
    var = jnp.mean(jnp.square(xf - mu), axis=-1, keepdims=True)
    y = (xf - mu) * lax.rsqrt(var + eps)
    return (y * g.astype(jnp.float32) + b.astype(jnp.float32)).astype(x.dtype)


def modulate(h, shift, scale):
    return h * (1 + scale) + shift


def dwconv(x, w, b):
    k = w.shape[0]
    pad = (k - 1) // 2
    y = lax.conv_general_dilated(
        x, w[:, None, :].astype(x.dtype), window_strides=(1,),
        padding=[(pad, pad)], dimension_numbers=('NWC', 'WIO', 'NWC'),
        feature_group_count=x.shape[-1])
    return y + b.astype(x.dtype)


def axial_rope_tables(n_tokens):
    rows = n_tokens // GRID_W
    row = jnp.broadcast_to(jnp.arange(rows)[:, None], (rows, GRID_W)).reshape(-1)
    col = jnp.broadcast_to(jnp.arange(GRID_W)[None, :], (rows, GRID_W)).reshape(-1)
    n_pairs_axis = HEAD_DIM // 4
    freqs = ROPE_BASE ** (-jnp.arange(n_pairs_axis, dtype=jnp.float32) / n_pairs_axis)
    ang = jnp.concatenate([row.astype(jnp.float32)[:, None] * freqs,
                           col.astype(jnp.float32)[:, None] * freqs], axis=-1)
    return jnp.cos(ang), jnp.sin(ang)


def apply_rope(x, cos, sin):
    xp = x.reshape(x.shape[:-1] + (HEAD_DIM // 2, 2))
    xe, xo = xp[..., 0], xp[..., 1]
    c = cos[None, :, None, None, :].astype(x.dtype)
    s = sin[None, :, None, None, :].astype(x.dtype)
    out = jnp.stack([xe * c - xo * s, xe * s + xo * c], axis=-1)
    return out.reshape(x.shape)


def split_proj(p):
    return jnp.split(p, [QK_W, 2 * QK_W, 2 * QK_W + V_W, 2 * QK_W + V_W + CONV_W], axis=-1)


def qk_heads(t, g):
    b, n = t.shape[:2]
    return rms_norm(t.reshape(b, n, N_HEADS, 2, HEAD_DIM), g)


def diff_lambda(lq1, lk1, lq2, lk2, lam_init):
    f = jnp.float32
    return (jnp.exp(jnp.sum(lq1.astype(f) * lk1.astype(f)))
            - jnp.exp(jnp.sum(lq2.astype(f) * lk2.astype(f))) + lam_init)


def diff_attn_blocks(q, k, v, lam):
    b, n = q.shape[:2]
    nb = n // Q_BLOCK
    qb = (q * (HEAD_DIM ** -0.5)).reshape(b, nb, Q_BLOCK, N_HEADS, 2, HEAD_DIM)
    qb = qb.transpose(1, 0, 3, 4, 2, 5)
    kt = k.transpose(0, 2, 3, 1, 4)
    vt = v.transpose(0, 2, 1, 3)

    def block(qblk):
        s = jnp.einsum('bhiqd,bhikd->bhiqk', qblk, kt).astype(jnp.float32)
        p = jax.nn.softmax(s, axis=-1)
        a = p[:, :, 0] - lam * p[:, :, 1]
        return jnp.einsum('bhqk,bhkd->bhqd', a.astype(vt.dtype), vt)

    o = lax.map(block, qb)
    return o.transpose(1, 0, 3, 2, 4).reshape(b, n, N_HEADS, V_HEAD_DIM)


def diff_attn_dense(q, k, v, lam):
    s = jnp.einsum('bqhid,bkhid->bhiqk', q * (HEAD_DIM ** -0.5), k).astype(jnp.float32)
    p = jax.nn.softmax(s, axis=-1)
    a = p[:, :, 0] - lam * p[:, :, 1]
    return jnp.einsum('bhqk,bkhd->bqhd', a.astype(v.dtype), v)


def diff_head_out(o, subln_g, lam_init):
    b, n = o.shape[:2]
    return (rms_norm(o, subln_g) * (1 - lam_init)).reshape(b, n, V_W)


def conformer_conv(ca, cg, dw_w, dw_b, ln_g, ln_b):
    u = ca * jax.nn.sigmoid(cg)
    u = dwconv(u, dw_w, dw_b)
    u = layer_norm(u, ln_g, ln_b)
    return jax.nn.silu(u)


def conv_ffn(h, w_up, dw_w, dw_b, w_down):
    up = dwconv(h @ w_up, dw_w, dw_b)
    u, g = jnp.split(up, 2, axis=-1)
    return (jax.nn.silu(g) * u) @ w_down


def setup_inputs(seed: int = 0) -> dict:
    key = jax.random.key(seed)
    ks = jax.random.split(key, 32)

    def nrm(k, shape, scale):
        return jax.random.normal(k, shape, jnp.float32) * scale

    L, D = DEPTH, D_MODEL
    return {
        "x": nrm(ks[0], (BATCH, SEQ, D), 1.0),
        "c": nrm(ks[1], (BATCH, D), 1.0),
        "ctx": nrm(ks[2], (BATCH, CTX_LEN, D), 1.0),
        "c_ctx": nrm(ks[3], (D,), 1.0),
        "w_ada": nrm(ks[4], (L, D, N_MOD * D), D ** -0.5),
        "b_ada": nrm(ks[5], (L, N_MOD * D), 0.02),
        "norm1_g": 1.0 + nrm(ks[6], (L, D), 0.05),
        "norm2_g": 1.0 + nrm(ks[7], (L, D), 0.05),
        "w_in": nrm(ks[8], (L, D, PROJ_W), D ** -0.5),
        "q_norm_g": 1.0 + nrm(ks[9], (L, HEAD_DIM), 0.05),
        "k_norm_g": 1.0 + nrm(ks[10], (L, HEAD_DIM), 0.05),
        "lambda_q1": nrm(ks[11], (L, HEAD_DIM), 0.1),
        "lambda_k1": nrm(ks[12], (L, HEAD_DIM), 0.1),
        "lambda_q2": nrm(ks[13], (L, HEAD_DIM), 0.1),
        "lambda_k2": nrm(ks[14], (L, HEAD_DIM), 0.1),
        "subln_g": 1.0 + nrm(ks[15], (L, V_HEAD_DIM), 0.05),
        "conv_dw_w": nrm(ks[16], (L, CONV_K, CONV_W), CONV_K ** -0.5),
        "conv_dw_b": nrm(ks[17], (L, CONV_W), 0.02),
        "conv_ln_g": 1.0 + nrm(ks[18], (L, CONV_W), 0.05),
        "conv_ln_b": nrm(ks[19], (L, CONV_W), 0.02),
        "w_out": nrm(ks[20], (L, MIX_W, D), MIX_W ** -0.5),
        "w_up": nrm(ks[21], (L, D, 2 * D_FF), D ** -0.5),
        "ffn_dw_w": nrm(ks[22], (L, FFN_CONV_K, 2 * D_FF), FFN_CONV_K ** -0.5),
        "ffn_dw_b": nrm(ks[23], (L, 2 * D_FF), 0.02),
        "w_down": nrm(ks[24], (L, D_FF, D), D_FF ** -0.5),
    }


def reference(x, c, ctx, c_ctx, w_ada, b_ada, norm1_g, norm2_g, w_in, q_norm_g, k_norm_g,
              lambda_q1, lambda_k1, lambda_q2, lambda_k2, subln_g, conv_dw_w, conv_dw_b,
              conv_ln_g, conv_ln_b, w_out, w_up, ffn_dw_w, ffn_dw_b, w_down):
    b, n, d = x.shape
    n_ctx = ctx.shape[1]
    cos, sin = axial_rope_tables(n)
    for i in range(DEPTH):
        update_ctx = i < DEPTH - 1
        lam_init = 0.8 - 0.6 * math.exp(-0.3 * i)
        lam = diff_lambda(lambda_q1[i], lambda_k1[i], lambda_q2[i], lambda_k2[i], lam_init)

        mod_x = (jax.nn.silu(c) @ w_ada[i] + b_ada[i]).reshape(b, N_MOD, d)
        mod_x = mod_x.transpose(1, 0, 2)[:, :, None, :]
        mod_c = (jax.nn.silu(c_ctx) @ w_ada[i] + b_ada[i]).reshape(N_MOD, 1, 1, d)

        h_x = modulate(rms_norm(x, norm1_g[i]), mod_x[0], mod_x[1])
        h_c = modulate(rms_norm(ctx, norm1_g[i]), mod_c[0], mod_c[1])

        q_x, k_x, v_x, ca_x, cg_x = split_proj(h_x @ w_in[i])
        q_x = apply_rope(qk_heads(q_x, q_norm_g[i]), cos, sin)
        k_x = apply_rope(qk_heads(k_x, k_norm_g[i]), cos, sin)
        v_x = v_x.reshape(b, n, N_HEADS, V_HEAD_DIM)

        if update_ctx:
            q_c, k_c, v_c, ca_c, cg_c = split_proj(h_c @ w_in[i])
        else:
            k_c, v_c = jnp.split(h_c @ w_in[i][:, QK_W:2 * QK_W + V_W], [QK_W], axis=-1)
        k_c = qk_heads(k_c, k_norm_g[i])
        v_c = v_c.reshape(b, n_ctx, N_HEADS, V_HEAD_DIM)

        o_x = diff_attn_blocks(q_x, jnp.concatenate([k_c, k_x], axis=1),
                               jnp.concatenate([v_c, v_x], axis=1), lam)
        att_x = diff_head_out(o_x, subln_g[i], lam_init)
        conv_x = conformer_conv(ca_x, cg_x, conv_dw_w[i], conv_dw_b[i], conv_ln_g[i], conv_ln_b[i])
        mix_x = jnp.concatenate([att_x, conv_x], axis=-1) @ w_out[i]

        if update_ctx:
            q_c = qk_heads(q_c, q_norm_g[i])
            att_c = diff_head_out(diff_attn_dense(q_c, k_c, v_c, lam), subln_g[i], lam_init)
            conv_c = conformer_conv(ca_c, cg_c, conv_dw_w[i], conv_dw_b[i], conv_ln_g[i], conv_ln_b[i])
            ctx = ctx + mod_c[2] * (jnp.concatenate([att_c, conv_c], axis=-1) @ w_out[i])
            h2_c = modulate(rms_norm(ctx, norm2_g[i]), mod_c[3], mod_c[4])
            ctx = ctx + mod_c[5] * conv_ffn(h2_c, w_up[i], ffn_dw_w[i], ffn_dw_b[i], w_down[i])

        x = x + mod_x[2] * mix_x

        h2_x = modulate(rms_norm(x, norm2_g[i]), mod_x[3], mod_x[4])
        x = x + mod_x[5] * conv_ffn(h2_x, w_up[i], ffn_dw_w[i], ffn_dw_b[i], w_down[i])
    return x
```

```python
from contextlib import ExitStack
import numpy as np
import ml_dtypes
import concourse.bass as bass
import concourse.mybir as mybir
from concourse.bass_utils import run_bass_kernel_spmd

F32 = mybir.dt.float32
BF16 = mybir.dt.bfloat16
ALU = mybir.AluOpType
AF = mybir.ActivationFunctionType

D = 4096
NCH = 32
SEQ = 8192
NCTX = 256
NT = 67
NS = NT * 128
NEXT = 2176
QO = 63
NQ = 2050
UO = 48
NU = 2080
DFF = 11008
NJ = 86
EPS6 = 1e-6
EPS5 = 1e-5
LAM_INIT = 0.2
ENGS = ("pe", "act", "dve", "pool", "sp")

_VOFF = {}
_o = 0
for _n, _w in (("g1", 32), ("g2", 32), ("bada", 192), ("gq", 1), ("gk", 1), ("gs", 1),
               ("cw", 31 * 16), ("cb", 16), ("lg", 16), ("lb", 16),
               ("fw", 3 * 172), ("fb", 172), ("cT", 32), ("ccT", 32), ("lam", 256)):
    _VOFF[_n] = _o
    _o += _w
NV = _o


class Buf:
    def __init__(self, ap, name):
        self.ap = ap
        self.name = name
        self.w_ev = {}
        self.r_ev = {}
        self.dsem = None
        self.dcount = 0

    def __getitem__(self, idx):
        return self.ap[idx]

    def full(self):
        return self.ap[(slice(None),) * len(self.ap.shape)]


class Op:
    __slots__ = ("eng", "fn", "waits", "signal", "semval", "dbuf", "seq")

    def __init__(self, eng, fn, seq):
        self.eng = eng
        self.fn = fn
        self.waits = {}
        self.signal = False
        self.semval = None
        self.dbuf = None
        self.seq = seq


class Prog:
    def __init__(self, nc):
        self.nc = nc
        self.ctx = ExitStack()
        self.pstack = None
        self.ops = {e: [] for e in ENGS}
        self.seq = {e: 0 for e in ENGS}
        self.ecount = {e: 0 for e in ENGS}
        self.waited = {e: {} for e in ENGS}
        self.esem = {e: self.ctx.enter_context(nc.semaphore("e_" + e)) for e in ENGS}
        self.nbuf = 0
        self.touched = {}
        self.bar_src = None
        self.bar_dst = None
        self.tok = None
        self.dbg = ()
        self.uid = 0

    def sbuf(self, name, shape, dtype, glob=False):
        st = self.ctx if (glob or self.pstack is None) else self.pstack
        self.uid += 1
        return Buf(st.enter_context(self.nc.sbuf_tensor("%s_%d" % (name, self.uid), list(shape), dtype)), name)

    def psum(self, name, shape, dtype=F32):
        st = self.ctx if self.pstack is None else self.pstack
        self.uid += 1
        return Buf(st.enter_context(self.nc.psum_tensor("%s_%d" % (name, self.uid), list(shape), dtype)), name)

    def dram(self, name, shape, dtype, kind="Internal"):
        if self.dbg and name in self.dbg:
            kind = "ExternalOutput"
        return Buf(self.nc.dram_tensor(name, list(shape), dtype, kind=kind), name)

    def op(self, eng, fn, reads=(), writes=(), dma=None, join=False):
        o = Op(eng, fn, self.seq[eng])
        self.seq[eng] += 1
        waits = o.waits

        def need(evd):
            for key, val in evd.items():
                if key[0] == "E":
                    if key[1] == eng and eng == "pe":
                        continue
                    old = waits.get(key)
                    if old is None or old.seq < val.seq:
                        waits[key] = val
                else:
                    waits[key] = key[1].dcount

        for r in reads:
            need(r.w_ev)
            self.touched[id(r)] = r
        for w in writes:
            if not join:
                need(w.w_ev)
            need(w.r_ev)
            self.touched[id(w)] = w
        for key, val in waits.items():
            if key[0] == "E":
                val.signal = True
        if dma is not None:
            if dma.dsem is None:
                dma.dsem = self.ctx.enter_context(self.nc.semaphore("d%d" % self.nbuf))
                self.nbuf += 1
            dma.dcount += 16
            key, val = ("D", dma), dma.dcount
            o.dbuf = dma
        else:
            key, val = ("E", eng), o
        for w in writes:
            if join:
                w.w_ev[key] = val
            else:
                w.w_ev = {key: val}
                w.r_ev = {}
        for r in reads:
            r.r_ev[key] = val
        self.ops[eng].append(o)
        return o

    def I(self, eng, meth, reads, writes, *a, dma=None, join=False, **kw):
        return self.op(eng, lambda e: getattr(e, meth)(*a, **kw), reads, writes, dma=dma, join=join)

    def wait_only(self, eng, bufs):
        o = Op(eng, None, self.seq[eng])
        self.seq[eng] += 1
        for b in bufs:
            for key, val in list(b.w_ev.items()) + list(b.r_ev.items()):
                if key[0] == "E":
                    if key[1] == eng:
                        continue
                    val.signal = True
                    old = o.waits.get(key)
                    if old is None or old.seq < val.seq:
                        o.waits[key] = val
                else:
                    o.waits[key] = key[1].dcount
        self.ops[eng].append(o)

    def barrier(self):
        bufs = list(self.touched.values())
        self.touched = {}
        src, dst, tok = self.bar_src, self.bar_dst, self.tok
        o = Op("sp", lambda e: e.dma_start(out=dst[0:1, 0:16], in_=src[0:1, 0:16]), self.seq["sp"])
        self.seq["sp"] += 1
        for b in bufs + [tok]:
            for key, val in list(b.w_ev.items()) + list(b.r_ev.items()):
                if key[0] == "E":
                    val.signal = True
                    old = o.waits.get(key)
                    if old is None or old.seq < val.seq:
                        o.waits[key] = val
                else:
                    o.waits[key] = key[1].dcount
        if tok.dsem is None:
            tok.dsem = self.ctx.enter_context(self.nc.semaphore("d%d" % self.nbuf))
            self.nbuf += 1
        tok.dcount += 16
        o.dbuf = tok
        self.ops["sp"].append(o)
        for b in bufs:
            b.w_ev = {}
            b.r_ev = {}
        tok.w_ev = {("D", tok): tok.dcount}
        tok.r_ev = {}
        for e in ENGS:
            if e != "sp":
                self.wait_only(e, [tok])

    def emit(self):
        nc = self.nc
        for e in ENGS:
            for o in self.ops[e]:
                if o.signal:
                    self.ecount[e] += 1
                    o.semval = self.ecount[e]
        prog = self

        def run(engname, engine):
            waited = prog.waited[engname]
            for o in prog.ops[engname]:
                for key, val in o.waits.items():
                    if key[0] == "E":
                        v = val.semval
                        sem = prog.esem[key[1]]
                    else:
                        v = val
                        sem = key[1].dsem
                    if waited.get(key, 0) >= v:
                        continue
                    waited[key] = v
                    engine.wait_ge(sem, v)
                if o.fn is None:
                    continue
                ins = o.fn(engine)
                if o.dbuf is not None:
                    ins.then_inc(o.dbuf.dsem, 16)
                elif o.signal:
                    ins.then_inc(prog.esem[engname], 1)

        with nc.Block() as block:
            @block.sync
            def _(e):
                run("sp", e)

            @block.tensor
            def _(e):
                run("pe", e)

            @block.scalar
            def _(e):
                run("act", e)

            @block.vector
            def _(e):
                run("dve", e)

            @block.gpsimd
            def _(e):
                run("pool", e)
        self.ops = {e: [] for e in ENGS}

    def phase_begin(self):
        self.pstack = ExitStack()

    def phase_end(self, final=False):
        if not final:
            self.barrier()
        self.emit()
        self.pstack.close()
        self.pstack = None


def build_program(stop_after=99, dbg=False):
    nc = bass.Bass("TRN2", target_bir_lowering=False)
    P = Prog(nc)
    P.dbg = dbg or ()
    print('sbuf bytes remaining', nc.sbuf_bytes_remaining)

    def din(name, shape, dt=F32):
        return Buf(nc.dram_tensor(name, list(shape), dt, kind="ExternalInput"), name)

    XS = din("xs", [NS, D])
    VEC = din("vecs", [128, NV])
    KB = din("keybias", [128, NT])
    VAL = din("valid", [128, NEXT])
    COS = din("cosT", [128, NS])
    SIN = din("sinT", [128, NS])
    CB = din("constb", [128, 4 * 128], BF16)
    CF = din("constf", [128, 3 * 128])
    WADA = din("w_ada", [D, 6 * D])
    WIN = din("w_in", [D, 10240])
    WOUT = din("w_out", [D, D])
    WUP = din("w_up", [D, 2 * DFF])
    WDN = din("w_down", [DFF, D])
    OUT = Buf(nc.dram_tensor("out", [2048, D], F32, kind="ExternalOutput"), "out")

    H1T = P.dram("h1t", [D, NS], BF16)
    XT = P.dram("xtT", [D, NEXT], F32)
    KT = P.dram("kt", [16, 128, NS], BF16)
    VS = P.dram("vs", [16, 128, NT, 128], BF16)
    QT = P.dram("qt", [16, 128, NQ], BF16)
    YS = P.dram("ys", [16, 128, NQ], F32)
    CAT = P.dram("cat", [32, 128, NQ], BF16)
    X1T = P.dram("x1t", [32, 128, NQ], F32)
    WUPB = P.dram("wupb", [43, 128, 32, 512], BF16)
    WDNB = P.dram("wdnb", [32, 128, NJ, 128], BF16)
    BARD = P.dram("bard", [1, 16], F32)
    P.bar_dst = BARD
    H1Tv = H1T.ap.rearrange("(c p) s -> p c s", p=128)
    XTv = XT.ap.rearrange("(c p) s -> p c s", p=128)
    WINv = WIN.ap.rearrange("(k p) n -> p k n", p=128)
    WADAv = WADA.ap.rearrange("(k p) n -> p k n", p=128)
    WOUTv = WOUT.ap.rearrange("(k p) n -> p k n", p=128)
    WUPv = WUP.ap.rearrange("(k p) n -> p k n", p=128)
    WDNv = WDN.ap.rearrange("(k p) n -> p k n", p=128)

    vec = P.sbuf("vec", [128, NV], F32, glob=True)
    cb = P.sbuf("cb", [128, 4 * 128], BF16, glob=True)
    cf = P.sbuf("cf", [128, 3 * 128], F32, glob=True)
    mod = P.sbuf("mod", [128, 192, 2], F32, glob=True)
    der = P.sbuf("der", [128, 8, 32], F32, glob=True)
    sml = P.sbuf("sml", [128, 16], F32, glob=True)
    kb = P.sbuf("kb", [128, NT], F32, glob=True)
    tok = P.sbuf("tok", [1, 16], F32, glob=True)
    rstd2 = P.sbuf("rstd2", [128, NQ], F32, glob=True)
    P.tok = tok
    P.bar_src = tok
    identb = cb.ap[:, 0:128]
    pswap = cb.ap[:, 128:256]
    ones64 = cb.ap[:, 256:384]
    onesb = cb.ap[:, 384:512]
    identf = cf.ap[:, 0:128]
    ones128f = cf.ap[:, 128:256]
    onesDf = cf.ap[:, 256:384]

    def V(name, i=0, w=1):
        o = _VOFF[name] + i
        return vec.ap[:, o:o + w]

    A1x, A1c, A2 = der.ap[:, 0, :], der.ap[:, 1, :], der.ap[:, 2, :]
    B1x, B1c = mod.ap[:, 0:32, 0], mod.ap[:, 0:32, 1]
    gate1, B2, gate2 = mod.ap[:, 64:96, 0], mod.ap[:, 96:128, 0], mod.ap[:, 160:192, 0]

    P.phase_begin()
    P.I("dve", "memset", [], [tok], tok[:, :], 0.0)
    for dst, src in ((vec, VEC), (cb, CB), (cf, CF), (kb, KB)):
        P.I("sp", "dma_start", [src], [dst], out=dst.full(), in_=src.full(), dma=dst)
    sc = P.sbuf("sc", [128, 32, 2], BF16)
    P.I("act", "activation", [vec], [sc], out=sc[:, :, 0], in_=V("cT", 0, 32), func=AF.Silu)
    P.I("act", "activation", [vec], [sc], out=sc[:, :, 1], in_=V("ccT", 0, 32), func=AF.Silu, join=True)
    wa = [P.sbuf("wa%d" % i, [128, 32, 1024], BF16) for i in range(2)]
    psmod = P.psum("psmod", [128, 256, 2])
    for nb in range(24):
        w = wa[nb % 2]
        P.I("pool", "dma_start", [WADA], [w], out=w[:, :, :], in_=WADAv[:, :, nb * 1024:(nb + 1) * 1024], dma=w)
        for m in range(8):
            for k in range(32):
                P.I("pe", "matmul", [w, sc], [psmod], psmod[:, nb * 8 + m, :], lhsT=w[:, k, m * 128:(m + 1) * 128],
                    rhs=sc[:, k, :], start=(k == 0), stop=(k == 31))
    for j in range(2):
        P.I("dve", "tensor_tensor", [psmod, vec], [mod], out=mod[:, :, j], in0=psmod[:, 0:192, j],
            in1=V("bada", 0, 192), op=ALU.add, join=True)
    for di, (sl, j, g) in enumerate(((slice(32, 64), 0, "g1"), (slice(32, 64), 1, "g1"), (slice(128, 160), 0, "g2"))):
        P.I("dve", "tensor_scalar", [mod], [der], out=der[:, 3, :], in0=mod[:, sl, j], scalar1=1.0, scalar2=None, op0=ALU.add)
        P.I("dve", "tensor_tensor", [der, vec], [der], out=der[:, di, :], in0=der[:, 3, :], in1=V(g, 0, 32), op=ALU.mult)
    P.I("dve", "tensor_scalar", [vec], [sml], out=sml[:, 0:1], in0=V("gq"), scalar1=0.125, scalar2=None, op0=ALU.mult)
    P.I("dve", "tensor_scalar", [vec], [sml], out=sml[:, 1:2], in0=V("gs"), scalar1=1.0 - LAM_INIT, scalar2=None, op0=ALU.mult)
    lamt = P.sbuf("lamt", [128, 128], F32)
    P.I("dve", "tensor_tensor", [vec], [lamt], out=lamt[:, 0:64], in0=V("lam", 0, 64), in1=V("lam", 64, 64), op=ALU.mult)
    P.I("dve", "tensor_tensor", [vec], [lamt], out=lamt[:, 64:128], in0=V("lam", 128, 64), in1=V("lam", 192, 64), op=ALU.mult)
    P.I("dve", "tensor_reduce", [lamt], [sml], out=sml[:, 3:4], in_=lamt[:, 0:64], axis=mybir.AxisListType.X, op=ALU.add)
    P.I("dve", "tensor_reduce", [lamt], [sml], out=sml[:, 4:5], in_=lamt[:, 64:128], axis=mybir.AxisListType.X, op=ALU.add)
    P.I("act", "activation", [sml], [sml], out=sml[:, 5:7], in_=sml[:, 3:5], func=AF.Exp)
    P.I("dve", "tensor_tensor", [sml], [sml], out=sml[:, 7:8], in0=sml[:, 6:7], in1=sml[:, 5:6], op=ALU.subtract)
    P.I("dve", "tensor_scalar", [sml], [sml], out=sml[:, 2:3], in0=sml[:, 7:8], scalar1=-LAM_INIT, scalar2=None, op0=ALU.add)
    P.phase_end()
    if stop_after <= 0:
        return _finish(P, nc, OUT, dbg, {"mod": mod, "sml": sml})

    P.phase_begin()
    xt = [P.sbuf("xt%d" % i, [128, D], F32) for i in range(2)]
    junk = P.sbuf("junk", [128, D], BF16)
    xsb = [P.sbuf("xsb%d" % i, [128, D], BF16) for i in range(2)]
    ss = [P.sbuf("ss%d" % i, [128, 4], F32) for i in range(2)]
    hst = [P.sbuf("hst%d" % i, [128, 32, 128], BF16) for i in range(2)]
    xst = [P.sbuf("xst%d" % i, [128, 32, 128], F32) for i in range(2)]
    ptb = [P.psum("ptb%d" % i, [128, 8, 128], BF16) for i in range(2)]
    ptf = [P.psum("ptf%d" % i, [128, 4, 128], F32) for i in range(2)]
    epsb = P.sbuf("epsb", [128, 1], F32)
    P.I("dve", "memset", [], [epsb], epsb[:, :], EPS6)
    P.I("sp", "dma_start", [XS], [xt[0]], out=xt[0][:, :], in_=XS[0:128, :], dma=xt[0])
    for t in range(NT):
        i = t % 2
        x = xt[i]
        if t + 1 < NT:
            xn = xt[(t + 1) % 2]
            P.I("sp", "dma_start", [XS], [xn], out=xn[:, :], in_=XS[(t + 1) * 128:(t + 2) * 128, :], dma=xn)
        P.I("dve", "memset", [], [ss[i]], ss[i][:, :], 0.0)
        P.I("act", "activation", [x, ss[i]], [junk, ss[i]], out=junk[:, :], in_=x[:, :], func=AF.Square, accum_out=ss[i][:, 0:1])
        P.I("act", "activation", [ss[i], epsb], [ss[i]], out=ss[i][:, 1:2], in_=ss[i][:, 0:1], func=AF.Sqrt, scale=1.0 / D, bias=epsb[:, 0:1])
        P.I("dve", "reciprocal", [ss[i]], [ss[i]], out=ss[i][:, 2:3], in_=ss[i][:, 1:2])
        P.I("pool", "tensor_scalar", [x, ss[i]], [xsb[i]], out=xsb[i][:, :], in0=x[:, :], scalar1=ss[i][:, 2:3], scalar2=None, op0=ALU.mult)
        isctx = 17 <= t < 19
        Aa, Bb = (A1c, B1c) if isctx else (A1x, B1x)
        for g in range(4):
            pb = ptb[g % 2]
            for c8 in range(8):
                c = g * 8 + c8
                P.I("pe", "transpose", [xsb[i], cb], [pb], out=pb[:, c8, :], in_=xsb[i][:, c * 128:(c + 1) * 128], identity=identb)
            for c8 in range(8):
                c = g * 8 + c8
                P.I("dve", "tensor_scalar", [pb, mod, der], [hst[i]], out=hst[i][:, c, :], in0=pb[:, c8, :], scalar1=Aa[:, c:c + 1],
                    scalar2=Bb[:, c:c + 1], op0=ALU.mult, op1=ALU.add, join=True)
        for q4 in range(4):
            P.I("sp", "dma_start", [hst[i]], [H1T], out=H1Tv[:, q4 * 8:(q4 + 1) * 8, t * 128:(t + 1) * 128], in_=hst[i][:, q4 * 8:(q4 + 1) * 8, :],
                dma=hst[i], join=True)
        if t < 17:
            for g in range(8):
                pf = ptf[g % 2]
                for c4 in range(4):
                    c = g * 4 + c4
                    P.I("pe", "transpose", [x, cf], [pf], out=pf[:, c4, :], in_=x[:, c * 128:(c + 1) * 128], identity=identf)
                eng = "act" if g % 2 == 0 else "dve"
                if eng == "act":
                    P.I("act", "copy", [pf], [xst[i]], out=xst[i][:, g * 4:(g + 1) * 4, :], in_=pf[:, :, :], join=True)
                else:
                    P.I("dve", "tensor_copy", [pf], [xst[i]], out=xst[i][:, g * 4:(g + 1) * 4, :], in_=pf[:, :, :], join=True)
            for q4 in range(4):
                P.I("sp", "dma_start", [xst[i]], [XT], out=XTv[:, q4 * 8:(q4 + 1) * 8, t * 128:(t + 1) * 128], in_=xst[i][:, q4 * 8:(q4 + 1) * 8, :],
                    dma=xst[i], join=True)
    P.phase_end()
    if stop_after <= 1:
        return _finish(P, nc, OUT, dbg, {})

    def qk_post_a(T, ps, n, gcol):
        sq, sd, kg, t1, t2, psN, psR = T[:7]
        P.I("act", "activation", [ps], [sq], out=sq[:, 0:n], in_=ps[:, 0:n], func=AF.Square)
        P.I("pe", "matmul", [sq, cb], [psN], psN[:, 0:n], lhsT=ones64, rhs=sq[:, 0:n], start=True, stop=True)
        P.I("act", "activation", [psN, T[7]], [sd], out=sd[:, 0:n], in_=psN[:, 0:n], func=AF.Sqrt, bias=T[7][:, 0:1], scale=1.0)
        P.I("dve", "reciprocal", [sd], [sd], out=sd[:, 0:n], in_=sd[:, 0:n])
        P.I("dve", "scalar_tensor_tensor", [ps, sd, vec, sml], [kg], out=kg[:, 0:n], in0=ps[:, 0:n], scalar=gcol, in1=sd[:, 0:n],
            op0=ALU.mult, op1=ALU.mult)

    def qk_post_b(T, n, cs, dst_ap, dst_buf):
        sq, sd, kg, t1, t2, psN, psR = T[:7]
        P.I("pe", "matmul", [kg, cb], [psR], psR[:, 0:n], lhsT=pswap, rhs=kg[:, 0:n], start=True, stop=True)
        P.I("pool", "tensor_tensor", [kg, cs], [t1], out=t1[:, 0:n], in0=kg[:, 0:n], in1=cs[:, 0, 0:n], op=ALU.mult)
        P.I("dve", "tensor_tensor", [psR, cs], [t2], out=t2[:, 0:n], in0=psR[:, 0:n], in1=cs[:, 1, 0:n], op=ALU.mult)
        P.I("pool", "tensor_tensor", [t1, t2], [dst_buf], out=dst_ap, in0=t1[:, 0:n], in1=t2[:, 0:n], op=ALU.add)

    P.phase_begin()
    hb = [P.sbuf("hb%d" % i, [128, 32, 512], BF16) for i in range(2)]
    wk = [P.sbuf("wk0", [128, 32, 512], BF16)]
    wv = [P.sbuf("wv0", [128, 32, 512], BF16)]
    csb = [P.sbuf("csb%d" % i, [128, 2, 512], F32) for i in range(2)]
    psK = [P.psum("psK%d" % i, [128, 512]) for i in range(4)]
    psV = [P.psum("psV%d" % i, [128, 512]) for i in range(2)]
    epsq = P.sbuf("epsq", [128, 1], F32)
    P.I("dve", "memset", [], [epsq], epsq[:, :], EPS6)
    psN_, psR_ = P.psum("psN", [128, 512]), P.psum("psR", [128, 512])
    t1_, t2_ = P.sbuf("t1", [128, 512], F32), P.sbuf("t2", [128, 512], F32)
    TT = [(P.sbuf("sq%d" % i, [128, 512], BF16), P.sbuf("sd%d" % i, [128, 512], F32), P.sbuf("kg%d" % i, [128, 512], BF16),
           t1_, t2_, psN_, psR_, epsq) for i in range(4)]
    kst = [P.sbuf("kst%d" % i, [128, 512], BF16) for i in range(4)]
    vst = [P.sbuf("vst%d" % i, [128, 4, 512], BF16) for i in range(2)]
    cnt = 0
    for g in range(4):
        P.I("pool", "dma_start", [WIN], [wk[0]], out=wk[0][:, :, :], in_=WINv[:, :, 2048 + g * 512:2048 + (g + 1) * 512], dma=wk[0])
        P.I("pool", "dma_start", [WIN], [wv[0]], out=wv[0][:, :, :], in_=WINv[:, :, 4096 + g * 512:4096 + (g + 1) * 512], dma=wv[0])
        for bl in range(17):
            s0 = bl * 512
            n = min(512, NS - s0)
            h = hb[cnt % 2]
            cs = csb[cnt % 2]
            cnt += 1
            P.I("sp", "dma_start", [H1T], [h], out=h[:, :, 0:n], in_=H1Tv[:, :, s0:s0 + n], dma=h)
            P.I("sp", "dma_start", [COS], [cs], out=cs[:, 0, 0:n], in_=COS[:, s0:s0 + n], dma=cs)
            P.I("sp", "dma_start", [SIN], [cs], out=cs[:, 1, 0:n], in_=SIN[:, s0:s0 + n], dma=cs, join=True)
            for hh in range(4):
                pk = psK[hh]
                for k in range(32):
                    P.I("pe", "matmul", [wk[0], h], [pk], pk[:, 0:n], lhsT=wk[0][:, k, hh * 128:(hh + 1) * 128], rhs=h[:, k, 0:n],
                        start=(k == 0), stop=(k == 31))
            vs_ = vst[bl % 2]
            ntile = n // 128

            def vtile(tt):
                if tt >= ntile:
                    return
                pv = psV[tt % 2]
                for k in range(32):
                    P.I("pe", "matmul", [wv[0], h], [pv], pv[:, :], lhsT=h[:, k, tt * 128:(tt + 1) * 128], rhs=wv[0][:, k, :],
                        start=(k == 0), stop=(k == 31))
                if tt % 2 == 0:
                    P.I("act", "copy", [pv], [vs_], out=vs_[:, tt, :], in_=pv[:, :], join=True)
                else:
                    P.I("dve", "tensor_copy", [pv], [vs_], out=vs_[:, tt, :], in_=pv[:, :], join=True)

            for hh in range(4):
                qk_post_a(TT[hh], psK[hh], n, V("gk"))
                vtile(hh)
            for hh in range(4):
                ks = kst[hh]
                qk_post_b(TT[hh], n, cs, ks[:, 0:n], ks)
                P.I("pool", "dma_start", [ks], [KT], out=KT[g * 4 + hh, :, s0:s0 + n], in_=ks[:, 0:n], dma=ks, join=True)
            for hh in range(4):
                P.I("pool", "dma_start", [vs_], [VS], out=VS[g * 4 + hh, :, bl * 4:bl * 4 + ntile, :],
                    in_=vs_[:, 0:ntile, hh * 128:(hh + 1) * 128], dma=vs_, join=True)
    for g in range(4):
        P.I("pool", "dma_start", [WIN], [wk[0]], out=wk[0][:, :, :], in_=WINv[:, :, g * 512:(g + 1) * 512], dma=wk[0])
        for bl in range(5):
            s0 = QO + bl * 410
            n = 410
            h = hb[cnt % 2]
            cs = csb[cnt % 2]
            cnt += 1
            P.I("sp", "dma_start", [H1T], [h], out=h[:, :, 0:n], in_=H1Tv[:, :, s0:s0 + n], dma=h)
            P.I("sp", "dma_start", [COS], [cs], out=cs[:, 0, 0:n], in_=COS[:, s0:s0 + n], dma=cs)
            P.I("sp", "dma_start", [SIN], [cs], out=cs[:, 1, 0:n], in_=SIN[:, s0:s0 + n], dma=cs, join=True)
            for hh in range(4):
                pk = psK[hh]
                for k in range(32):
                    P.I("pe", "matmul", [wk[0], h], [pk], pk[:, 0:n], lhsT=wk[0][:, k, hh * 128:(hh + 1) * 128], rhs=h[:, k, 0:n],
                        start=(k == 0), stop=(k == 31))
            for hh in range(4):
                qk_post_a(TT[hh], psK[hh], n, sml[:, 0:1])
            for hh in range(4):
                ks = kst[hh]
                qk_post_b(TT[hh], n, cs, ks[:, 0:n], ks)
                P.I("pool", "dma_start", [ks], [QT], out=QT[g * 4 + hh, :, bl * 410:bl * 410 + n], in_=ks[:, 0:n], dma=ks, join=True)
    P.phase_end()
    if stop_after <= 2:
        return _finish(P, nc, OUT, dbg, {})

    P.phase_begin()
    val = P.sbuf("val", [128, NEXT], F32)
    P.I("sp", "dma_start", [VAL], [val], out=val.full(), in_=VAL.full(), dma=val)
    P.I("dve", "tensor_copy", [val], [sml], out=sml[:, 8:9], in_=val[:, 63:64])
    P.I("dve", "tensor_copy", [val], [sml], out=sml[:, 9:10], in_=val[:, 2112:2113])
    hb = [P.sbuf("hb%d" % i, [128, 32, 416], BF16) for i in range(2)]
    wc = [P.sbuf("wc%d" % i, [128, 32, 256], BF16) for i in range(2)]
    psA = [P.psum("psA%d" % i, [128, 512]) for i in range(2)]
    psG = [P.psum("psG%d" % i, [128, 512]) for i in range(2)]
    sg = [P.sbuf("sg%d" % i, [128, 416], F32) for i in range(2)]
    ub = [P.sbuf("ub%d" % i, [128, NU], BF16) for i in range(2)]
    dg = [P.sbuf("dg%d" % i, [128, 31, 128], BF16) for i in range(2)]
    psC = [P.psum("psC%d" % i, [128, 512]) for i in range(2)]
    acc = [P.sbuf("acc%d" % i, [128, NQ], F32) for i in range(2)]
    ysq = P.sbuf("ysq", [128, NQ], F32)
    s1 = P.sbuf("s1", [128, NQ], F32)
    s2 = P.sbuf("s2", [128, NQ], F32)
    P.I("dve", "memset", [], [s1], s1[:, :], 0.0)
    P.I("dve", "memset", [], [s2], s2[:, :], 0.0)
    cnt = 0
    for c in range(16):
        w = wc[c % 2]
        P.I("pool", "dma_start", [WIN], [w], out=w[:, :, 0:128], in_=WINv[:, :, 6144 + c * 128:6144 + (c + 1) * 128], dma=w)
        P.I("pool", "dma_start", [WIN], [w], out=w[:, :, 128:256], in_=WINv[:, :, 8192 + c * 128:8192 + (c + 1) * 128], dma=w, join=True)
        u = ub[c % 2]
        for bl in range(5):
            s0 = UO + bl * 416
            n = 416
            h = hb[cnt % 2]
            pa, pg, sgi = psA[cnt % 2], psG[cnt % 2], sg[cnt % 2]
            cnt += 1
            P.I("sp", "dma_start", [H1T], [h], out=h[:, :, 0:n], in_=H1Tv[:, :, s0:s0 + n], dma=h)
            for k in range(32):
                P.I("pe", "matmul", [w, h], [pa], pa[:, 0:n], lhsT=w[:, k, 0:128], rhs=h[:, k, 0:n], start=(k == 0), stop=(k == 31))
            for k in range(32):
                P.I("pe", "matmul", [w, h], [pg], pg[:, 0:n], lhsT=w[:, k, 128:256], rhs=h[:, k, 0:n], start=(k == 0), stop=(k == 31))
            P.I("act", "activation", [pg], [sgi], out=sgi[:, 0:n], in_=pg[:, 0:n], func=AF.Sigmoid)
            P.I("dve", "tensor_tensor", [pa, sgi], [u], out=u[:, bl * 416:bl * 416 + n], in0=pa[:, 0:n], in1=sgi[:, 0:n], op=ALU.mult, join=True)
        P.I("dve", "tensor_tensor", [u, val], [u], out=u[:, 0:16], in0=u[:, 0:16], in1=val[:, 48:64], op=ALU.mult)
        P.I("dve", "tensor_tensor", [u, val], [u], out=u[:, 2064:2080], in0=u[:, 2064:2080], in1=val[:, 2112:2128], op=ALU.mult)
        a = acc[c % 2]
        dgc = dg[c % 2]
        for k in range(31):
            P.I("dve", "tensor_scalar", [cb, vec], [dgc], out=dgc[:, k, :], in0=identb, scalar1=V("cw", k * 16 + c), scalar2=None,
                op0=ALU.mult, join=True)
        for bl in range(5):
            pc = psC[bl % 2]
            for k in range(31):
                P.I("pe", "matmul", [dgc, u], [pc], pc[:, 0:410], lhsT=dgc[:, k, :], rhs=u[:, bl * 410 + k:bl * 410 + k + 410],
                    start=(k == 0), stop=(k == 30))
            P.I("act", "activation", [pc, vec], [a], out=a[:, bl * 410:(bl + 1) * 410], in_=pc[:, 0:410], func=AF.Identity,
                bias=V("cb", c), scale=1.0, join=True)
        P.I("pool", "tensor_tensor", [s1, a], [s1], out=s1[:, :], in0=s1[:, :], in1=a[:, :], op=ALU.add)
        P.I("act", "activation", [a], [ysq], out=ysq[:, :], in_=a[:, :], func=AF.Square)
        P.I("pool", "tensor_tensor", [s2, ysq], [s2], out=s2[:, :], in0=s2[:, :], in1=ysq[:, :], op=ALU.add)
        P.I("pool", "dma_start", [a], [YS], out=YS[c, :, :], in_=a[:, :], dma=a, join=True)
    mean = P.sbuf("mean", [128, NQ], F32)
    rstd = P.sbuf("rstdc", [128, NQ], F32)
    eps5 = P.sbuf("eps5", [128, 1], F32)
    P.I("dve", "memset", [], [eps5], eps5[:, :], EPS5)
    for bl in range(5):
        sl = slice(bl * 410, bl * 410 + 410)
        pa, pg = psA[bl % 2], psG[bl % 2]
        P.I("pe", "matmul", [s1, cf], [pa], pa[:, 0:410], lhsT=onesDf, rhs=s1[:, sl], start=True, stop=True)
        P.I("pe", "matmul", [s2, cf], [pg], pg[:, 0:410], lhsT=onesDf, rhs=s2[:, sl], start=True, stop=True)
        P.I("act", "activation", [pa], [mean], out=mean[:, sl], in_=pa[:, 0:410], func=AF.Identity, scale=1.0 / 2048, join=True)
        P.I("act", "activation", [pg], [rstd], out=rstd[:, sl], in_=pg[:, 0:410], func=AF.Identity, scale=1.0 / 2048, join=True)
    P.I("dve", "tensor_tensor", [mean], [ysq], out=ysq[:, :], in0=mean[:, :], in1=mean[:, :], op=ALU.mult)
    P.I("dve", "tensor_tensor", [rstd, ysq], [rstd], out=rstd[:, :], in0=rstd[:, :], in1=ysq[:, :], op=ALU.subtract)
    P.I("act", "activation", [rstd, eps5], [rstd], out=rstd[:, :], in_=rstd[:, :], func=AF.Sqrt, bias=eps5[:, 0:1], scale=1.0)
    P.I("dve", "reciprocal", [rstd], [rstd], out=rstd[:, :], in_=rstd[:, :])
    P.I("dve", "tensor_tensor", [mean, rstd], [mean], out=mean[:, :], in0=mean[:, :], in1=rstd[:, :], op=ALU.mult)
    cst = [P.sbuf("cst%d" % i, [128, NQ], BF16) for i in range(2)]
    for c in range(16):
        a = acc[c % 2]
        P.I("sp", "dma_start", [YS], [a], out=a[:, :], in_=YS[c, :, :], dma=a)
        P.I("dve", "tensor_tensor", [a, rstd], [a], out=a[:, :], in0=a[:, :], in1=rstd[:, :], op=ALU.mult)
        P.I("pool", "tensor_tensor", [a, mean], [a], out=a[:, :], in0=a[:, :], in1=mean[:, :], op=ALU.subtract)
        o = cst[c % 2]
        P.I("dve", "tensor_scalar", [a, vec], [a], out=a[:, :], in0=a[:, :], scalar1=V("lg", c), scalar2=V("lb", c), op0=ALU.mult, op1=ALU.add)
        P.I("act", "activation", [a], [o], out=o[:, :], in_=a[:, :], func=AF.Silu)
        P.I("pool", "dma_start", [o], [CAT], out=CAT[16 + c, :, :], in_=o[:, :], dma=o, join=True)
    P.phase_end()
    if stop_after <= 3:
        return _finish(P, nc, OUT, dbg, {})

    P.phase_begin()
    kt = [P.sbuf("kt%d" % i, [128, NS], BF16) for i in range(2)]
    vt = [P.sbuf("vt%d" % i, [128, NT, 128], BF16) for i in range(2)]
    qt = [P.sbuf("qt%d" % i, [128, NQ], BF16) for i in range(2)]
    psS = [P.psum("psS%d" % i, [128, 2, 512]) for i in range(2)]
    psO = [P.psum("psO%d" % i, [128, 512]) for i in range(2)]
    psL = [P.psum("psL%d" % i, [128, 512]) for i in range(2)]
    pT = [P.sbuf("pT%d" % i, [128, 2, 410], BF16) for i in range(3)]
    rr = [P.sbuf("rr%d" % i, [128, 410], F32) for i in range(2)]
    o_ = P.sbuf("o_", [128, 410], F32)
    r32 = P.sbuf("r32", [128, 410], F32)
    osq = P.sbuf("osq", [128, 410], F32)
    ast = [P.sbuf("ast%d" % i, [128, 410], BF16) for i in range(2)]
    eps6 = P.sbuf("eps6a", [128, 1], F32)
    P.I("dve", "memset", [], [eps6], eps6[:, :], EPS6)
    stf = [P.sbuf("stf%d" % i, [128, 2048], F32) for i in range(3)]
    stb = [P.sbuf("stb%d" % i, [128, 2048], BF16) for i in range(3)]
    csteps = []
    for kk in range(32):
        for half in range(2):
            for c6 in range(6):
                jg0 = c6 * 8
                nj = min(8, 43 - jg0)
                src = WUP[kk * 128:(kk + 1) * 128, half * DFF + jg0 * 256:half * DFF + (jg0 + nj) * 256]
                dst = WUPB.ap[jg0:jg0 + nj, :, kk, half * 256:(half + 1) * 256].rearrange("j p c -> p j c")
                csteps.append((WUP, src, WUPB, dst, nj, 256))
    for kk in range(NJ):
        for hh in range(2):
            src = WDN[kk * 128:(kk + 1) * 128, hh * 2048:(hh + 1) * 2048]
            dst = WDNB.ap[hh * 16:(hh + 1) * 16, :, kk, :].rearrange("m p c -> p m c")
            csteps.append((WDN, src, WDNB, dst, 16, 128))
    cstate = [0]

    def conv_store(i):
        SB, src, DB, dst, nj, w = csteps[i]
        b_ = stb[i % 3]
        P.I("sp", "dma_start", [b_], [DB], out=dst, in_=b_[:, 0:nj * w].rearrange("p (j c) -> p j c", c=w), dma=b_, join=True)

    def conv_advance():
        i = cstate[0]
        if i > len(csteps):
            return
        if i < len(csteps):
            SB, src, DB, dst, nj, w = csteps[i]
            f_, b_ = stf[i % 3], stb[i % 3]
            P.I("sp", "dma_start", [SB], [f_], out=f_[:, 0:nj * w], in_=src, dma=f_)
            P.I("pool", "tensor_copy", [f_], [b_], out=b_[:, 0:nj * w], in_=f_[:, 0:nj * w])
        if i >= 1:
            conv_store(i - 1)
        cstate[0] = i + 1

    ucnt = 0
    for head in range(16):
        hi = head % 2
        k_, v_, q_ = kt[hi], vt[hi], qt[hi]
        P.I("sp", "dma_start", [KT], [k_], out=k_[:, :], in_=KT[head, :, :], dma=k_)
        P.I("sp", "dma_start", [VS], [v_], out=v_[:, :, :], in_=VS[head, :, :, :], dma=v_)
        P.I("sp", "dma_start", [QT], [q_], out=q_[:, :], in_=QT[head, :, :], dma=q_)
        for qb in range(5):
            qc = qb * 410
            n = 410
            pend = None
            for t in range(NT + 1):
                if t < NT:
                    ps = psS[ucnt % 2]
                    pt_ = pT[ucnt % 3]
                    ucnt += 1
                    P.I("pe", "matmul", [k_, q_], [ps], ps[:, 0, 0:n], lhsT=k_[0:64, t * 128:(t + 1) * 128], rhs=q_[0:64, qc:qc + n],
                        start=True, stop=True)
                    P.I("pe", "matmul", [k_, q_], [ps], ps[:, 1, 0:n], lhsT=k_[64:128, t * 128:(t + 1) * 128], rhs=q_[64:128, qc:qc + n],
                        start=True, stop=True)
                    P.I("act", "activation", [ps, kb], [pt_], out=pt_[:, :, :], in_=ps[:, :, 0:n], func=AF.Exp, bias=kb[:, t:t + 1], scale=1.0)
                    cur = (t, pt_)
                    if ucnt % 9 == 0:
                        conv_advance()
                else:
                    cur = None
                if pend is not None:
                    tp, pp = pend
                    st, sp_ = (tp == 0), (tp == NT - 1)
                    for s in range(2):
                        P.I("pe", "matmul", [v_, pp], [psO[s]], psO[s][:, 0:n], lhsT=v_[:, tp, :], rhs=pp[:, s, :], start=st, stop=sp_)
                    for s in range(2):
                        P.I("pe", "matmul", [cb, pp], [psL[s]], psL[s][32 * s:32 * s + 32, 0:n], lhsT=onesb[:, 32 * s:32 * s + 32], rhs=pp[:, s, :],
                            start=st, stop=sp_, tile_position=(0, 32 * s))
                pend = cur
            for s in range(2):
                P.I("dve", "reciprocal", [psL[s]], [r32], out=r32[32 * s:32 * s + 32, :], in_=psL[s][32 * s:32 * s + 32, 0:n], join=True)
            for s in range(2):
                P.I("pe", "matmul", [r32, cf], [psL[s]], psL[s][:, 0:n], lhsT=onesDf[32 * s:32 * s + 32, :], rhs=r32[32 * s:32 * s + 32, :],
                    start=True, stop=True)
            for s in range(2):
                P.I("act", "activation", [psL[s]], [rr[s]], out=rr[s][:, :], in_=psL[s][:, 0:n], func=AF.Identity, scale=1.0 / 32)
                P.I("dve", "tensor_tensor", [psO[s], rr[s]], [rr[s]], out=rr[s][:, :], in0=psO[s][:, 0:n], in1=rr[s][:, :], op=ALU.mult)
            P.I("dve", "scalar_tensor_tensor", [rr[0], rr[1], sml], [o_], out=o_[:, :], in0=rr[1][:, :], scalar=sml[:, 2:3], in1=rr[0][:, :],
                op0=ALU.mult, op1=ALU.add)
            P.I("act", "activation", [o_], [osq], out=osq[:, :], in_=o_[:, :], func=AF.Square)
            P.I("pe", "matmul", [osq, cf], [psL[0]], psL[0][:, 0:n], lhsT=ones128f, rhs=osq[:, :], start=True, stop=True)
            P.I("act", "activation", [psL[0], eps6], [osq], out=osq[:, :], in_=psL[0][:, 0:n], func=AF.Sqrt, bias=eps6[:, 0:1], scale=1.0)
            P.I("dve", "reciprocal", [osq], [osq], out=osq[:, :], in_=osq[:, :])
            a_ = ast[(head * 5 + qb) % 2]
            P.I("dve", "scalar_tensor_tensor", [o_, osq, sml], [a_], out=a_[:, :], in0=o_[:, :], scalar=sml[:, 1:2], in1=osq[:, :],
                op0=ALU.mult, op1=ALU.mult)
            P.I("sp", "dma_start", [a_], [CAT], out=CAT[head, :, qc:qc + n], in_=a_[:, :], dma=a_, join=True)
    while cstate[0] <= len(csteps):
        conv_advance()
    P.phase_end()
    if stop_after <= 4:
        return _finish(P, nc, OUT, dbg, {})

    P.phase_begin()
    cbk = [P.sbuf("cbk%d" % i, [128, 32, 410], BF16) for i in range(2)]
    wo = [P.sbuf("wo%d" % i, [128, 32, 512], BF16) for i in range(2)]
    xtb = [P.sbuf("xtb%d" % i, [128, 4, 410], F32) for i in range(2)]
    x1b = [P.sbuf("x1b%d" % i, [128, 4, 410], F32) for i in range(2)]
    psM = [P.psum("psM%d" % i, [128, 512]) for i in range(4)]
    sqb = P.sbuf("sqb", [128, 4, 410], F32)
    s2 = P.sbuf("s2n", [128, NQ], F32)
    P.I("dve", "memset", [], [s2], s2[:, :], 0.0)
    CATv = CAT.ap.rearrange("c p s -> p c s")
    X1Tv = X1T.ap.rearrange("c p s -> p c s")
    cnt = 0
    for mg in range(8):
        w = wo[mg % 2]
        P.I("pool", "dma_start", [WOUT], [w], out=w[:, :, :], in_=WOUTv[:, :, mg * 512:(mg + 1) * 512], dma=w)
        for bl in range(5):
            sl = slice(bl * 410, bl * 410 + 410)
            cbl, xb, x1 = cbk[cnt % 2], xtb[cnt % 2], x1b[cnt % 2]
            cnt += 1
            P.I("sp", "dma_start", [CAT], [cbl], out=cbl[:, :, :], in_=CATv[:, :, sl], dma=cbl)
            P.I("sp", "dma_start", [XT], [xb], out=xb[:, :, :], in_=XTv[:, mg * 4:(mg + 1) * 4, QO + bl * 410:QO + bl * 410 + 410], dma=xb)
            for m in range(4):
                pm = psM[m]
                for k in range(32):
                    P.I("pe", "matmul", [w, cbl], [pm], pm[:, 0:410], lhsT=w[:, k, m * 128:(m + 1) * 128], rhs=cbl[:, k, :], start=(k == 0), stop=(k == 31))
                P.I("dve", "scalar_tensor_tensor", [pm, xb, mod], [x1], out=x1[:, m, :], in0=pm[:, 0:410], scalar=gate1[:, mg * 4 + m:mg * 4 + m + 1],
                    in1=xb[:, m, :], op0=ALU.mult, op1=ALU.add, join=True)
            P.I("act", "activation", [x1], [sqb], out=sqb[:, :, :], in_=x1[:, :, :], func=AF.Square)
            for m in range(4):
                P.I("pool", "tensor_tensor", [s2, sqb], [s2], out=s2[:, sl], in0=s2[:, sl], in1=sqb[:, m, :], op=ALU.add)
            P.I("pool", "dma_start", [x1], [X1T], out=X1Tv[:, mg * 4:(mg + 1) * 4, sl], in_=x1[:, :, :], dma=x1, join=True)
    eps6b = P.sbuf("eps6b", [128, 1], F32)
    P.I("dve", "memset", [], [eps6b], eps6b[:, :], EPS6)
    for bl in range(5):
        sl = slice(bl * 410, bl * 410 + 410)
        pm = psM[bl % 4]
        P.I("pe", "matmul", [s2, cf], [pm], pm[:, 0:410], lhsT=onesDf, rhs=s2[:, sl], start=True, stop=True)
        P.I("act", "activation", [pm, eps6b], [rstd2], out=rstd2[:, sl], in_=pm[:, 0:410], func=AF.Sqrt, bias=eps6b[:, 0:1], scale=1.0 / D, join=True)
    P.I("dve", "reciprocal", [rstd2], [rstd2], out=rstd2[:, :], in_=rstd2[:, :])
    P.phase_end()
    if stop_after <= 5:
        return _finish(P, nc, OUT, dbg, {})

    P.phase_begin()
    h2 = P.sbuf("h2", [128, 32, 412], BF16)
    x1s = [P.sbuf("x1s%d" % i, [128, 4, 412], F32) for i in range(1)]
    Ab = P.sbuf("Ab", [128, NJ, 410], BF16)
    wb = [P.sbuf("wb%d" % i, [128, 32, 512], BF16) for i in range(2)]
    psU = [P.psum("psU%d" % i, [128, 512]) for i in range(2)]
    psGt = [P.psum("psGt%d" % i, [128, 512]) for i in range(2)]
    psY = [P.psum("psY%d" % i, [128, 512]) for i in range(2)]
    psT = [P.psum("psT%d" % i, [128, 4, 128]) for i in range(2)]
    cu = [P.sbuf("cu%d" % i, [128, 410], F32) for i in range(2)]
    cg = [P.sbuf("cg%d" % i, [128, 410], F32) for i in range(2)]
    sgt = [P.sbuf("sgt%d" % i, [128, 410], F32) for i in range(2)]
    x1r = [P.sbuf("x1r%d" % i, [128, 410], F32) for i in range(2)]
    yT = [P.sbuf("yT%d" % i, [128, 410], F32) for i in range(2)]
    ost = [P.sbuf("ost%d" % i, [128, 4, 128], F32) for i in range(2)]
    b2v = mod.ap
    wcnt = 0
    for b in range(5):
        c0 = b * 410
        nin = min(412, NQ - c0)
        nout = nin - 2
        for cg4 in range(8):
            xs_ = x1s[0]
            P.I("pool", "dma_start", [X1T], [xs_], out=xs_[:, :, 0:nin], in_=X1Tv[:, cg4 * 4:(cg4 + 1) * 4, c0:c0 + nin], dma=xs_)
            for c4 in range(4):
                c = cg4 * 4 + c4
                P.I("dve", "tensor_tensor", [xs_, rstd2], [xs_], out=xs_[:, c4, 0:nin], in0=xs_[:, c4, 0:nin], in1=rstd2[:, c0:c0 + nin], op=ALU.mult, join=True)
                P.I("dve", "tensor_scalar", [xs_, der, mod], [h2], out=h2[:, c, 0:nin], in0=xs_[:, c4, 0:nin], scalar1=A2[:, c:c + 1],
                    scalar2=B2[:, c:c + 1], op0=ALU.mult, op1=ALU.add, join=True)
        if b == 0:
            for c in range(32):
                P.I("dve", "tensor_scalar", [h2, sml], [h2], out=h2[:, c, 0:1], in0=h2[:, c, 0:1], scalar1=sml[:, 8:9], scalar2=None, op0=ALU.mult, join=True)
        if b == 4:
            for c in range(32):
                P.I("dve", "tensor_scalar", [h2, sml], [h2], out=h2[:, c, nin - 1:nin], in0=h2[:, c, nin - 1:nin], scalar1=sml[:, 9:10],
                    scalar2=None, op0=ALU.mult, join=True)
        for jg in range(43):
            w = wb[wcnt % 2]
            wcnt += 1
            P.I("sp", "dma_start", [WUPB], [w], out=w[:, :, :], in_=WUPB[jg, :, :, :], dma=w)
            for jj in range(2):
                j = jg * 2 + jj
                pu, pg = psU[jj], psGt[jj]
                for k in range(32):
                    P.I("pe", "matmul", [w, h2], [pu], pu[:, 0:nin], lhsT=w[:, k, jj * 128:(jj + 1) * 128], rhs=h2[:, k, 0:nin], start=(k == 0), stop=(k == 31))
                for k in range(32):
                    P.I("pe", "matmul", [w, h2], [pg], pg[:, 0:nin], lhsT=w[:, k, 256 + jj * 128:256 + (jj + 1) * 128], rhs=h2[:, k, 0:nin],
                        start=(k == 0), stop=(k == 31))
                for (pp, dst, fo) in ((pu, cu[jj], j), (pg, cg[jj], NJ + j)):
                    P.I("dve", "tensor_scalar", [pp, vec], [dst], out=dst[:, 0:nout], in0=pp[:, 0:nout], scalar1=V("fw", 0 * 172 + fo), scalar2=V("fb", fo),
                        op0=ALU.mult, op1=ALU.add)
                    for kk in (1, 2):
                        P.I("dve", "scalar_tensor_tensor", [pp, vec, dst], [dst], out=dst[:, 0:nout], in0=pp[:, kk:kk + nout], scalar=V("fw", kk * 172 + fo),
                            in1=dst[:, 0:nout], op0=ALU.mult, op1=ALU.add)
                P.I("act", "activation", [cg[jj]], [sgt[jj]], out=sgt[jj][:, 0:nout], in_=cg[jj][:, 0:nout], func=AF.Silu)
                P.I("pool", "tensor_tensor", [sgt[jj], cu[jj]], [Ab], out=Ab[:, j, 0:nout], in0=sgt[jj][:, 0:nout], in1=cu[jj][:, 0:nout], op=ALU.mult, join=True)
        ntl = (nout + 127) // 128
        for m in range(32):
            w = wb[wcnt % 2]
            wcnt += 1
            wd = w.ap.rearrange("p a b -> p (a b)")[:, 0:NJ * 128].rearrange("p (k n) -> p k n", n=128)
            P.I("sp", "dma_start", [WDNB], [w], out=wd, in_=WDNB[m, :, :, :], dma=w)
            xr = x1r[m % 2]
            P.I("pool", "dma_start", [X1T], [xr], out=xr[:, 0:nout], in_=X1T[m, :, c0 + 1:c0 + 1 + nout], dma=xr)
            py = psY[m % 2]
            for k in range(NJ):
                P.I("pe", "matmul", [w, Ab], [py], py[:, 0:nout], lhsT=wd[:, k, :], rhs=Ab[:, k, 0:nout], start=(k == 0), stop=(k == NJ - 1))
            y = yT[m % 2]
            P.I("dve", "scalar_tensor_tensor", [py, xr, mod], [y], out=y[:, 0:nout], in0=py[:, 0:nout], scalar=gate2[:, m:m + 1], in1=xr[:, 0:nout],
                op0=ALU.mult, op1=ALU.add)
            pt_ = psT[m % 2]
            os_ = ost[m % 2]
            for tl in range(ntl):
                nt_ = min(128, nout - tl * 128)
                P.I("pe", "transpose", [y, cf], [pt_], out=pt_[0:nt_, tl, :], in_=y[:, tl * 128:tl * 128 + nt_], identity=identf)
            nfull = nout // 128
            P.I("act", "copy", [pt_], [os_], out=os_[:, 0:nfull, :], in_=pt_[:, 0:nfull, :])
            r0 = c0
            P.I("pool", "dma_start", [os_], [OUT], out=OUT.ap[r0:r0 + nfull * 128, m * 128:(m + 1) * 128].rearrange("(t p) f -> p t f", p=128),
                in_=os_[:, 0:nfull, :], dma=os_, join=True)
            rem = nout - nfull * 128
            if rem:
                P.I("dve", "tensor_copy", [pt_], [os_], out=os_[0:rem, nfull, :], in_=pt_[0:rem, nfull, :], join=True)
                P.I("pool", "dma_start", [os_], [OUT], out=OUT.ap[r0 + nfull * 128:r0 + nout, m * 128:(m + 1) * 128], in_=os_[0:rem, nfull, :],
                    dma=os_, join=True)
    P.wait_only("sp", [OUT])
    P.phase_end(final=True)
    P.ctx.close()
    return nc


def _finish(P, nc, OUT, dbg, taps):
    P.phase_begin()
    outs = [OUT]
    for name, buf in taps.items():
        shp = list(buf.ap.shape)
        t = Buf(nc.dram_tensor("tap_" + name, shp, buf.ap.dtype, kind="ExternalOutput"), "tap_" + name)
        P.I("sp", "dma_start", [buf], [t], out=t.full(), in_=buf.full(), dma=buf)
        outs.append(t)
    P.wait_only("sp", outs)
    P.phase_end(final=True)
    P.ctx.close()
    return nc


def _rope_tables():
    rows = SEQ // 64
    row = np.repeat(np.arange(rows), 64).astype(np.float32)
    col = np.tile(np.arange(64), rows).astype(np.float32)
    freqs = (np.float32(10000.0) ** (-np.arange(16, dtype=np.float32) / np.float32(16))).astype(np.float32)
    ang = np.concatenate([row[:, None] * freqs, col[:, None] * freqs], axis=-1).astype(np.float32)
    return np.cos(ang).astype(np.float32), np.sin(ang).astype(np.float32)


def _core_layout(qi):
    T0 = qi * 2048
    slot_tok = np.full(NS, -1, dtype=np.int64)
    ext = np.arange(T0 - 64, T0 + 2112)
    ok = (ext >= 0) & (ext < SEQ)
    slot_tok[0:NEXT] = np.where(ok, ext, -1)
    slot_tok[NEXT:NEXT + NCTX] = -2 - np.arange(NCTX)
    lo, hi = max(T0 - 64, 0), min(T0 + 2112, SEQ)
    others = np.concatenate([np.arange(0, lo), np.arange(hi, SEQ)])
    slot_tok[NEXT + NCTX:NEXT + NCTX + len(others)] = others
    return slot_tok


def _pm(v, n):
    return np.ascontiguousarray(np.asarray(v, dtype=np.float32).reshape(n, 128).T)


def make_in_maps(x, c, ctx, c_ctx, w_ada, b_ada, norm1_g, norm2_g, w_in, q_norm_g, k_norm_g,
                 lambda_q1, lambda_k1, lambda_q2, lambda_k2, subln_g, conv_dw_w, conv_dw_b,
                 conv_ln_g, conv_ln_b, w_out, w_up, ffn_dw_w, ffn_dw_b, w_down, cores=range(8)):
    x = np.asarray(x, dtype=np.float32)
    ctx = np.asarray(ctx, dtype=np.float32)
    cosT, sinT = _rope_tables()
    eye = np.eye(128, dtype=np.float32)
    sw = np.zeros((128, 128), np.float32)
    idx = np.arange(128)
    sw[idx, idx ^ 1] = 1.0
    o64 = np.zeros((128, 128), np.float32)
    o64[:64, :64] = 1.0 / 64
    o64[64:, 64:] = 1.0 / 64
    constb = np.concatenate([eye, sw, o64, np.ones((128, 128), np.float32)], axis=1).astype(ml_dtypes.bfloat16)
    constf = np.concatenate([eye, np.full((128, 128), 1.0 / 128, np.float32), np.ones((128, 128), np.float32)], axis=1)
    p = np.arange(128)
    pair = (p % 64) // 2
    sgn = np.where(p % 2 == 0, -1.0, 1.0).astype(np.float32)

    base = np.zeros((128, NV), np.float32)

    def put(name, arr):
        arr = np.asarray(arr, np.float32)
        base[:, _VOFF[name]:_VOFF[name] + arr.shape[1]] = arr

    put("g1", _pm(norm1_g[0], 32))
    put("g2", _pm(norm2_g[0], 32))
    put("bada", _pm(b_ada[0], 192))
    put("gq", np.asarray(q_norm_g[0], np.float32)[p % 64][:, None])
    put("gk", np.asarray(k_norm_g[0], np.float32)[p % 64][:, None])
    put("gs", np.asarray(subln_g[0], np.float32)[:, None])
    cw = np.asarray(conv_dw_w[0], np.float32)
    put("cw", cw.reshape(31, 16, 128).transpose(2, 0, 1).reshape(128, 31 * 16))
    put("cb", _pm(conv_dw_b[0], 16))
    put("lg", _pm(conv_ln_g[0], 16))
    put("lb", _pm(conv_ln_b[0], 16))
    fw = np.asarray(ffn_dw_w[0], np.float32)
    put("fw", fw.reshape(3, 172, 128).transpose(2, 0, 1).reshape(128, 3 * 172))
    put("fb", _pm(ffn_dw_b[0], 172))
    put("ccT", _pm(c_ctx, 32))
    lamv = np.concatenate([np.asarray(v[0], np.float32) for v in (lambda_q1, lambda_k1, lambda_q2, lambda_k2)])
    put("lam", np.broadcast_to(lamv[None, :], (128, 256)))

    wa2, wi2, wo2, wu2, wd2 = (np.asarray(w[0], dtype=np.float32) for w in (w_ada, w_in, w_out, w_up, w_down))
    in_maps = []
    for core in cores:
        b, qi = core // 4, core % 4
        st = _core_layout(qi)
        xs = np.zeros((NS, D), np.float32)
        mx = st >= 0
        xs[mx] = x[b][st[mx]]
        mc = st <= -2
        xs[mc] = ctx[b][-2 - st[mc]]
        validslot = (st != -1)
        keybias = np.where(validslot, 0.0, -30000.0).astype(np.float32).reshape(NT, 128).T
        valid = np.broadcast_to(validslot[:NEXT].astype(np.float32)[None, :], (128, NEXT))
        cs = np.ones((128, NS), np.float32)
        sn = np.zeros((128, NS), np.float32)
        cs[:, mx] = cosT[st[mx]][:, pair].T
        sn[:, mx] = sinT[st[mx]][:, pair].T * sgn[:, None]
        vecs = base.copy()
        vecs[:, _VOFF["cT"]:_VOFF["cT"] + 32] = _pm(c[b], 32)
        in_maps.append({
            "xs": xs, "vecs": vecs, "keybias": np.ascontiguousarray(keybias), "valid": np.ascontiguousarray(valid),
            "cosT": cs, "sinT": sn, "constb": constb, "constf": constf,
            "w_ada": wa2, "w_in": wi2, "w_out": wo2, "w_up": wu2, "w_down": wd2,
        })
    return in_maps


def kernel(**inputs):
    in_maps = make_in_maps(**inputs)
    nc = build_program()
    res = run_bass_kernel_spmd(nc, in_maps, core_ids=list(range(8)))
    out = np.empty((2, SEQ, D), np.float32)
    for core in range(8):
        b, qi = core // 4, core % 4
        out[b, qi * 2048:(qi + 1) * 2048] = res.results[core]["out"]
    return out
```

```python
from contextlib import ExitStack
import numpy as np
import ml_dtypes
import concourse.bass as bass
import concourse.mybir as mybir
from concourse.bass_utils import run_bass_kernel_spmd

F32 = mybir.dt.float32
BF16 = mybir.dt.bfloat16
ALU = mybir.AluOpType
AF = mybir.ActivationFunctionType

D = 4096
NCH = 32
SEQ = 8192
NCTX = 256
NT = 67
NS = NT * 128
NEXT = 2176
QO = 63
NQ = 2050
UO = 48
NU = 2080
DFF = 11008
NJ = 86
EPS6 = 1e-6
EPS5 = 1e-5
LAM_INIT = 0.2
ENGS = ("pe", "act", "dve", "pool", "sp")

_VOFF = {}
_o = 0
for _n, _w in (("g1", 32), ("g2", 32), ("bada", 192), ("gq", 1), ("gk", 1), ("gs", 1),
               ("cw", 31 * 16), ("cb", 16), ("lg", 16), ("lb", 16),
               ("fw", 3 * 172), ("fb", 172), ("cT", 32), ("ccT", 32), ("lam", 256)):
    _VOFF[_n] = _o
    _o += _w
NV = _o


class Buf:
    def __init__(self, ap, name):
        self.ap = ap
        self.name = name
        self.w_ev = {}
        self.r_ev = {}
        self.dsem = None
        self.dcount = 0

    def __getitem__(self, idx):
        return self.ap[idx]

    def full(self):
        return self.ap[(slice(None),) * len(self.ap.shape)]


class Op:
    __slots__ = ("eng", "fn", "waits", "signal", "semval", "dbuf", "seq")

    def __init__(self, eng, fn, seq):
        self.eng = eng
        self.fn = fn
        self.waits = {}
        self.signal = False
        self.semval = None
        self.dbuf = None
        self.seq = seq


class Prog:
    def __init__(self, nc):
        self.nc = nc
        self.ctx = ExitStack()
        self.pstack = None
        self.ops = {e: [] for e in ENGS}
        self.seq = {e: 0 for e in ENGS}
        self.ecount = {e: 0 for e in ENGS}
        self.waited = {e: {} for e in ENGS}
        self.esem = {e: self.ctx.enter_context(nc.semaphore("e_" + e)) for e in ENGS}
        self.nbuf = 0
        self.touched = {}
        self.bar_src = None
        self.bar_dst = None
        self.tok = None
        self.dbg = ()
        self.uid = 0

    def sbuf(self, name, shape, dtype, glob=False):
        st = self.ctx if (glob or self.pstack is None) else self.pstack
        self.uid += 1
        return Buf(st.enter_context(self.nc.sbuf_tensor("%s_%d" % (name, self.uid), list(shape), dtype)), name)

    def psum(self, name, shape, dtype=F32):
        st = self.ctx if self.pstack is None else self.pstack
        self.uid += 1
        return Buf(st.enter_context(self.nc.psum_tensor("%s_%d" % (name, self.uid), list(shape), dtype)), name)

    def dram(self, name, shape, dtype, kind="Internal"):
        if self.dbg and name in self.dbg:
            kind = "ExternalOutput"
        return Buf(self.nc.dram_tensor(name, list(shape), dtype, kind=kind), name)

    def op(self, eng, fn, reads=(), writes=(), dma=None, join=False):
        o = Op(eng, fn, self.seq[eng])
        self.seq[eng] += 1
        waits = o.waits

        def need(evd):
            for key, val in evd.items():
                if key[0] == "E":
                    if key[1] == eng and eng == "pe":
                        continue
                    old = waits.get(key)
                    if old is None or old.seq < val.seq:
                        waits[key] = val
                else:
                    waits[key] = key[1].dcount

        for r in reads:
            need(r.w_ev)
            self.touched[id(r)] = r
        for w in writes:
            if not join:
                need(w.w_ev)
            need(w.r_ev)
            self.touched[id(w)] = w
        for key, val in waits.items():
            if key[0] == "E":
                val.signal = True
        if dma is not None:
            if dma.dsem is None:
                dma.dsem = self.ctx.enter_context(self.nc.semaphore("d%d" % self.nbuf))
                self.nbuf += 1
            dma.dcount += 16
            key, val = ("D", dma), dma.dcount
            o.dbuf = dma
        else:
            key, val = ("E", eng), o
        for w in writes:
            if join:
                w.w_ev[key] = val
            else:
                w.w_ev = {key: val}
                w.r_ev = {}
        for r in reads:
            r.r_ev[key] = val
        self.ops[eng].append(o)
        return o

    def I(self, eng, meth, reads, writes, *a, dma=None, join=False, **kw):
        return self.op(eng, lambda e: getattr(e, meth)(*a, **kw), reads, writes, dma=dma, join=join)

    def wait_only(self, eng, bufs):
        o = Op(eng, None, self.seq[eng])
        self.seq[eng] += 1
        for b in bufs:
            for key, val in list(b.w_ev.items()) + list(b.r_ev.items()):
                if key[0] == "E":
                    if key[1] == eng:
                        continue
                    val.signal = True
                    old = o.waits.get(key)
                    if old is None or old.seq < val.seq:
                        o.waits[key] = val
                else:
                    o.waits[key] = key[1].dcount
        self.ops[eng].append(o)

    def barrier(self):
        bufs = list(self.touched.values())
        self.touched = {}
        src, dst, tok = self.bar_src, self.bar_dst, self.tok
        o = Op("sp", lambda e: e.dma_start(out=dst[0:1, 0:16], in_=src[0:1, 0:16]), self.seq["sp"])
        self.seq["sp"] += 1
        for b in bufs + [tok]:
            for key, val in list(b.w_ev.items()) + list(b.r_ev.items()):
                if key[0] == "E":
                    val.signal = True
                    old = o.waits.get(key)
                    if old is None or old.seq < val.seq:
                        o.waits[key] = val
                else:
                    o.waits[key] = key[1].dcount
        if tok.dsem is None:
            tok.dsem = self.ctx.enter_context(self.nc.semaphore("d%d" % self.nbuf))
            self.nbuf += 1
        tok.dcount += 16
        o.dbuf = tok
        self.ops["sp"].append(o)
        for b in bufs:
            b.w_ev = {}
            b.r_ev = {}
        tok.w_ev = {("D", tok): tok.dcount}
        tok.r_ev = {}
        for e in ENGS:
            if e != "sp":
                self.wait_only(e, [tok])

    def emit(self):
        nc = self.nc
        for e in ENGS:
            for o in self.ops[e]:
                if o.signal:
                    self.ecount[e] += 1
                    o.semval = self.ecount[e]
        prog = self

        def run(engname, engine):
            waited = prog.waited[engname]
            for o in prog.ops[engname]:
                for key, val in o.waits.items():
                    if key[0] == "E":
                        v = val.semval
                        sem = prog.esem[key[1]]
                    else:
                        v = val
                        sem = key[1].dsem
                    if waited.get(key, 0) >= v:
                        continue
                    waited[key] = v
                    engine.wait_ge(sem, v)
                if o.fn is None:
                    continue
                ins = o.fn(engine)
                if o.dbuf is not None:
                    ins.then_inc(o.dbuf.dsem, 16)
                elif o.signal:
                    ins.then_inc(prog.esem[engname], 1)

        with nc.Block() as block:
            @block.sync
            def _(e):
                run("sp", e)

            @block.tensor
            def _(e):
                run("pe", e)

            @block.scalar
            def _(e):
                run("act", e)

            @block.vector
            def _(e):
                run("dve", e)

            @block.gpsimd
            def _(e):
                run("pool", e)
        self.ops = {e: [] for e in ENGS}

    def phase_begin(self):
        self.pstack = ExitStack()

    def phase_end(self, final=False):
        if not final:
            self.barrier()
        self.emit()
        self.pstack.close()
        self.pstack = None


def build_program(stop_after=99, dbg=False):
    nc = bass.Bass("TRN2", target_bir_lowering=False)
    P = Prog(nc)
    P.dbg = dbg or ()
    print('sbuf bytes remaining', nc.sbuf_bytes_remaining)

    def din(name, shape, dt=F32):
        return Buf(nc.dram_tensor(name, list(shape), dt, kind="ExternalInput"), name)

    XS = din("xs", [NS, D])
    VEC = din("vecs", [128, NV])
    KB = din("keybias", [128, NT])
    VAL = din("valid", [128, NEXT])
    COS = din("cosT", [128, NS])
    SIN = din("sinT", [128, NS])
    CB = din("constb", [128, 4 * 128], BF16)
    CF = din("constf", [128, 3 * 128])
    WADA = din("w_ada", [D, 6 * D])
    WIN = din("w_in", [D, 10240])
    WOUT = din("w_out", [D, D])
    WUP = din("w_up", [D, 2 * DFF])
    WDN = din("w_down", [DFF, D])
    OUT = Buf(nc.dram_tensor("out", [2048, D], F32, kind="ExternalOutput"), "out")

    H1T = P.dram("h1t", [D, NS], BF16)
    XT = P.dram("xtT", [D, NEXT], F32)
    KT = P.dram("kt", [16, 128, NS], BF16)
    VS = P.dram("vs", [16, 128, NT, 128], BF16)
    QT = P.dram("qt", [16, 128, NQ], BF16)
    YS = P.dram("ys", [16, 128, NQ], F32)
    CAT = P.dram("cat", [32, 128, NQ], BF16)
    X1T = P.dram("x1t", [32, 128, NQ], F32)
    WUPB = P.dram("wupb", [43, 128, 32, 512], BF16)
    WDNB = P.dram("wdnb", [32, 128, NJ, 128], BF16)
    BARD = P.dram("bard", [1, 16], F32)
    P.bar_dst = BARD
    H1Tv = H1T.ap.rearrange("(c p) s -> p c s", p=128)
    XTv = XT.ap.rearrange("(c p) s -> p c s", p=128)
    WINv = WIN.ap.rearrange("(k p) n -> p k n", p=128)
    WADAv = WADA.ap.rearrange("(k p) n -> p k n", p=128)
    WOUTv = WOUT.ap.rearrange("(k p) n -> p k n", p=128)
    WUPv = WUP.ap.rearrange("(k p) n -> p k n", p=128)
    WDNv = WDN.ap.rearrange("(k p) n -> p k n", p=128)

    vec = P.sbuf("vec", [128, NV], F32, glob=True)
    cb = P.sbuf("cb", [128, 4 * 128], BF16, glob=True)
    cf = P.sbuf("cf", [128, 3 * 128], F32, glob=True)
    mod = P.sbuf("mod", [128, 192, 2], F32, glob=True)
    der = P.sbuf("der", [128, 8, 32], F32, glob=True)
    sml = P.sbuf("sml", [128, 16], F32, glob=True)
    kb = P.sbuf("kb", [128, NT], F32, glob=True)
    tok = P.sbuf("tok", [1, 16], F32, glob=True)
    rstd2 = P.sbuf("rstd2", [128, NQ], F32, glob=True)
    P.tok = tok
    P.bar_src = tok
    identb = cb.ap[:, 0:128]
    pswap = cb.ap[:, 128:256]
    ones64 = cb.ap[:, 256:384]
    onesb = cb.ap[:, 384:512]
    identf = cf.ap[:, 0:128]
    ones128f = cf.ap[:, 128:256]
    onesDf = cf.ap[:, 256:384]

    def V(name, i=0, w=1):
        o = _VOFF[name] + i
        return vec.ap[:, o:o + w]

    A1x, A1c, A2 = der.ap[:, 0, :], der.ap[:, 1, :], der.ap[:, 2, :]
    B1x, B1c = mod.ap[:, 0:32, 0], mod.ap[:, 0:32, 1]
    gate1, B2, gate2 = mod.ap[:, 64:96, 0], mod.ap[:, 96:128, 0], mod.ap[:, 160:192, 0]

    P.phase_begin()
    P.I("dve", "memset", [], [tok], tok[:, :], 0.0)
    for dst, src in ((vec, VEC), (cb, CB), (cf, CF), (kb, KB)):
        P.I("sp", "dma_start", [src], [dst], out=dst.full(), in_=src.full(), dma=dst)
    sc = P.sbuf("sc", [128, 32, 2], BF16)
    P.I("act", "activation", [vec], [sc], out=sc[:, :, 0], in_=V("cT", 0, 32), func=AF.Silu)
    P.I("act", "activation", [vec], [sc], out=sc[:, :, 1], in_=V("ccT", 0, 32), func=AF.Silu, join=True)
    wa = [P.sbuf("wa%d" % i, [128, 32, 1024], BF16) for i in range(2)]
    psmod = P.psum("psmod", [128, 256, 2])
    for nb in range(24):
        w = wa[nb % 2]
        P.I("pool", "dma_start", [WADA], [w], out=w[:, :, :], in_=WADAv[:, :, nb * 1024:(nb + 1) * 1024], dma=w)
        for m in range(8):
            for k in range(32):
                P.I("pe", "matmul", [w, sc], [psmod], psmod[:, nb * 8 + m, :], lhsT=w[:, k, m * 128:(m + 1) * 128],
                    rhs=sc[:, k, :], start=(k == 0), stop=(k == 31))
    for j in range(2):
        P.I("dve", "tensor_tensor", [psmod, vec], [mod], out=mod[:, :, j], in0=psmod[:, 0:192, j],
            in1=V("bada", 0, 192), op=ALU.add, join=True)
    for di, (sl, j, g) in enumerate(((slice(32, 64), 0, "g1"), (slice(32, 64), 1, "g1"), (slice(128, 160), 0, "g2"))):
        P.I("dve", "tensor_scalar", [mod], [der], out=der[:, 3, :], in0=mod[:, sl, j], scalar1=1.0, scalar2=None, op0=ALU.add)
        P.I("dve", "tensor_tensor", [der, vec], [der], out=der[:, di, :], in0=der[:, 3, :], in1=V(g, 0, 32), op=ALU.mult)
    P.I("dve", "tensor_scalar", [vec], [sml], out=sml[:, 0:1], in0=V("gq"), scalar1=0.125, scalar2=None, op0=ALU.mult)
    P.I("dve", "tensor_scalar", [vec], [sml], out=sml[:, 1:2], in0=V("gs"), scalar1=1.0 - LAM_INIT, scalar2=None, op0=ALU.mult)
    lamt = P.sbuf("lamt", [128, 128], F32)
    P.I("dve", "tensor_tensor", [vec], [lamt], out=lamt[:, 0:64], in0=V("lam", 0, 64), in1=V("lam", 64, 64), op=ALU.mult)
    P.I("dve", "tensor_tensor", [vec], [lamt], out=lamt[:, 64:128], in0=V("lam", 128, 64), in1=V("lam", 192, 64), op=ALU.mult)
    P.I("dve", "tensor_reduce", [lamt], [sml], out=sml[:, 3:4], in_=lamt[:, 0:64], axis=mybir.AxisListType.X, op=ALU.add)
    P.I("dve", "tensor_reduce", [lamt], [sml], out=sml[:, 4:5], in_=lamt[:, 64:128], axis=mybir.AxisListType.X, op=ALU.add)
    P.I("act", "activation", [sml], [sml], out=sml[:, 5:7], in_=sml[:, 3:5], func=AF.Exp)
    P.I("dve", "tensor_tensor", [sml], [sml], out=sml[:, 7:8], in0=sml[:, 6:7], in1=sml[:, 5:6], op=ALU.subtract)
    P.I("dve", "tensor_scalar", [sml], [sml], out=sml[:, 2:3], in0=sml[:, 7:8], scalar1=-LAM_INIT, scalar2=None, op0=ALU.add)
    P.phase_end()
    if stop_after <= 0:
        return _finish(P, nc, OUT, dbg, {"mod": mod, "sml": sml})

    P.phase_begin()
    xt = [P.sbuf("xt%d" % i, [128, D], F32) for i in range(2)]
    junk = P.sbuf("junk", [128, D], BF16)
    xsb = [P.sbuf("xsb%d" % i, [128, D], BF16) for i in range(2)]
    ss = [P.sbuf("ss%d" % i, [128, 4], F32) for i in range(2)]
    hst = [P.sbuf("hst%d" % i, [128, 32, 128], BF16) for i in range(2)]
    xst = [P.sbuf("xst%d" % i, [128, 32, 128], F32) for i in range(2)]
    ptb = [P.psum("ptb%d" % i, [128, 8, 128], BF16) for i in range(2)]
    ptf = [P.psum("ptf%d" % i, [128, 4, 128], F32) for i in range(2)]
    epsb = P.sbuf("epsb", [128, 1], F32)
    P.I("dve", "memset", [], [epsb], epsb[:, :], EPS6)
    P.I("sp", "dma_start", [XS], [xt[0]], out=xt[0][:, :], in_=XS[0:128, :], dma=xt[0])
    for t in range(NT):
        i = t % 2
        x = xt[i]
        if t + 1 < NT:
            xn = xt[(t + 1) % 2]
            P.I("sp", "dma_start", [XS], [xn], out=xn[:, :], in_=XS[(t + 1) * 128:(t + 2) * 128, :], dma=xn)
        P.I("dve", "memset", [], [ss[i]], ss[i][:, :], 0.0)
        P.I("act", "activation", [x, ss[i]], [junk, ss[i]], out=junk[:, :], in_=x[:, :], func=AF.Square, accum_out=ss[i][:, 0:1])
        P.I("act", "activation", [ss[i], epsb], [ss[i]], out=ss[i][:, 1:2], in_=ss[i][:, 0:1], func=AF.Sqrt, scale=1.0 / D, bias=epsb[:, 0:1])
        P.I("dve", "reciprocal", [ss[i]], [ss[i]], out=ss[i][:, 2:3], in_=ss[i][:, 1:2])
        P.I("dve", "tensor_scalar", [x, ss[i]], [xsb[i]], out=xsb[i][:, :], in0=x[:, :], scalar1=ss[i][:, 2:3], scalar2=None, op0=ALU.mult)
        isctx = 17 <= t < 19
        Aa, Bb = (A1c, B1c) if isctx else (A1x, B1x)
        for g in range(4):
            pb = ptb[g % 2]
            for c8 in range(8):
                c = g * 8 + c8
                P.I("pe", "transpose", [xsb[i], cb], [pb], out=pb[:, c8, :], in_=xsb[i][:, c * 128:(c + 1) * 128], identity=identb)
            for c8 in range(8):
                c = g * 8 + c8
                P.I("dve", "tensor_scalar", [pb, mod, der], [hst[i]], out=hst[i][:, c, :], in0=pb[:, c8, :], scalar1=Aa[:, c:c + 1],
                    scalar2=Bb[:, c:c + 1], op0=ALU.mult, op1=ALU.add, join=True)
        for q4 in range(4):
            P.I("sp", "dma_start", [hst[i]], [H1T], out=H1Tv[:, q4 * 8:(q4 + 1) * 8, t * 128:(t + 1) * 128], in_=hst[i][:, q4 * 8:(q4 + 1) * 8, :],
                dma=hst[i], join=True)
        if t < 17:
            for g in range(8):
                pf = ptf[g % 2]
                for c4 in range(4):
                    c = g * 4 + c4
                    P.I("pe", "transpose", [x, cf], [pf], out=pf[:, c4, :], in_=x[:, c * 128:(c + 1) * 128], identity=identf)
                eng = "act" if g % 2 == 0 else "dve"
                if eng == "act":
                    P.I("act", "copy", [pf], [xst[i]], out=xst[i][:, g * 4:(g + 1) * 4, :], in_=pf[:, :, :], join=True)
                else:
                    P.I("dve", "tensor_copy", [pf], [xst[i]], out=xst[i][:, g * 4:(g + 1) * 4, :], in_=pf[:, :, :], join=True)
            for q4 in range(4):
                P.I("sp", "dma_start", [xst[i]], [XT], out=XTv[:, q4 * 8:(q4 + 1) * 8, t * 128:(t + 1) * 128], in_=xst[i][:, q4 * 8:(q4 + 1) * 8, :],
                    dma=xst[i], join=True)
    P.phase_end()
    if stop_after <= 1:
        return _finish(P, nc, OUT, dbg, {})

    def qk_post_a(T, ps, n, gcol):
        sq, sd, kg, t1, t2, psN, psR = T[:7]
        P.I("act", "activation", [ps], [sq], out=sq[:, 0:n], in_=ps[:, 0:n], func=AF.Square)
        P.I("pe", "matmul", [sq, cb], [psN], psN[:, 0:n], lhsT=ones64, rhs=sq[:, 0:n], start=True, stop=True)
        P.I("act", "activation", [psN, T[7]], [sd], out=sd[:, 0:n], in_=psN[:, 0:n], func=AF.Sqrt, bias=T[7][:, 0:1], scale=1.0)
        P.I("dve", "reciprocal", [sd], [sd], out=sd[:, 0:n], in_=sd[:, 0:n])
        P.I("dve", "scalar_tensor_tensor", [ps, sd, vec, sml], [kg], out=kg[:, 0:n], in0=ps[:, 0:n], scalar=gcol, in1=sd[:, 0:n],
            op0=ALU.mult, op1=ALU.mult)

    def qk_post_b(T, n, cs, dst_ap, dst_buf):
        sq, sd, kg, t1, t2, psN, psR = T[:7]
        P.I("pe", "matmul", [kg, cb], [psR], psR[:, 0:n], lhsT=pswap, rhs=kg[:, 0:n], start=True, stop=True)
        P.I("pool", "tensor_tensor", [kg, cs], [t1], out=t1[:, 0:n], in0=kg[:, 0:n], in1=cs[:, 0, 0:n], op=ALU.mult)
        P.I("dve", "tensor_tensor", [psR, cs], [t2], out=t2[:, 0:n], in0=psR[:, 0:n], in1=cs[:, 1, 0:n], op=ALU.mult)
        P.I("pool", "tensor_tensor", [t1, t2], [dst_buf], out=dst_ap, in0=t1[:, 0:n], in1=t2[:, 0:n], op=ALU.add)

    P.phase_begin()
    hb = [P.sbuf("hb%d" % i, [128, 32, 512], BF16) for i in range(2)]
    wk = [P.sbuf("wk0", [128, 32, 512], BF16)]
    wv = [P.sbuf("wv0", [128, 32, 512], BF16)]
    csb = [P.sbuf("csb%d" % i, [128, 2, 512], F32) for i in range(2)]
    psK = [P.psum("psK%d" % i, [128, 512]) for i in range(4)]
    psV = [P.psum("psV%d" % i, [128, 512]) for i in range(2)]
    epsq = P.sbuf("epsq", [128, 1], F32)
    P.I("dve", "memset", [], [epsq], epsq[:, :], EPS6)
    psN_, psR_ = P.psum("psN", [128, 512]), P.psum("psR", [128, 512])
    t1_, t2_ = P.sbuf("t1", [128, 512], F32), P.sbuf("t2", [128, 512], F32)
    TT = [(P.sbuf("sq%d" % i, [128, 512], BF16), P.sbuf("sd%d" % i, [128, 512], F32), P.sbuf("kg%d" % i, [128, 512], BF16),
           t1_, t2_, psN_, psR_, epsq) for i in range(4)]
    kst = [P.sbuf("kst%d" % i, [128, 512], BF16) for i in range(4)]
    vst = [P.sbuf("vst%d" % i, [128, 4, 512], BF16) for i in range(2)]
    cnt = 0
    for g in range(4):
        P.I("pool", "dma_start", [WIN], [wk[0]], out=wk[0][:, :, :], in_=WINv[:, :, 2048 + g * 512:2048 + (g + 1) * 512], dma=wk[0])
        P.I("pool", "dma_start", [WIN], [wv[0]], out=wv[0][:, :, :], in_=WINv[:, :, 4096 + g * 512:4096 + (g + 1) * 512], dma=wv[0])
        for bl in range(17):
            s0 = bl * 512
            n = min(512, NS - s0)
            h = hb[cnt % 2]
            cs = csb[cnt % 2]
            cnt += 1
            P.I("sp", "dma_start", [H1T], [h], out=h[:, :, 0:n], in_=H1Tv[:, :, s0:s0 + n], dma=h)
            P.I("sp", "dma_start", [COS], [cs], out=cs[:, 0, 0:n], in_=COS[:, s0:s0 + n], dma=cs)
            P.I("sp", "dma_start", [SIN], [cs], out=cs[:, 1, 0:n], in_=SIN[:, s0:s0 + n], dma=cs, join=True)
            for hh in range(4):
                pk = psK[hh]
                for k in range(32):
                    P.I("pe", "matmul", [wk[0], h], [pk], pk[:, 0:n], lhsT=wk[0][:, k, hh * 128:(hh + 1) * 128], rhs=h[:, k, 0:n],
                        start=(k == 0), stop=(k == 31))
            vs_ = vst[bl % 2]
            ntile = n // 128

            def vtile(tt):
                if tt >= ntile:
                    return
                pv = psV[tt % 2]
                for k in range(32):
                    P.I("pe", "matmul", [wv[0], h], [pv], pv[:, :], lhsT=h[:, k, tt * 128:(tt + 1) * 128], rhs=wv[0][:, k, :],
                        start=(k == 0), stop=(k == 31))
                if tt % 2 == 0:
                    P.I("act", "copy", [pv], [vs_], out=vs_[:, tt, :], in_=pv[:, :], join=True)
                else:
                    P.I("dve", "tensor_copy", [pv], [vs_], out=vs_[:, tt, :], in_=pv[:, :], join=True)

            for hh in range(4):
                qk_post_a(TT[hh], psK[hh], n, V("gk"))
                vtile(hh)
            for hh in range(4):
                ks = kst[hh]
                qk_post_b(TT[hh], n, cs, ks[:, 0:n], ks)
                P.I("pool", "dma_start", [ks], [KT], out=KT[g * 4 + hh, :, s0:s0 + n], in_=ks[:, 0:n], dma=ks, join=True)
            for hh in range(4):
                P.I("pool", "dma_start", [vs_], [VS], out=VS[g * 4 + hh, :, bl * 4:bl * 4 + ntile, :],
                    in_=vs_[:, 0:ntile, hh * 128:(hh + 1) * 128], dma=vs_, join=True)
    for g in range(4):
        P.I("pool", "dma_start", [WIN], [wk[0]], out=wk[0][:, :, :], in_=WINv[:, :, g * 512:(g + 1) * 512], dma=wk[0])
        for bl in range(5):
            s0 = QO + bl * 410
            n = 410
            h = hb[cnt % 2]
            cs = csb[cnt % 2]
            cnt += 1
            P.I("sp", "dma_start", [H1T], [h], out=h[:, :, 0:n], in_=H1Tv[:, :, s0:s0 + n], dma=h)
            P.I("sp", "dma_start", [COS], [cs], out=cs[:, 0, 0:n], in_=COS[:, s0:s0 + n], dma=cs)
            P.I("sp", "dma_start", [SIN], [cs], out=cs[:, 1, 0:n], in_=SIN[:, s0:s0 + n], dma=cs, join=True)
            for hh in range(4):
                pk = psK[hh]
                for k in range(32):
                    P.I("pe", "matmul", [wk[0], h], [pk], pk[:, 0:n], lhsT=wk[0][:, k, hh * 128:(hh + 1) * 128], rhs=h[:, k, 0:n],
                        start=(k == 0), stop=(k == 31))
            for hh in range(4):
                qk_post_a(TT[hh], psK[hh], n, sml[:, 0:1])
            for hh in range(4):
                ks = kst[hh]
                qk_post_b(TT[hh], n, cs, ks[:, 0:n], ks)
                P.I("pool", "dma_start", [ks], [QT], out=QT[g * 4 + hh, :, bl * 410:bl * 410 + n], in_=ks[:, 0:n], dma=ks, join=True)
    P.phase_end()
    if stop_after <= 2:
        return _finish(P, nc, OUT, dbg, {})

    P.phase_begin()
    val = P.sbuf("val", [128, NEXT], F32)
    P.I("sp", "dma_start", [VAL], [val], out=val.full(), in_=VAL.full(), dma=val)
    P.I("dve", "tensor_copy", [val], [sml], out=sml[:, 8:9], in_=val[:, 63:64])
    P.I("dve", "tensor_copy", [val], [sml], out=sml[:, 9:10], in_=val[:, 2112:2113])
    hb = [P.sbuf("hb%d" % i, [128, 32, 416], BF16) for i in range(2)]
    wc = [P.sbuf("wc%d" % i, [128, 32, 256], BF16) for i in range(2)]
    psA = [P.psum("psA%d" % i, [128, 512]) for i in range(2)]
    psG = [P.psum("psG%d" % i, [128, 512]) for i in range(2)]
    sg = [P.sbuf("sg%d" % i, [128, 416], F32) for i in range(2)]
    ub = [P.sbuf("ub%d" % i, [128, NU], BF16) for i in range(2)]
    dg = [P.sbuf("dg%d" % i, [128, 31, 128], BF16) for i in range(2)]
    psC = [P.psum("psC%d" % i, [128, 512]) for i in range(2)]
    acc = [P.sbuf("acc%d" % i, [128, NQ], F32) for i in range(2)]
    ysq = P.sbuf("ysq", [128, NQ], F32)
    s1 = P.sbuf("s1", [128, NQ], F32)
    s2 = P.sbuf("s2", [128, NQ], F32)
    P.I("dve", "memset", [], [s1], s1[:, :], 0.0)
    P.I("dve", "memset", [], [s2], s2[:, :], 0.0)
    cnt = 0
    for c in range(16):
        w = wc[c % 2]
        P.I("pool", "dma_start", [WIN], [w], out=w[:, :, 0:128], in_=WINv[:, :, 6144 + c * 128:6144 + (c + 1) * 128], dma=w)
        P.I("pool", "dma_start", [WIN], [w], out=w[:, :, 128:256], in_=WINv[:, :, 8192 + c * 128:8192 + (c + 1) * 128], dma=w, join=True)
        u = ub[c % 2]
        for bl in range(5):
            s0 = UO + bl * 416
            n = 416
            h = hb[cnt % 2]
            pa, pg, sgi = psA[cnt % 2], psG[cnt % 2], sg[cnt % 2]
            cnt += 1
            P.I("sp", "dma_start", [H1T], [h], out=h[:, :, 0:n], in_=H1Tv[:, :, s0:s0 + n], dma=h)
            for k in range(32):
                P.I("pe", "matmul", [w, h], [pa], pa[:, 0:n], lhsT=w[:, k, 0:128], rhs=h[:, k, 0:n], start=(k == 0), stop=(k == 31))
            for k in range(32):
                P.I("pe", "matmul", [w, h], [pg], pg[:, 0:n], lhsT=w[:, k, 128:256], rhs=h[:, k, 0:n], start=(k == 0), stop=(k == 31))
            P.I("act", "activation", [pg], [sgi], out=sgi[:, 0:n], in_=pg[:, 0:n], func=AF.Sigmoid)
            P.I("dve", "tensor_tensor", [pa, sgi], [u], out=u[:, bl * 416:bl * 416 + n], in0=pa[:, 0:n], in1=sgi[:, 0:n], op=ALU.mult, join=True)
        P.I("dve", "tensor_tensor", [u, val], [u], out=u[:, 0:16], in0=u[:, 0:16], in1=val[:, 48:64], op=ALU.mult)
        P.I("dve", "tensor_tensor", [u, val], [u], out=u[:, 2064:2080], in0=u[:, 2064:2080], in1=val[:, 2112:2128], op=ALU.mult)
        a = acc[c % 2]
        dgc = dg[c % 2]
        for k in range(31):
            P.I("dve", "tensor_scalar", [cb, vec], [dgc], out=dgc[:, k, :], in0=identb, scalar1=V("cw", k * 16 + c), scalar2=None,
                op0=ALU.mult, join=True)
        for bl in range(5):
            pc = psC[bl % 2]
            for k in range(31):
                P.I("pe", "matmul", [dgc, u], [pc], pc[:, 0:410], lhsT=dgc[:, k, :], rhs=u[:, bl * 410 + k:bl * 410 + k + 410],
                    start=(k == 0), stop=(k == 30))
            P.I("act", "activation", [pc, vec], [a], out=a[:, bl * 410:(bl + 1) * 410], in_=pc[:, 0:410], func=AF.Identity,
                bias=V("cb", c), scale=1.0, join=True)
        P.I("pool", "tensor_tensor", [s1, a], [s1], out=s1[:, :], in0=s1[:, :], in1=a[:, :], op=ALU.add)
        P.I("act", "activation", [a], [ysq], out=ysq[:, :], in_=a[:, :], func=AF.Square)
        P.I("pool", "tensor_tensor", [s2, ysq], [s2], out=s2[:, :], in0=s2[:, :], in1=ysq[:, :], op=ALU.add)
        P.I("pool", "dma_start", [a], [YS], out=YS[c, :, :], in_=a[:, :], dma=a, join=True)
    mean = P.sbuf("mean", [128, NQ], F32)
    rstd = P.sbuf("rstdc", [128, NQ], F32)
    eps5 = P.sbuf("eps5", [128, 1], F32)
    P.I("dve", "memset", [], [eps5], eps5[:, :], EPS5)
    for bl in range(5):
        sl = slice(bl * 410, bl * 410 + 410)
        pa, pg = psA[bl % 2], psG[bl % 2]
        P.I("pe", "matmul", [s1, cf], [pa], pa[:, 0:410], lhsT=onesDf, rhs=s1[:, sl], start=True, stop=True)
        P.I("pe", "matmul", [s2, cf], [pg], pg[:, 0:410], lhsT=onesDf, rhs=s2[:, sl], start=True, stop=True)
        P.I("act", "activation", [pa], [mean], out=mean[:, sl], in_=pa[:, 0:410], func=AF.Identity, scale=1.0 / 2048, join=True)
        P.I("act", "activation", [pg], [rstd], out=rstd[:, sl], in_=pg[:, 0:410], func=AF.Identity, scale=1.0 / 2048, join=True)
    P.I("dve", "tensor_tensor", [mean], [ysq], out=ysq[:, :], in0=mean[:, :], in1=mean[:, :], op=ALU.mult)
    P.I("dve", "tensor_tensor", [rstd, ysq], [rstd], out=rstd[:, :], in0=rstd[:, :], in1=ysq[:, :], op=ALU.subtract)
    P.I("act", "activation", [rstd, eps5], [rstd], out=rstd[:, :], in_=rstd[:, :], func=AF.Sqrt, bias=eps5[:, 0:1], scale=1.0)
    P.I("dve", "reciprocal", [rstd], [rstd], out=rstd[:, :], in_=rstd[:, :])
    P.I("dve", "tensor_tensor", [mean, rstd], [mean], out=mean[:, :], in0=mean[:, :], in1=rstd[:, :], op=ALU.mult)
    cst = [P.sbuf("cst%d" % i, [128, NQ], BF16) for i in range(2)]
    for c in range(16):
        a = acc[c % 2]
        P.I("sp", "dma_start", [YS], [a], out=a[:, :], in_=YS[c, :, :], dma=a)
        P.I("dve", "tensor_tensor", [a, rstd], [a], out=a[:, :], in0=a[:, :], in1=rstd[:, :], op=ALU.mult)
        P.I("pool", "tensor_tensor", [a, mean], [a], out=a[:, :], in0=a[:, :], in1=mean[:, :], op=ALU.subtract)
        o = cst[c % 2]
        P.I("dve", "tensor_scalar", [a, vec], [a], out=a[:, :], in0=a[:, :], scalar1=V("lg", c), scalar2=V("lb", c), op0=ALU.mult, op1=ALU.add)
        P.I("act", "activation", [a], [o], out=o[:, :], in_=a[:, :], func=AF.Silu)
        P.I("pool", "dma_start", [o], [CAT], out=CAT[16 + c, :, :], in_=o[:, :], dma=o, join=True)
    P.phase_end()
    if stop_after <= 3:
        return _finish(P, nc, OUT, dbg, {})

    P.phase_begin()
    kt = [P.sbuf("kt%d" % i, [128, NS], BF16) for i in range(2)]
    vt = [P.sbuf("vt%d" % i, [128, NT, 128], BF16) for i in range(2)]
    qt = [P.sbuf("qt%d" % i, [128, NQ], BF16) for i in range(2)]
    psS = [P.psum("psS%d" % i, [128, 2, 512]) for i in range(2)]
    psO = [P.psum("psO%d" % i, [128, 512]) for i in range(2)]
    psL = [P.psum("psL%d" % i, [128, 512]) for i in range(2)]
    pT = [P.sbuf("pT%d" % i, [128, 2, 410], BF16) for i in range(3)]
    rr = [P.sbuf("rr%d" % i, [128, 410], F32) for i in range(2)]
    o_ = P.sbuf("o_", [128, 410], F32)
    r32 = P.sbuf("r32", [128, 410], F32)
    osq = P.sbuf("osq", [128, 410], F32)
    ast = [P.sbuf("ast%d" % i, [128, 410], BF16) for i in range(2)]
    eps6 = P.sbuf("eps6a", [128, 1], F32)
    P.I("dve", "memset", [], [eps6], eps6[:, :], EPS6)
    stf = [P.sbuf("stf%d" % i, [128, 2048], F32) for i in range(3)]
    stb = [P.sbuf("stb%d" % i, [128, 2048], BF16) for i in range(3)]
    csteps = []
    for kk in range(32):
        for half in range(2):
            for c6 in range(6):
                jg0 = c6 * 8
                nj = min(8, 43 - jg0)
                src = WUP[kk * 128:(kk + 1) * 128, half * DFF + jg0 * 256:half * DFF + (jg0 + nj) * 256]
                dst = WUPB.ap[jg0:jg0 + nj, :, kk, half * 256:(half + 1) * 256].rearrange("j p c -> p j c")
                csteps.append((WUP, src, WUPB, dst, nj, 256))
    for kk in range(NJ):
        for hh in range(2):
            src = WDN[kk * 128:(kk + 1) * 128, hh * 2048:(hh + 1) * 2048]
            dst = WDNB.ap[hh * 16:(hh + 1) * 16, :, kk, :].rearrange("m p c -> p m c")
            csteps.append((WDN, src, WDNB, dst, 16, 128))
    cstate = [0]

    def conv_store(i):
        SB, src, DB, dst, nj, w = csteps[i]
        b_ = stb[i % 3]
        P.I("sp", "dma_start", [b_], [DB], out=dst, in_=b_[:, 0:nj * w].rearrange("p (j c) -> p j c", c=w), dma=b_, join=True)

    def conv_advance():
        i = cstate[0]
        if i > len(csteps):
            return
        if i < len(csteps):
            SB, src, DB, dst, nj, w = csteps[i]
            f_, b_ = stf[i % 3], stb[i % 3]
            P.I("sp", "dma_start", [SB], [f_], out=f_[:, 0:nj * w], in_=src, dma=f_)
            P.I("pool", "tensor_copy", [f_], [b_], out=b_[:, 0:nj * w], in_=f_[:, 0:nj * w])
        if i >= 1:
            conv_store(i - 1)
        cstate[0] = i + 1

    ucnt = 0
    for head in range(16):
        hi = head % 2
        k_, v_, q_ = kt[hi], vt[hi], qt[hi]
        P.I("sp", "dma_start", [KT], [k_], out=k_[:, :], in_=KT[head, :, :], dma=k_)
        P.I("sp", "dma_start", [VS], [v_], out=v_[:, :, :], in_=VS[head, :, :, :], dma=v_)
        P.I("sp", "dma_start", [QT], [q_], out=q_[:, :], in_=QT[head, :, :], dma=q_)
        for qb in range(5):
            qc = qb * 410
            n = 410
            pend = None
            for t in range(NT + 1):
                if t < NT:
                    ps = psS[ucnt % 2]
                    pt_ = pT[ucnt % 3]
                    ucnt += 1
                    P.I("pe", "matmul", [k_, q_], [ps], ps[:, 0, 0:n], lhsT=k_[0:64, t * 128:(t + 1) * 128], rhs=q_[0:64, qc:qc + n],
                        start=True, stop=True)
                    P.I("pe", "matmul", [k_, q_], [ps], ps[:, 1, 0:n], lhsT=k_[64:128, t * 128:(t + 1) * 128], rhs=q_[64:128, qc:qc + n],
                        start=True, stop=True)
                    P.I("act", "activation", [ps, kb], [pt_], out=pt_[:, :, :], in_=ps[:, :, 0:n], func=AF.Exp, bias=kb[:, t:t + 1], scale=1.0)
                    cur = (t, pt_)
                    if ucnt % 9 == 0:
                        conv_advance()
                else:
                    cur = None
                if pend is not None:
                    tp, pp = pend
                    st, sp_ = (tp == 0), (tp == NT - 1)
                    for s in range(2):
                        P.I("pe", "matmul", [v_, pp], [psO[s]], psO[s][:, 0:n], lhsT=v_[:, tp, :], rhs=pp[:, s, :], start=st, stop=sp_)
                    for s in range(2):
                        P.I("pe", "matmul", [cb, pp], [psL[s]], psL[s][32 * s:32 * s + 32, 0:n], lhsT=onesb[:, 32 * s:32 * s + 32], rhs=pp[:, s, :],
                            start=st, stop=sp_, tile_position=(0, 32 * s))
                pend = cur
            for s in range(2):
                P.I("dve", "reciprocal", [psL[s]], [r32], out=r32[32 * s:32 * s + 32, :], in_=psL[s][32 * s:32 * s + 32, 0:n], join=True)
            for s in range(2):
                P.I("pe", "matmul", [r32, cf], [psL[s]], psL[s][:, 0:n], lhsT=onesDf[32 * s:32 * s + 32, :], rhs=r32[32 * s:32 * s + 32, :],
                    start=True, stop=True)
            for s in range(2):
                P.I("act", "activation", [psL[s]], [rr[s]], out=rr[s][:, :], in_=psL[s][:, 0:n], func=AF.Identity, scale=1.0 / 32)
                P.I("dve", "tensor_tensor", [psO[s], rr[s]], [rr[s]], out=rr[s][:, :], in0=psO[s][:, 0:n], in1=rr[s][:, :], op=ALU.mult)
            P.I("dve", "scalar_tensor_tensor", [rr[0], rr[1], sml], [o_], out=o_[:, :], in0=rr[1][:, :], scalar=sml[:, 2:3], in1=rr[0][:, :],
                op0=ALU.mult, op1=ALU.add)
            P.I("act", "activation", [o_], [osq], out=osq[:, :], in_=o_[:, :], func=AF.Square)
            P.I("pe", "matmul", [osq, cf], [psL[0]], psL[0][:, 0:n], lhsT=ones128f, rhs=osq[:, :], start=True, stop=True)
            P.I("act", "activation", [psL[0], eps6], [osq], out=osq[:, :], in_=psL[0][:, 0:n], func=AF.Sqrt, bias=eps6[:, 0:1], scale=1.0)
            P.I("dve", "reciprocal", [osq], [osq], out=osq[:, :], in_=osq[:, :])
            a_ = ast[(head * 5 + qb) % 2]
            P.I("dve", "scalar_tensor_tensor", [o_, osq, sml], [a_], out=a_[:, :], in0=o_[:, :], scalar=sml[:, 1:2], in1=osq[:, :],
                op0=ALU.mult, op1=ALU.mult)
            P.I("sp", "dma_start", [a_], [CAT], out=CAT[head, :, qc:qc + n], in_=a_[:, :], dma=a_, join=True)
    while cstate[0] <= len(csteps):
        conv_advance()
    P.phase_end()
    if stop_after <= 4:
        return _finish(P, nc, OUT, dbg, {})

    P.phase_begin()
    cbk = [P.sbuf("cbk%d" % i, [128, 32, 410], BF16) for i in range(2)]
    wo = [P.sbuf("wo%d" % i, [128, 32, 512], BF16) for i in range(2)]
    xtb = [P.sbuf("xtb%d" % i, [128, 4, 410], F32) for i in range(2)]
    x1b = [P.sbuf("x1b%d" % i, [128, 4, 410], F32) for i in range(2)]
    psM = [P.psum("psM%d" % i, [128, 512]) for i in range(4)]
    sqb = P.sbuf("sqb", [128, 4, 410], F32)
    s2 = P.sbuf("s2n", [128, NQ], F32)
    P.I("dve", "memset", [], [s2], s2[:, :], 0.0)
    CATv = CAT.ap.rearrange("c p s -> p c s")
    X1Tv = X1T.ap.rearrange("c p s -> p c s")
    cnt = 0
    for mg in range(8):
        w = wo[mg % 2]
        P.I("pool", "dma_start", [WOUT], [w], out=w[:, :, :], in_=WOUTv[:, :, mg * 512:(mg + 1) * 512], dma=w)
        for bl in range(5):
            sl = slice(bl * 410, bl * 410 + 410)
            cbl, xb, x1 = cbk[cnt % 2], xtb[cnt % 2], x1b[cnt % 2]
            cnt += 1
            P.I("sp", "dma_start", [CAT], [cbl], out=cbl[:, :, :], in_=CATv[:, :, sl], dma=cbl)
            P.I("sp", "dma_start", [XT], [xb], out=xb[:, :, :], in_=XTv[:, mg * 4:(mg + 1) * 4, QO + bl * 410:QO + bl * 410 + 410], dma=xb)
            for m in range(4):
                pm = psM[m]
                for k in range(32):
                    P.I("pe", "matmul", [w, cbl], [pm], pm[:, 0:410], lhsT=w[:, k, m * 128:(m + 1) * 128], rhs=cbl[:, k, :], start=(k == 0), stop=(k == 31))
                P.I("dve", "scalar_tensor_tensor", [pm, xb, mod], [x1], out=x1[:, m, :], in0=pm[:, 0:410], scalar=gate1[:, mg * 4 + m:mg * 4 + m + 1],
                    in1=xb[:, m, :], op0=ALU.mult, op1=ALU.add, join=True)
            P.I("act", "activation", [x1], [sqb], out=sqb[:, :, :], in_=x1[:, :, :], func=AF.Square)
            for m in range(4):
                P.I("pool", "tensor_tensor", [s2, sqb], [s2], out=s2[:, sl], in0=s2[:, sl], in1=sqb[:, m, :], op=ALU.add)
            P.I("pool", "dma_start", [x1], [X1T], out=X1Tv[:, mg * 4:(mg + 1) * 4, sl], in_=x1[:, :, :], dma=x1, join=True)
    eps6b = P.sbuf("eps6b", [128, 1], F32)
    P.I("dve", "memset", [], [eps6b], eps6b[:, :], EPS6)
    for bl in range(5):
        sl = slice(bl * 410, bl * 410 + 410)
        pm = psM[bl % 4]
        P.I("pe", "matmul", [s2, cf], [pm], pm[:, 0:410], lhsT=onesDf, rhs=s2[:, sl], start=True, stop=True)
        P.I("act", "activation", [pm, eps6b], [rstd2], out=rstd2[:, sl], in_=pm[:, 0:410], func=AF.Sqrt, bias=eps6b[:, 0:1], scale=1.0 / D, join=True)
    P.I("dve", "reciprocal", [rstd2], [rstd2], out=rstd2[:, :], in_=rstd2[:, :])
    P.phase_end()
    if stop_after <= 5:
        return _finish(P, nc, OUT, dbg, {})

    P.phase_begin()
    h2 = P.sbuf("h2", [128, 32, 412], BF16)
    x1s = [P.sbuf("x1s%d" % i, [128, 4, 412], F32) for i in range(1)]
    Ab = P.sbuf("Ab", [128, NJ, 410], BF16)
    wb = [P.sbuf("wb%d" % i, [128, 32, 512], BF16) for i in range(2)]
    psU = [P.psum("psU%d" % i, [128, 512]) for i in range(2)]
    psGt = [P.psum("psGt%d" % i, [128, 512]) for i in range(2)]
    psY = [P.psum("psY%d" % i, [128, 512]) for i in range(2)]
    psT = [P.psum("psT%d" % i, [128, 4, 128]) for i in range(2)]
    cu = [P.sbuf("cu%d" % i, [128, 410], F32) for i in range(2)]
    cg = [P.sbuf("cg%d" % i, [128, 410], F32) for i in range(2)]
    sgt = [P.sbuf("sgt%d" % i, [128, 410], F32) for i in range(2)]
    x1r = [P.sbuf("x1r%d" % i, [128, 410], F32) for i in range(2)]
    yT = [P.sbuf("yT%d" % i, [128, 410], F32) for i in range(2)]
    ost = [P.sbuf("ost%d" % i, [128, 4, 128], F32) for i in range(2)]
    b2v = mod.ap
    wcnt = 0
    for b in range(5):
        c0 = b * 410
        nin = min(412, NQ - c0)
        nout = nin - 2
        for cg4 in range(8):
            xs_ = x1s[0]
            P.I("pool", "dma_start", [X1T], [xs_], out=xs_[:, :, 0:nin], in_=X1Tv[:, cg4 * 4:(cg4 + 1) * 4, c0:c0 + nin], dma=xs_)
            for c4 in range(4):
                c = cg4 * 4 + c4
                P.I("dve", "tensor_tensor", [xs_, rstd2], [xs_], out=xs_[:, c4, 0:nin], in0=xs_[:, c4, 0:nin], in1=rstd2[:, c0:c0 + nin], op=ALU.mult, join=True)
                P.I("dve", "tensor_scalar", [xs_, der, mod], [h2], out=h2[:, c, 0:nin], in0=xs_[:, c4, 0:nin], scalar1=A2[:, c:c + 1],
                    scalar2=B2[:, c:c + 1], op0=ALU.mult, op1=ALU.add, join=True)
        if b == 0:
            for c in range(32):
                P.I("dve", "tensor_scalar", [h2, sml], [h2], out=h2[:, c, 0:1], in0=h2[:, c, 0:1], scalar1=sml[:, 8:9], scalar2=None, op0=ALU.mult, join=True)
        if b == 4:
            for c in range(32):
                P.I("dve", "tensor_scalar", [h2, sml], [h2], out=h2[:, c, nin - 1:nin], in0=h2[:, c, nin - 1:nin], scalar1=sml[:, 9:10],
                    scalar2=None, op0=ALU.mult, join=True)
        for jg in range(43):
            w = wb[wcnt % 2]
            wcnt += 1
            P.I("sp", "dma_start", [WUPB], [w], out=w[:, :, :], in_=WUPB[jg, :, :, :], dma=w)
            for jj in range(2):
                j = jg * 2 + jj
                pu, pg = psU[jj], psGt[jj]
                for k in range(32):
                    P.I("pe", "matmul", [w, h2], [pu], pu[:, 0:nin], lhsT=w[:, k, jj * 128:(jj + 1) * 128], rhs=h2[:, k, 0:nin], start=(k == 0), stop=(k == 31))
                for k in range(32):
                    P.I("pe", "matmul", [w, h2], [pg], pg[:, 0:nin], lhsT=w[:, k, 256 + jj * 128:256 + (jj + 1) * 128], rhs=h2[:, k, 0:nin],
                        start=(k == 0), stop=(k == 31))
                for (pp, dst, fo) in ((pu, cu[jj], j), (pg, cg[jj], NJ + j)):
                    P.I("dve", "tensor_scalar", [pp, vec], [dst], out=dst[:, 0:nout], in0=pp[:, 0:nout], scalar1=V("fw", 0 * 172 + fo), scalar2=V("fb", fo),
                        op0=ALU.mult, op1=ALU.add)
                    for kk in (1, 2):
                        P.I("dve", "scalar_tensor_tensor", [pp, vec, dst], [dst], out=dst[:, 0:nout], in0=pp[:, kk:kk + nout], scalar=V("fw", kk * 172 + fo),
                            in1=dst[:, 0:nout], op0=ALU.mult, op1=ALU.add)
                P.I("act", "activation", [cg[jj]], [sgt[jj]], out=sgt[jj][:, 0:nout], in_=cg[jj][:, 0:nout], func=AF.Silu)
                P.I("pool", "tensor_tensor", [sgt[jj], cu[jj]], [Ab], out=Ab[:, j, 0:nout], in0=sgt[jj][:, 0:nout], in1=cu[jj][:, 0:nout], op=ALU.mult, join=True)
        ntl = (nout + 127) // 128
        for m in range(32):
            w = wb[wcnt % 2]
            wcnt += 1
            wd = w.ap.rearrange("p a b -> p (a b)")[:, 0:NJ * 128].rearrange("p (k n) -> p k n", n=128)
            P.I("sp", "dma_start", [WDNB], [w], out=wd, in_=WDNB[m, :, :, :], dma=w)
            xr = x1r[m % 2]
            P.I("pool", "dma_start", [X1T], [xr], out=xr[:, 0:nout], in_=X1T[m, :, c0 + 1:c0 + 1 + nout], dma=xr)
            py = psY[m % 2]
            for k in range(NJ):
                P.I("pe", "matmul", [w, Ab], [py], py[:, 0:nout], lhsT=wd[:, k, :], rhs=Ab[:, k, 0:nout], start=(k == 0), stop=(k == NJ - 1))
            y = yT[m % 2]
            P.I("dve", "scalar_tensor_tensor", [py, xr, mod], [y], out=y[:, 0:nout], in0=py[:, 0:nout], scalar=gate2[:, m:m + 1], in1=xr[:, 0:nout],
                op0=ALU.mult, op1=ALU.add)
            pt_ = psT[m % 2]
            os_ = ost[m % 2]
            for tl in range(ntl):
                nt_ = min(128, nout - tl * 128)
                P.I("pe", "transpose", [y, cf], [pt_], out=pt_[0:nt_, tl, :], in_=y[:, tl * 128:tl * 128 + nt_], identity=identf)
            nfull = nout // 128
            P.I("act", "copy", [pt_], [os_], out=os_[:, 0:nfull, :], in_=pt_[:, 0:nfull, :])
            r0 = c0
            P.I("pool", "dma_start", [os_], [OUT], out=OUT.ap[r0:r0 + nfull * 128, m * 128:(m + 1) * 128].rearrange("(t p) f -> p t f", p=128),
                in_=os_[:, 0:nfull, :], dma=os_, join=True)
            rem = nout - nfull * 128
            if rem:
                P.I("dve", "tensor_copy", [pt_], [os_], out=os_[0:rem, nfull, :], in_=pt_[0:rem, nfull, :], join=True)
                P.I("pool", "dma_start", [os_], [OUT], out=OUT.ap[r0 + nfull * 128:r0 + nout, m * 128:(m + 1) * 128], in_=os_[0:rem, nfull, :],
                    dma=os_, join=True)
    P.wait_only("sp", [OUT])
    P.phase_end(final=True)
    P.ctx.close()
    return nc


def _finish(P, nc, OUT, dbg, taps):
    P.phase_begin()
    outs = [OUT]
    for name, buf in taps.items():
        shp = list(buf.ap.shape)
        t = Buf(nc.dram_tensor("tap_" + name, shp, buf.ap.dtype, kind="ExternalOutput"), "tap_" + name)
        P.I("sp", "dma_start", [buf], [t], out=t.full(), in_=buf.full(), dma=buf)
        outs.append(t)
    P.wait_only("sp", outs)
    P.phase_end(final=True)
    P.ctx.close()
    return nc


def _rope_tables():
    rows = SEQ // 64
    row = np.repeat(np.arange(rows), 64).astype(np.float32)
    col = np.tile(np.arange(64), rows).astype(np.float32)
    freqs = (np.float32(10000.0) ** (-np.arange(16, dtype=np.float32) / np.float32(16))).astype(np.float32)
    ang = np.concatenate([row[:, None] * freqs, col[:, None] * freqs], axis=-1).astype(np.float32)
    return np.cos(ang).astype(np.float32), np.sin(ang).astype(np.float32)


def _core_layout(qi):
    T0 = qi * 2048
    slot_tok = np.full(NS, -1, dtype=np.int64)
    ext = np.arange(T0 - 64, T0 + 2112)
    ok = (ext >= 0) & (ext < SEQ)
    slot_tok[0:NEXT] = np.where(ok, ext, -1)
    slot_tok[NEXT:NEXT + NCTX] = -2 - np.arange(NCTX)
    lo, hi = max(T0 - 64, 0), min(T0 + 2112, SEQ)
    others = np.concatenate([np.arange(0, lo), np.arange(hi, SEQ)])
    slot_tok[NEXT + NCTX:NEXT + NCTX + len(others)] = others
    return slot_tok


def _pm(v, n):
    return np.ascontiguousarray(np.asarray(v, dtype=np.float32).reshape(n, 128).T)


def make_in_maps(x, c, ctx, c_ctx, w_ada, b_ada, norm1_g, norm2_g, w_in, q_norm_g, k_norm_g,
                 lambda_q1, lambda_k1, lambda_q2, lambda_k2, subln_g, conv_dw_w, conv_dw_b,
                 conv_ln_g, conv_ln_b, w_out, w_up, ffn_dw_w, ffn_dw_b, w_down, cores=range(8)):
    x = np.asarray(x, dtype=np.float32)
    ctx = np.asarray(ctx, dtype=np.float32)
    cosT, sinT = _rope_tables()
    eye = np.eye(128, dtype=np.float32)
    sw = np.zeros((128, 128), np.float32)
    idx = np.arange(128)
    sw[idx, idx ^ 1] = 1.0
    o64 = np.zeros((128, 128), np.float32)
    o64[:64, :64] = 1.0 / 64
    o64[64:, 64:] = 1.0 / 64
    constb = np.concatenate([eye, sw, o64, np.ones((128, 128), np.float32)], axis=1).astype(ml_dtypes.bfloat16)
    constf = np.concatenate([eye, np.full((128, 128), 1.0 / 128, np.float32), np.ones((128, 128), np.float32)], axis=1)
    p = np.arange(128)
    pair = (p % 64) // 2
    sgn = np.where(p % 2 == 0, -1.0, 1.0).astype(np.float32)

    base = np.zeros((128, NV), np.float32)

    def put(name, arr):
        arr = np.asarray(arr, np.float32)
        base[:, _VOFF[name]:_VOFF[name] + arr.shape[1]] = arr

    put("g1", _pm(norm1_g[0], 32))
    put("g2", _pm(norm2_g[0], 32))
    put("bada", _pm(b_ada[0], 192))
    put("gq", np.asarray(q_norm_g[0], np.float32)[p % 64][:, None])
    put("gk", np.asarray(k_norm_g[0], np.float32)[p % 64][:, None])
    put("gs", np.asarray(subln_g[0], np.float32)[:, None])
    cw = np.asarray(conv_dw_w[0], np.float32)
    put("cw", cw.reshape(31, 16, 128).transpose(2, 0, 1).reshape(128, 31 * 16))
    put("cb", _pm(conv_dw_b[0], 16))
    put("lg", _pm(conv_ln_g[0], 16))
    put("lb", _pm(conv_ln_b[0], 16))
    fw = np.asarray(ffn_dw_w[0], np.float32)
    put("fw", fw.reshape(3, 172, 128).transpose(2, 0, 1).reshape(128, 3 * 172))
    put("fb", _pm(ffn_dw_b[0], 172))
    put("ccT", _pm(c_ctx, 32))
    lamv = np.concatenate([np.asarray(v[0], np.float32) for v in (lambda_q1, lambda_k1, lambda_q2, lambda_k2)])
    put("lam", np.broadcast_to(lamv[None, :], (128, 256)))

    wa2, wi2, wo2, wu2, wd2 = (np.asarray(w[0], dtype=np.float32) for w in (w_ada, w_in, w_out, w_up, w_down))
    in_maps = []
    for core in cores:
        b, qi = core // 4, core % 4
        st = _core_layout(qi)
        xs = np.zeros((NS, D), np.float32)
        mx = st >= 0
        xs[mx] = x[b][st[mx]]
        mc = st <= -2
        xs[mc] = ctx[b][-2 - st[mc]]
        validslot = (st != -1)
        keybias = np.where(validslot, 0.0, -30000.0).astype(np.float32).reshape(NT, 128).T
        valid = np.broadcast_to(validslot[:NEXT].astype(np.float32)[None, :], (128, NEXT))
        cs = np.ones((128, NS), np.float32)
        sn = np.zeros((128, NS), np.float32)
        cs[:, mx] = cosT[st[mx]][:, pair].T
        sn[:, mx] = sinT[st[mx]][:, pair].T * sgn[:, None]
        vecs = base.copy()
        vecs[:, _VOFF["cT"]:_VOFF["cT"] + 32] = _pm(c[b], 32)
        in_maps.append({
            "xs": xs, "vecs": vecs, "keybias": np.ascontiguousarray(keybias), "valid": np.ascontiguousarray(valid),
            "cosT": cs, "sinT": sn, "constb": constb, "constf": constf,
            "w_ada": wa2, "w_in": wi2, "w_out": wo2, "w_up": wu2, "w_down": wd2,
        })
    return in_maps


def kernel(**inputs):
    in_maps = make_in_maps(**inputs)
    nc = build_program()
    res = run_bass_kernel_spmd(nc, in_maps, core_ids=list(range(8)))
    out = np.empty((2, SEQ, D), np.float32)
    for core in range(8):
        b, qi = core // 4, core % 4
        out[b, qi * 2048:(qi + 1) * 2048] = res.results[core]["out"]
    return out
```

```python
from contextlib import ExitStack
import numpy as np
import ml_dtypes
import concourse.bass as bass
import concourse.mybir as mybir
from concourse.bass_utils import run_bass_kernel_spmd

F32 = mybir.dt.float32
BF16 = mybir.dt.bfloat16
ALU = mybir.AluOpType
AF = mybir.ActivationFunctionType

D = 4096
NCH = 32
SEQ = 8192
NCTX = 256
NT = 67
NS = NT * 128
NEXT = 2176
QO = 63
NQ = 2050
UO = 48
NU = 2080
DFF = 11008
NJ = 86
EPS6 = 1e-6
EPS5 = 1e-5
LAM_INIT = 0.2
ENGS = ("pe", "act", "dve", "pool", "sp")

_VOFF = {}
_o = 0
for _n, _w in (("g1", 32), ("g2", 32), ("bada", 192), ("gq", 1), ("gk", 1), ("gs", 1),
               ("cw", 31 * 16), ("cb", 16), ("lg", 16), ("lb", 16),
               ("fw", 3 * 172), ("fb", 172), ("cT", 32), ("ccT", 32), ("lam", 256)):
    _VOFF[_n] = _o
    _o += _w
NV = _o


class Buf:
    def __init__(self, ap, name):
        self.ap = ap
        self.name = name
        self.w_ev = {}
        self.r_ev = {}
        self.dsem = None
        self.dcount = 0

    def __getitem__(self, idx):
        return self.ap[idx]

    def full(self):
        return self.ap[(slice(None),) * len(self.ap.shape)]


class Op:
    __slots__ = ("eng", "fn", "waits", "signal", "semval", "dbuf", "seq")

    def __init__(self, eng, fn, seq):
        self.eng = eng
        self.fn = fn
        self.waits = {}
        self.signal = False
        self.semval = None
        self.dbuf = None
        self.seq = seq


class Prog:
    def __init__(self, nc):
        self.nc = nc
        self.ctx = ExitStack()
        self.pstack = None
        self.ops = {e: [] for e in ENGS}
        self.seq = {e: 0 for e in ENGS}
        self.ecount = {e: 0 for e in ENGS}
        self.waited = {e: {} for e in ENGS}
        self.esem = {e: self.ctx.enter_context(nc.semaphore("e_" + e)) for e in ENGS}
        self.nbuf = 0
        self.touched = {}
        self.bar_src = None
        self.bar_dst = None
        self.tok = None
        self.dbg = ()
        self.uid = 0

    def sbuf(self, name, shape, dtype, glob=False):
        st = self.ctx if (glob or self.pstack is None) else self.pstack
        self.uid += 1
        return Buf(st.enter_context(self.nc.sbuf_tensor("%s_%d" % (name, self.uid), list(shape), dtype)), name)

    def psum(self, name, shape, dtype=F32):
        st = self.ctx if self.pstack is None else self.pstack
        self.uid += 1
        return Buf(st.enter_context(self.nc.psum_tensor("%s_%d" % (name, self.uid), list(shape), dtype)), name)

    def dram(self, name, shape, dtype, kind="Internal"):
        if self.dbg and name in self.dbg:
            kind = "ExternalOutput"
        return Buf(self.nc.dram_tensor(name, list(shape), dtype, kind=kind), name)

    def op(self, eng, fn, reads=(), writes=(), dma=None, join=False):
        o = Op(eng, fn, self.seq[eng])
        self.seq[eng] += 1
        waits = o.waits

        def need(evd):
            for key, val in evd.items():
                if key[0] == "E":
                    if key[1] == eng and eng == "pe":
                        continue
                    old = waits.get(key)
                    if old is None or old.seq < val.seq:
                        waits[key] = val
                else:
                    waits[key] = key[1].dcount

        for r in reads:
            need(r.w_ev)
            self.touched[id(r)] = r
        for w in writes:
            if not join:
                need(w.w_ev)
            need(w.r_ev)
            self.touched[id(w)] = w
        for key, val in waits.items():
            if key[0] == "E":
                val.signal = True
        if dma is not None:
            if dma.dsem is None:
                dma.dsem = self.ctx.enter_context(self.nc.semaphore("d%d" % self.nbuf))
                self.nbuf += 1
            dma.dcount += 16
            key, val = ("D", dma), dma.dcount
            o.dbuf = dma
        else:
            key, val = ("E", eng), o
        for w in writes:
            if join:
                w.w_ev[key] = val
            else:
                w.w_ev = {key: val}
                w.r_ev = {}
        for r in reads:
            r.r_ev[key] = val
        self.ops[eng].append(o)
        return o

    def I(self, eng, meth, reads, writes, *a, dma=None, join=False, **kw):
        return self.op(eng, lambda e: getattr(e, meth)(*a, **kw), reads, writes, dma=dma, join=join)

    def wait_only(self, eng, bufs):
        o = Op(eng, None, self.seq[eng])
        self.seq[eng] += 1
        for b in bufs:
            for key, val in list(b.w_ev.items()) + list(b.r_ev.items()):
                if key[0] == "E":
                    if key[1] == eng:
                        continue
                    val.signal = True
                    old = o.waits.get(key)
                    if old is None or old.seq < val.seq:
                        o.waits[key] = val
                else:
                    o.waits[key] = key[1].dcount
        self.ops[eng].append(o)

    def barrier(self):
        bufs = list(self.touched.values())
        self.touched = {}
        src, dst, tok = self.bar_src, self.bar_dst, self.tok
        o = Op("sp", lambda e: e.dma_start(out=dst[0:1, 0:16], in_=src[0:1, 0:16]), self.seq["sp"])
        self.seq["sp"] += 1
        for b in bufs + [tok]:
            for key, val in list(b.w_ev.items()) + list(b.r_ev.items()):
                if key[0] == "E":
                    val.signal = True
                    old = o.waits.get(key)
                    if old is None or old.seq < val.seq:
                        o.waits[key] = val
                else:
                    o.waits[key] = key[1].dcount
        if tok.dsem is None:
            tok.dsem = self.ctx.enter_context(self.nc.semaphore("d%d" % self.nbuf))
            self.nbuf += 1
        tok.dcount += 16
        o.dbuf = tok
        self.ops["sp"].append(o)
        for b in bufs:
            b.w_ev = {}
            b.r_ev = {}
        tok.w_ev = {("D", tok): tok.dcount}
        tok.r_ev = {}
        for e in ENGS:
            if e != "sp":
                self.wait_only(e, [tok])

    def emit(self):
        nc = self.nc
        for e in ENGS:
            for o in self.ops[e]:
                if o.signal:
                    self.ecount[e] += 1
                    o.semval = self.ecount[e]
        prog = self

        def run(engname, engine):
            waited = prog.waited[engname]
            for o in prog.ops[engname]:
                for key, val in o.waits.items():
                    if key[0] == "E":
                        v = val.semval
                        sem = prog.esem[key[1]]
                    else:
                        v = val
                        sem = key[1].dsem
                    if waited.get(key, 0) >= v:
                        continue
                    waited[key] = v
                    engine.wait_ge(sem, v)
                if o.fn is None:
                    continue
                ins = o.fn(engine)
                if o.dbuf is not None:
                    ins.then_inc(o.dbuf.dsem, 16)
                elif o.signal:
                    ins.then_inc(prog.esem[engname], 1)

        with nc.Block() as block:
            @block.sync
            def _(e):
                run("sp", e)

            @block.tensor
            def _(e):
                run("pe", e)

            @block.scalar
            def _(e):
                run("act", e)

            @block.vector
            def _(e):
                run("dve", e)

            @block.gpsimd
            def _(e):
                run("pool", e)
        self.ops = {e: [] for e in ENGS}

    def phase_begin(self):
        self.pstack = ExitStack()

    def phase_end(self, final=False):
        if not final:
            self.barrier()
        self.emit()
        self.pstack.close()
        self.pstack = None


def build_program(stop_after=99, dbg=False):
    nc = bass.Bass("TRN2", target_bir_lowering=False)
    P = Prog(nc)
    P.dbg = dbg or ()
    print('sbuf bytes remaining', nc.sbuf_bytes_remaining)

    def din(name, shape, dt=F32):
        return Buf(nc.dram_tensor(name, list(shape), dt, kind="ExternalInput"), name)

    XS = din("xs", [NS, D])
    VEC = din("vecs", [128, NV])
    KB = din("keybias", [128, NT])
    VAL = din("valid", [128, NEXT])
    COS = din("cosT", [128, NS])
    SIN = din("sinT", [128, NS])
    CB = din("constb", [128, 4 * 128], BF16)
    CF = din("constf", [128, 3 * 128])
    WADA = din("w_ada", [D, 6 * D])
    WIN = din("w_in", [D, 10240])
    WOUT = din("w_out", [D, D])
    WUP = din("w_up", [D, 2 * DFF])
    WDN = din("w_down", [DFF, D])
    OUT = Buf(nc.dram_tensor("out", [2048, D], F32, kind="ExternalOutput"), "out")

    H1T = P.dram("h1t", [D, NS], BF16)
    XT = P.dram("xtT", [D, NEXT], F32)
    KT = P.dram("kt", [16, 128, NS], BF16)
    VS = P.dram("vs", [16, 128, NT, 128], BF16)
    QT = P.dram("qt", [16, 128, NQ], BF16)
    YS = P.dram("ys", [16, 128, NQ], F32)
    CAT = P.dram("cat", [32, 128, NQ], BF16)
    X1T = P.dram("x1t", [32, 128, NQ], F32)
    WUPB = P.dram("wupb", [43, 128, 32, 512], BF16)
    WDNB = P.dram("wdnb", [32, 128, NJ, 128], BF16)
    BARD = P.dram("bard", [1, 16], F32)
    P.bar_dst = BARD
    H1Tv = H1T.ap.rearrange("(c p) s -> p c s", p=128)
    XTv = XT.ap.rearrange("(c p) s -> p c s", p=128)
    WINv = WIN.ap.rearrange("(k p) n -> p k n", p=128)
    WADAv = WADA.ap.rearrange("(k p) n -> p k n", p=128)
    WOUTv = WOUT.ap.rearrange("(k p) n -> p k n", p=128)
    WUPv = WUP.ap.rearrange("(k p) n -> p k n", p=128)
    WDNv = WDN.ap.rearrange("(k p) n -> p k n", p=128)

    vec = P.sbuf("vec", [128, NV], F32, glob=True)
    cb = P.sbuf("cb", [128, 4 * 128], BF16, glob=True)
    cf = P.sbuf("cf", [128, 3 * 128], F32, glob=True)
    mod = P.sbuf("mod", [128, 192, 2], F32, glob=True)
    der = P.sbuf("der", [128, 8, 32], F32, glob=True)
    sml = P.sbuf("sml", [128, 16], F32, glob=True)
    kb = P.sbuf("kb", [128, NT], F32, glob=True)
    tok = P.sbuf("tok", [1, 16], F32, glob=True)
    rstd2 = P.sbuf("rstd2", [128, NQ], F32, glob=True)
    P.tok = tok
    P.bar_src = tok
    identb = cb.ap[:, 0:128]
    pswap = cb.ap[:, 128:256]
    ones64 = cb.ap[:, 256:384]
    onesb = cb.ap[:, 384:512]
    identf = cf.ap[:, 0:128]
    ones128f = cf.ap[:, 128:256]
    onesDf = cf.ap[:, 256:384]

    def V(name, i=0, w=1):
        o = _VOFF[name] + i
        return vec.ap[:, o:o + w]

    A1x, A1c, A2 = der.ap[:, 0, :], der.ap[:, 1, :], der.ap[:, 2, :]
    B1x, B1c = mod.ap[:, 0:32, 0], mod.ap[:, 0:32, 1]
    gate1, B2, gate2 = mod.ap[:, 64:96, 0], mod.ap[:, 96:128, 0], mod.ap[:, 160:192, 0]

    P.phase_begin()
    P.I("dve", "memset", [], [tok], tok[:, :], 0.0)
    for dst, src in ((vec, VEC), (cb, CB), (cf, CF), (kb, KB)):
        P.I("sp", "dma_start", [src], [dst], out=dst.full(), in_=src.full(), dma=dst)
    sc = P.sbuf("sc", [128, 32, 2], BF16)
    P.I("act", "activation", [vec], [sc], out=sc[:, :, 0], in_=V("cT", 0, 32), func=AF.Silu)
    P.I("act", "activation", [vec], [sc], out=sc[:, :, 1], in_=V("ccT", 0, 32), func=AF.Silu, join=True)
    wa = [P.sbuf("wa%d" % i, [128, 32, 1024], BF16) for i in range(2)]
    psmod = P.psum("psmod", [128, 256, 2])
    for nb in range(24):
        w = wa[nb % 2]
        P.I("pool", "dma_start", [WADA], [w], out=w[:, :, :], in_=WADAv[:, :, nb * 1024:(nb + 1) * 1024], dma=w)
        for m in range(8):
            for k in range(32):
                P.I("pe", "matmul", [w, sc], [psmod], psmod[:, nb * 8 + m, :], lhsT=w[:, k, m * 128:(m + 1) * 128],
                    rhs=sc[:, k, :], start=(k == 0), stop=(k == 31))
    for j in range(2):
        P.I("dve", "tensor_tensor", [psmod, vec], [mod], out=mod[:, :, j], in0=psmod[:, 0:192, j],
            in1=V("bada", 0, 192), op=ALU.add, join=True)
    for di, (sl, j, g) in enumerate(((slice(32, 64), 0, "g1"), (slice(32, 64), 1, "g1"), (slice(128, 160), 0, "g2"))):
        P.I("dve", "tensor_scalar", [mod], [der], out=der[:, 3, :], in0=mod[:, sl, j], scalar1=1.0, scalar2=None, op0=ALU.add)
        P.I("dve", "tensor_tensor", [der, vec], [der], out=der[:, di, :], in0=der[:, 3, :], in1=V(g, 0, 32), op=ALU.mult)
    P.I("dve", "tensor_scalar", [vec], [sml], out=sml[:, 0:1], in0=V("gq"), scalar1=0.125, scalar2=None, op0=ALU.mult)
    P.I("dve", "tensor_scalar", [vec], [sml], out=sml[:, 1:2], in0=V("gs"), scalar1=1.0 - LAM_INIT, scalar2=None, op0=ALU.mult)
    lamt = P.sbuf("lamt", [128, 128], F32)
    P.I("dve", "tensor_tensor", [vec], [lamt], out=lamt[:, 0:64], in0=V("lam", 0, 64), in1=V("lam", 64, 64), op=ALU.mult)
    P.I("dve", "tensor_tensor", [vec], [lamt], out=lamt[:, 64:128], in0=V("lam", 128, 64), in1=V("lam", 192, 64), op=ALU.mult)
    P.I("dve", "tensor_reduce", [lamt], [sml], out=sml[:, 3:4], in_=lamt[:, 0:64], axis=mybir.AxisListType.X, op=ALU.add)
    P.I("dve", "tensor_reduce", [lamt], [sml], out=sml[:, 4:5], in_=lamt[:, 64:128], axis=mybir.AxisListType.X, op=ALU.add)
    P.I("act", "activation", [sml], [sml], out=sml[:, 5:7], in_=sml[:, 3:5], func=AF.Exp)
    P.I("dve", "tensor_tensor", [sml], [sml], out=sml[:, 7:8], in0=sml[:, 6:7], in1=sml[:, 5:6], op=ALU.subtract)
    P.I("dve", "tensor_scalar", [sml], [sml], out=sml[:, 2:3], in0=sml[:, 7:8], scalar1=-LAM_INIT, scalar2=None, op0=ALU.add)
    P.phase_end()
    if stop_after <= 0:
        return _finish(P, nc, OUT, dbg, {"mod": mod, "sml": sml})

    P.phase_begin()
    xt = [P.sbuf("xt%d" % i, [128, D], F32) for i in range(2)]
    junk = P.sbuf("junk", [128, D], BF16)
    xsb = [P.sbuf("xsb%d" % i, [128, D], BF16) for i in range(2)]
    ss = [P.sbuf("ss%d" % i, [128, 4], F32) for i in range(2)]
    hst = [P.sbuf("hst%d" % i, [128, 32, 128], BF16) for i in range(2)]
    xst = [P.sbuf("xst%d" % i, [128, 32, 128], F32) for i in range(2)]
    ptb = [P.psum("ptb%d" % i, [128, 8, 128], BF16) for i in range(2)]
    ptf = [P.psum("ptf%d" % i, [128, 4, 128], F32) for i in range(2)]
    epsb = P.sbuf("epsb", [128, 1], F32)
    P.I("dve", "memset", [], [epsb], epsb[:, :], EPS6)
    P.I("sp", "dma_start", [XS], [xt[0]], out=xt[0][:, :], in_=XS[0:128, :], dma=xt[0])
    for t in range(NT):
        i = t % 2
        x = xt[i]
        if t + 1 < NT:
            xn = xt[(t + 1) % 2]
            P.I("sp", "dma_start", [XS], [xn], out=xn[:, :], in_=XS[(t + 1) * 128:(t + 2) * 128, :], dma=xn)
        P.I("dve", "memset", [], [ss[i]], ss[i][:, :], 0.0)
        P.I("act", "activation", [x, ss[i]], [junk, ss[i]], out=junk[:, :], in_=x[:, :], func=AF.Square, accum_out=ss[i][:, 0:1])
        P.I("act", "activation", [ss[i], epsb], [ss[i]], out=ss[i][:, 1:2], in_=ss[i][:, 0:1], func=AF.Sqrt, scale=1.0 / D, bias=epsb[:, 0:1])
        P.I("dve", "reciprocal", [ss[i]], [ss[i]], out=ss[i][:, 2:3], in_=ss[i][:, 1:2])
        P.I("dve", "tensor_scalar", [x, ss[i]], [xsb[i]], out=xsb[i][:, :], in0=x[:, :], scalar1=ss[i][:, 2:3], scalar2=None, op0=ALU.mult)
        isctx = 17 <= t < 19
        Aa, Bb = (A1c, B1c) if isctx else (A1x, B1x)
        for g in range(4):
            pb = ptb[g % 2]
            for c8 in range(8):
                c = g * 8 + c8
                P.I("pe", "transpose", [xsb[i], cb], [pb], out=pb[:, c8, :], in_=xsb[i][:, c * 128:(c + 1) * 128], identity=identb)
            for c8 in range(8):
                c = g * 8 + c8
                P.I("dve", "tensor_scalar", [pb, mod, der], [hst[i]], out=hst[i][:, c, :], in0=pb[:, c8, :], scalar1=Aa[:, c:c + 1],
                    scalar2=Bb[:, c:c + 1], op0=ALU.mult, op1=ALU.add, join=True)
        for q4 in range(4):
            P.I("sp", "dma_start", [hst[i]], [H1T], out=H1Tv[:, q4 * 8:(q4 + 1) * 8, t * 128:(t + 1) * 128], in_=hst[i][:, q4 * 8:(q4 + 1) * 8, :],
                dma=hst[i], join=True)
        if t < 17:
            for g in range(8):
                pf = ptf[g % 2]
                for c4 in range(4):
                    c = g * 4 + c4
                    P.I("pe", "transpose", [x, cf], [pf], out=pf[:, c4, :], in_=x[:, c * 128:(c + 1) * 128], identity=identf)
                eng = "act" if g % 2 == 0 else "dve"
                if eng == "act":
                    P.I("act", "copy", [pf], [xst[i]], out=xst[i][:, g * 4:(g + 1) * 4, :], in_=pf[:, :, :], join=True)
                else:
                    P.I("dve", "tensor_copy", [pf], [xst[i]], out=xst[i][:, g * 4:(g + 1) * 4, :], in_=pf[:, :, :], join=True)
            for q4 in range(4):
                P.I("sp", "dma_start", [xst[i]], [XT], out=XTv[:, q4 * 8:(q4 + 1) * 8, t * 128:(t + 1) * 128], in_=xst[i][:, q4 * 8:(q4 + 1) * 8, :],
                    dma=xst[i], join=True)
    P.phase_end()
    if stop_after <= 1:
        return _finish(P, nc, OUT, dbg, {})

    def qk_post_a(T, ps, n, gcol):
        sq, sd, kg, t1, t2, psN, psR = T[:7]
        P.I("act", "activation", [ps], [sq], out=sq[:, 0:n], in_=ps[:, 0:n], func=AF.Square)
        P.I("pe", "matmul", [sq, cb], [psN], psN[:, 0:n], lhsT=ones64, rhs=sq[:, 0:n], start=True, stop=True)
        P.I("act", "activation", [psN, T[7]], [sd], out=sd[:, 0:n], in_=psN[:, 0:n], func=AF.Sqrt, bias=T[7][:, 0:1], scale=1.0)
        P.I("dve", "reciprocal", [sd], [sd], out=sd[:, 0:n], in_=sd[:, 0:n])
        P.I("dve", "scalar_tensor_tensor", [ps, sd, vec, sml], [kg], out=kg[:, 0:n], in0=ps[:, 0:n], scalar=gcol, in1=sd[:, 0:n],
            op0=ALU.mult, op1=ALU.mult)

    def qk_post_b(T, n, cs, dst_ap, dst_buf):
        sq, sd, kg, t1, t2, psN, psR = T[:7]
        P.I("pe", "matmul", [kg, cb], [psR], psR[:, 0:n], lhsT=pswap, rhs=kg[:, 0:n], start=True, stop=True)
        P.I("pool", "tensor_tensor", [kg, cs], [t1], out=t1[:, 0:n], in0=kg[:, 0:n], in1=cs[:, 0, 0:n], op=ALU.mult)
        P.I("dve", "tensor_tensor", [psR, cs], [t2], out=t2[:, 0:n], in0=psR[:, 0:n], in1=cs[:, 1, 0:n], op=ALU.mult)
        P.I("pool", "tensor_tensor", [t1, t2], [dst_buf], out=dst_ap, in0=t1[:, 0:n], in1=t2[:, 0:n], op=ALU.add)

    P.phase_begin()
    hb = [P.sbuf("hb%d" % i, [128, 32, 512], BF16) for i in range(2)]
    wk = [P.sbuf("wk0", [128, 32, 512], BF16)]
    wv = [P.sbuf("wv0", [128, 32, 512], BF16)]
    csb = [P.sbuf("csb%d" % i, [128, 2, 512], F32) for i in range(2)]
    psK = [P.psum("psK%d" % i, [128, 512]) for i in range(4)]
    psV = [P.psum("psV%d" % i, [128, 512]) for i in range(2)]
    epsq = P.sbuf("epsq", [128, 1], F32)
    P.I("dve", "memset", [], [epsq], epsq[:, :], EPS6)
    psN_, psR_ = P.psum("psN", [128, 512]), P.psum("psR", [128, 512])
    t1_, t2_ = P.sbuf("t1", [128, 512], F32), P.sbuf("t2", [128, 512], F32)
    TT = [(P.sbuf("sq%d" % i, [128, 512], BF16), P.sbuf("sd%d" % i, [128, 512], F32), P.sbuf("kg%d" % i, [128, 512], BF16),
           t1_, t2_, psN_, psR_, epsq) for i in range(4)]
    kst = [P.sbuf("kst%d" % i, [128, 512], BF16) for i in range(4)]
    vst = [P.sbuf("vst%d" % i, [128, 4, 512], BF16) for i in range(2)]
    cnt = 0
    for g in range(4):
        P.I("pool", "dma_start", [WIN], [wk[0]], out=wk[0][:, :, :], in_=WINv[:, :, 2048 + g * 512:2048 + (g + 1) * 512], dma=wk[0])
        P.I("pool", "dma_start", [WIN], [wv[0]], out=wv[0][:, :, :], in_=WINv[:, :, 4096 + g * 512:4096 + (g + 1) * 512], dma=wv[0])
        for bl in range(17):
            s0 = bl * 512
            n = min(512, NS - s0)
            h = hb[cnt % 2]
            cs = csb[cnt % 2]
            cnt += 1
            P.I("sp", "dma_start", [H1T], [h], out=h[:, :, 0:n], in_=H1Tv[:, :, s0:s0 + n], dma=h)
            P.I("sp", "dma_start", [COS], [cs], out=cs[:, 0, 0:n], in_=COS[:, s0:s0 + n], dma=cs)
            P.I("sp", "dma_start", [SIN], [cs], out=cs[:, 1, 0:n], in_=SIN[:, s0:s0 + n], dma=cs, join=True)
            for hh in range(4):
                pk = psK[hh]
                for k in range(32):
                    P.I("pe", "matmul", [wk[0], h], [pk], pk[:, 0:n], lhsT=wk[0][:, k, hh * 128:(hh + 1) * 128], rhs=h[:, k, 0:n],
                        start=(k == 0), stop=(k == 31))
            vs_ = vst[bl % 2]
            ntile = n // 128

            def vtile(tt):
                if tt >= ntile:
                    return
                pv = psV[tt % 2]
                for k in range(32):
                    P.I("pe", "matmul", [wv[0], h], [pv], pv[:, :], lhsT=h[:, k, tt * 128:(tt + 1) * 128], rhs=wv[0][:, k, :],
                        start=(k == 0), stop=(k == 31))
                if tt % 2 == 0:
                    P.I("act", "copy", [pv], [vs_], out=vs_[:, tt, :], in_=pv[:, :], join=True)
                else:
                    P.I("dve", "tensor_copy", [pv], [vs_], out=vs_[:, tt, :], in_=pv[:, :], join=True)

            for hh in range(4):
                qk_post_a(TT[hh], psK[hh], n, V("gk"))
                vtile(hh)
            for hh in range(4):
                ks = kst[hh]
                qk_post_b(TT[hh], n, cs, ks[:, 0:n], ks)
                P.I("pool", "dma_start", [ks], [KT], out=KT[g * 4 + hh, :, s0:s0 + n], in_=ks[:, 0:n], dma=ks, join=True)
            for hh in range(4):
                P.I("pool", "dma_start", [vs_], [VS], out=VS[g * 4 + hh, :, bl * 4:bl * 4 + ntile, :],
                    in_=vs_[:, 0:ntile, hh * 128:(hh + 1) * 128], dma=vs_, join=True)
    for g in range(4):
        P.I("pool", "dma_start", [WIN], [wk[0]], out=wk[0][:, :, :], in_=WINv[:, :, g * 512:(g + 1) * 512], dma=wk[0])
        for bl in range(5):
            s0 = QO + bl * 410
            n = 410
            h = hb[cnt % 2]
            cs = csb[cnt % 2]
            cnt += 1
            P.I("sp", "dma_start", [H1T], [h], out=h[:, :, 0:n], in_=H1Tv[:, :, s0:s0 + n], dma=h)
            P.I("sp", "dma_start", [COS], [cs], out=cs[:, 0, 0:n], in_=COS[:, s0:s0 + n], dma=cs)
            P.I("sp", "dma_start", [SIN], [cs], out=cs[:, 1, 0:n], in_=SIN[:, s0:s0 + n], dma=cs, join=True)
            for hh in range(4):
                pk = psK[hh]
                for k in range(32):
                    P.I("pe", "matmul", [wk[0], h], [pk], pk[:, 0:n], lhsT=wk[0][:, k, hh * 128:(hh + 1) * 128], rhs=h[:, k, 0:n],
                        start=(k == 0), stop=(k == 31))
            for hh in range(4):
                qk_post_a(TT[hh], psK[hh], n, sml[:, 0:1])
            for hh in range(4):
                ks = kst[hh]
                qk_post_b(TT[hh], n, cs, ks[:, 0:n], ks)
                P.I("pool", "dma_start", [ks], [QT], out=QT[g * 4 + hh, :, bl * 410:bl * 410 + n], in_=ks[:, 0:n], dma=ks, join=True)
    P.phase_end()
    if stop_after <= 2:
        return _finish(P, nc, OUT, dbg, {})

    P.phase_begin()
    val = P.sbuf("val", [128, NEXT], F32)
    P.I("sp", "dma_start", [VAL], [val], out=val.full(), in_=VAL.full(), dma=val)
    P.I("dve", "tensor_copy", [val], [sml], out=sml[:, 8:9], in_=val[:, 63:64])
    P.I("dve", "tensor_copy", [val], [sml], out=sml[:, 9:10], in_=val[:, 2112:2113])
    hb = [P.sbuf("hb%d" % i, [128, 32, 416], BF16) for i in range(2)]
    wc = [P.sbuf("wc%d" % i, [128, 32, 256], BF16) for i in range(2)]
    psA = [P.psum("psA%d" % i, [128, 512]) for i in range(2)]
    psG = [P.psum("psG%d" % i, [128, 512]) for i in range(2)]
    sg = [P.sbuf("sg%d" % i, [128, 416], F32) for i in range(2)]
    ub = [P.sbuf("ub%d" % i, [128, NU], BF16) for i in range(2)]
    dg = [P.sbuf("dg%d" % i, [128, 31, 128], BF16) for i in range(2)]
    psC = [P.psum("psC%d" % i, [128, 512]) for i in range(2)]
    acc = [P.sbuf("acc%d" % i, [128, NQ], F32) for i in range(2)]
    ysq = P.sbuf("ysq", [128, NQ], F32)
    s1 = P.sbuf("s1", [128, NQ], F32)
    s2 = P.sbuf("s2", [128, NQ], F32)
    P.I("dve", "memset", [], [s1], s1[:, :], 0.0)
    P.I("dve", "memset", [], [s2], s2[:, :], 0.0)
    cnt = 0
    for c in range(16):
        w = wc[c % 2]
        P.I("pool", "dma_start", [WIN], [w], out=w[:, :, 0:128], in_=WINv[:, :, 6144 + c * 128:6144 + (c + 1) * 128], dma=w)
        P.I("pool", "dma_start", [WIN], [w], out=w[:, :, 128:256], in_=WINv[:, :, 8192 + c * 128:8192 + (c + 1) * 128], dma=w, join=True)
        u = ub[c % 2]
        for bl in range(5):
            s0 = UO + bl * 416
            n = 416
            h = hb[cnt % 2]
            pa, pg, sgi = psA[cnt % 2], psG[cnt % 2], sg[cnt % 2]
            cnt += 1
            P.I("sp", "dma_start", [H1T], [h], out=h[:, :, 0:n], in_=H1Tv[:, :, s0:s0 + n], dma=h)
            for k in range(32):
                P.I("pe", "matmul", [w, h], [pa], pa[:, 0:n], lhsT=w[:, k, 0:128], rhs=h[:, k, 0:n], start=(k == 0), stop=(k == 31))
            for k in range(32):
                P.I("pe", "matmul", [w, h], [pg], pg[:, 0:n], lhsT=w[:, k, 128:256], rhs=h[:, k, 0:n], start=(k == 0), stop=(k == 31))
            P.I("act", "activation", [pg], [sgi], out=sgi[:, 0:n], in_=pg[:, 0:n], func=AF.Sigmoid)
            P.I("dve", "tensor_tensor", [pa, sgi], [u], out=u[:, bl * 416:bl * 416 + n], in0=pa[:, 0:n], in1=sgi[:, 0:n], op=ALU.mult, join=True)
        P.I("dve", "tensor_tensor", [u, val], [u], out=u[:, 0:16], in0=u[:, 0:16], in1=val[:, 48:64], op=ALU.mult)
        P.I("dve", "tensor_tensor", [u, val], [u], out=u[:, 2064:2080], in0=u[:, 2064:2080], in1=val[:, 2112:2128], op=ALU.mult)
        a = acc[c % 2]
        dgc = dg[c % 2]
        for k in range(31):
            P.I("dve", "tensor_scalar", [cb, vec], [dgc], out=dgc[:, k, :], in0=identb, scalar1=V("cw", k * 16 + c), scalar2=None,
                op0=ALU.mult, join=True)
        for bl in range(5):
            pc = psC[bl % 2]
            for k in range(31):
                P.I("pe", "matmul", [dgc, u], [pc], pc[:, 0:410], lhsT=dgc[:, k, :], rhs=u[:, bl * 410 + k:bl * 410 + k + 410],
                    start=(k == 0), stop=(k == 30))
            P.I("act", "activation", [pc, vec], [a], out=a[:, bl * 410:(bl + 1) * 410], in_=pc[:, 0:410], func=AF.Identity,
                bias=V("cb", c), scale=1.0, join=True)
        P.I("pool", "tensor_tensor", [s1, a], [s1], out=s1[:, :], in0=s1[:, :], in1=a[:, :], op=ALU.add)
        P.I("act", "activation", [a], [ysq], out=ysq[:, :], in_=a[:, :], func=AF.Square)
        P.I("pool", "tensor_tensor", [s2, ysq], [s2], out=s2[:, :], in0=s2[:, :], in1=ysq[:, :], op=ALU.add)
        P.I("pool", "dma_start", [a], [YS], out=YS[c, :, :], in_=a[:, :], dma=a, join=True)
    mean = P.sbuf("mean", [128, NQ], F32)
    rstd = P.sbuf("rstdc", [128, NQ], F32)
    eps5 = P.sbuf("eps5", [128, 1], F32)
    P.I("dve", "memset", [], [eps5], eps5[:, :], EPS5)
    for bl in range(5):
        sl = slice(bl * 410, bl * 410 + 410)
        pa, pg = psA[bl % 2], psG[bl % 2]
        P.I("pe", "matmul", [s1, cf], [pa], pa[:, 0:410], lhsT=onesDf, rhs=s1[:, sl], start=True, stop=True)
        P.I("pe", "matmul", [s2, cf], [pg], pg[:, 0:410], lhsT=onesDf, rhs=s2[:, sl], start=True, stop=True)
        P.I("act", "activation", [pa], [mean], out=mean[:, sl], in_=pa[:, 0:410], func=AF.Identity, scale=1.0 / 2048, join=True)
        P.I("act", "activation", [pg], [rstd], out=rstd[:, sl], in_=pg[:, 0:410], func=AF.Identity, scale=1.0 / 2048, join=True)
    P.I("dve", "tensor_tensor", [mean], [ysq], out=ysq[:, :], in0=mean[:, :], in1=mean[:, :], op=ALU.mult)
    P.I("dve", "tensor_tensor", [rstd, ysq], [rstd], out=rstd[:, :], in0=rstd[:, :], in1=ysq[:, :], op=ALU.subtract)
    P.I("act", "activation", [rstd, eps5], [rstd], out=rstd[:, :], in_=rstd[:, :], func=AF.Sqrt, bias=eps5[:, 0:1], scale=1.0)
    P.I("dve", "reciprocal", [rstd], [rstd], out=rstd[:, :], in_=rstd[:, :])
    P.I("dve", "tensor_tensor", [mean, rstd], [mean], out=mean[:, :], in0=mean[:, :], in1=rstd[:, :], op=ALU.mult)
    cst = [P.sbuf("cst%d" % i, [128, NQ], BF16) for i in range(2)]
    for c in range(16):
        a = acc[c % 2]
        P.I("sp", "dma_start", [YS], [a], out=a[:, :], in_=YS[c, :, :], dma=a)
        P.I("dve", "tensor_tensor", [a, rstd], [a], out=a[:, :], in0=a[:, :], in1=rstd[:, :], op=ALU.mult)
        P.I("pool", "tensor_tensor", [a, mean], [a], out=a[:, :], in0=a[:, :], in1=mean[:, :], op=ALU.subtract)
        o = cst[c % 2]
        P.I("dve", "tensor_scalar", [a, vec], [a], out=a[:, :], in0=a[:, :], scalar1=V("lg", c), scalar2=V("lb", c), op0=ALU.mult, op1=ALU.add)
        P.I("act", "activation", [a], [o], out=o[:, :], in_=a[:, :], func=AF.Silu)
        P.I("pool", "dma_start", [o], [CAT], out=CAT[16 + c, :, :], in_=o[:, :], dma=o, join=True)
    P.phase_end()
    if stop_after <= 3:
        return _finish(P, nc, OUT, dbg, {})

    P.phase_begin()
    kt = [P.sbuf("kt%d" % i, [128, NS], BF16) for i in range(2)]
    vt = [P.sbuf("vt%d" % i, [128, NT, 128], BF16) for i in range(2)]
    qt = [P.sbuf("qt%d" % i, [128, NQ], BF16) for i in range(2)]
    psS = [P.psum("psS%d" % i, [128, 2, 512]) for i in range(2)]
    psO = [P.psum("psO%d" % i, [128, 512]) for i in range(2)]
    psL = [P.psum("psL%d" % i, [128, 512]) for i in range(2)]
    pT = [P.sbuf("pT%d" % i, [128, 2, 410], BF16) for i in range(3)]
    rr = [P.sbuf("rr%d" % i, [128, 410], F32) for i in range(2)]
    o_ = P.sbuf("o_", [128, 410], F32)
    r32 = P.sbuf("r32", [128, 410], F32)
    osq = P.sbuf("osq", [128, 410], F32)
    ast = [P.sbuf("ast%d" % i, [128, 410], BF16) for i in range(2)]
    eps6 = P.sbuf("eps6a", [128, 1], F32)
    P.I("dve", "memset", [], [eps6], eps6[:, :], EPS6)
    stf = [P.sbuf("stf%d" % i, [128, 2048], F32) for i in range(3)]
    stb = [P.sbuf("stb%d" % i, [128, 2048], BF16) for i in range(3)]
    csteps = []
    for kk in range(32):
        for half in range(2):
            for c6 in range(6):
                jg0 = c6 * 8
                nj = min(8, 43 - jg0)
                src = WUP[kk * 128:(kk + 1) * 128, half * DFF + jg0 * 256:half * DFF + (jg0 + nj) * 256]
                dst = WUPB.ap[jg0:jg0 + nj, :, kk, half * 256:(half + 1) * 256].rearrange("j p c -> p j c")
                csteps.append((WUP, src, WUPB, dst, nj, 256))
    for kk in range(NJ):
        for hh in range(2):
            src = WDN[kk * 128:(kk + 1) * 128, hh * 2048:(hh + 1) * 2048]
            dst = WDNB.ap[hh * 16:(hh + 1) * 16, :, kk, :].rearrange("m p c -> p m c")
            csteps.append((WDN, src, WDNB, dst, 16, 128))
    cstate = [0]

    def conv_store(i):
        SB, src, DB, dst, nj, w = csteps[i]
        b_ = stb[i % 3]
        P.I("sp", "dma_start", [b_], [DB], out=dst, in_=b_[:, 0:nj * w].rearrange("p (j c) -> p j c", c=w), dma=b_, join=True)

    def conv_advance():
        i = cstate[0]
        if i > len(csteps):
            return
        if i < len(csteps):
            SB, src, DB, dst, nj, w = csteps[i]
            f_, b_ = stf[i % 3], stb[i % 3]
            P.I("sp", "dma_start", [SB], [f_], out=f_[:, 0:nj * w], in_=src, dma=f_)
            P.I("pool", "tensor_copy", [f_], [b_], out=b_[:, 0:nj * w], in_=f_[:, 0:nj * w])
        if i >= 1:
            conv_store(i - 1)
        cstate[0] = i + 1

    ucnt = 0
    for head in range(16):
        hi = head % 2
        k_, v_, q_ = kt[hi], vt[hi], qt[hi]
        P.I("sp", "dma_start", [KT], [k_], out=k_[:, :], in_=KT[head, :, :], dma=k_)
        P.I("sp", "dma_start", [VS], [v_], out=v_[:, :, :], in_=VS[head, :, :, :], dma=v_)
        P.I("sp", "dma_start", [QT], [q_], out=q_[:, :], in_=QT[head, :, :], dma=q_)
        for qb in range(5):
            qc = qb * 410
            n = 410
            pend = None
            for t in range(NT + 1):
                if t < NT:
                    ps = psS[ucnt % 2]
                    pt_ = pT[ucnt % 3]
                    ucnt += 1
                    P.I("pe", "matmul", [k_, q_], [ps], ps[:, 0, 0:n], lhsT=k_[0:64, t * 128:(t + 1) * 128], rhs=q_[0:64, qc:qc + n],
                        start=True, stop=True)
                    P.I("pe", "matmul", [k_, q_], [ps], ps[:, 1, 0:n], lhsT=k_[64:128, t * 128:(t + 1) * 128], rhs=q_[64:128, qc:qc + n],
                        start=True, stop=True)
                    P.I("act", "activation", [ps, kb], [pt_], out=pt_[:, :, :], in_=ps[:, :, 0:n], func=AF.Exp, bias=kb[:, t:t + 1], scale=1.0)
                    cur = (t, pt_)
                    if ucnt % 9 == 0:
                        conv_advance()
                else:
                    cur = None
                if pend is not None:
                    tp, pp = pend
                    st, sp_ = (tp == 0), (tp == NT - 1)
                    for s in range(2):
                        P.I("pe", "matmul", [v_, pp], [psO[s]], psO[s][:, 0:n], lhsT=v_[:, tp, :], rhs=pp[:, s, :], start=st, stop=sp_)
                    for s in range(2):
                        P.I("pe", "matmul", [cb, pp], [psL[s]], psL[s][32 * s:32 * s + 32, 0:n], lhsT=onesb[:, 32 * s:32 * s + 32], rhs=pp[:, s, :],
                            start=st, stop=sp_, tile_position=(0, 32 * s))
                pend = cur
            for s in range(2):
                P.I("dve", "reciprocal", [psL[s]], [r32], out=r32[32 * s:32 * s + 32, :], in_=psL[s][32 * s:32 * s + 32, 0:n], join=True)
            for s in range(2):
                P.I("pe", "matmul", [r32, cf], [psL[s]], psL[s][:, 0:n], lhsT=onesDf[32 * s:32 * s + 32, :], rhs=r32[32 * s:32 * s + 32, :],
                    start=True, stop=True)
            for s in range(2):
                P.I("act", "activation", [psL[s]], [rr[s]], out=rr[s][:, :], in_=psL[s][:, 0:n], func=AF.Identity, scale=1.0 / 32)
                P.I("dve", "tensor_tensor", [psO[s], rr[s]], [rr[s]], out=rr[s][:, :], in0=psO[s][:, 0:n], in1=rr[s][:, :], op=ALU.mult)
            P.I("dve", "scalar_tensor_tensor", [rr[0], rr[1], sml], [o_], out=o_[:, :], in0=rr[1][:, :], scalar=sml[:, 2:3], in1=rr[0][:, :],
                op0=ALU.mult, op1=ALU.add)
            P.I("act", "activation", [o_], [osq], out=osq[:, :], in_=o_[:, :], func=AF.Square)
            P.I("pe", "matmul", [osq, cf], [psL[0]], psL[0][:, 0:n], lhsT=ones128f, rhs=osq[:, :], start=True, stop=True)
            P.I("act", "activation", [psL[0], eps6], [osq], out=osq[:, :], in_=psL[0][:, 0:n], func=AF.Sqrt, bias=eps6[:, 0:1], scale=1.0)
            P.I("dve", "reciprocal", [osq], [osq], out=osq[:, :], in_=osq[:, :])
            a_ = ast[(head * 5 + qb) % 2]
            P.I("dve", "scalar_tensor_tensor", [o_, osq, sml], [a_], out=a_[:, :], in0=o_[:, :], scalar=sml[:, 1:2], in1=osq[:, :],
                op0=ALU.mult, op1=ALU.mult)
            P.I("sp", "dma_start", [a_], [CAT], out=CAT[head, :, qc:qc + n], in_=a_[:, :], dma=a_, join=True)
    while cstate[0] <= len(csteps):
        conv_advance()
    P.phase_end()
    if stop_after <= 4:
        return _finish(P, nc, OUT, dbg, {})

    P.phase_begin()
    cbk = [P.sbuf("cbk%d" % i, [128, 32, 410], BF16) for i in range(2)]
    wo = [P.sbuf("wo%d" % i, [128, 32, 512], BF16) for i in range(2)]
    xtb = [P.sbuf("xtb%d" % i, [128, 4, 410], F32) for i in range(2)]
    x1b = [P.sbuf("x1b%d" % i, [128, 4, 410], F32) for i in range(2)]
    psM = [P.psum("psM%d" % i, [128, 512]) for i in range(4)]
    sqb = P.sbuf("sqb", [128, 4, 410], F32)
    s2 = P.sbuf("s2n", [128, NQ], F32)
    P.I("dve", "memset", [], [s2], s2[:, :], 0.0)
    CATv = CAT.ap.rearrange("c p s -> p c s")
    X1Tv = X1T.ap.rearrange("c p s -> p c s")
    cnt = 0
    for mg in range(8):
        w = wo[mg % 2]
        P.I("pool", "dma_start", [WOUT], [w], out=w[:, :, :], in_=WOUTv[:, :, mg * 512:(mg + 1) * 512], dma=w)
        for bl in range(5):
            sl = slice(bl * 410, bl * 410 + 410)
            cbl, xb, x1 = cbk[cnt % 2], xtb[cnt % 2], x1b[cnt % 2]
            cnt += 1
            P.I("sp", "dma_start", [CAT], [cbl], out=cbl[:, :, :], in_=CATv[:, :, sl], dma=cbl)
            P.I("sp", "dma_start", [XT], [xb], out=xb[:, :, :], in_=XTv[:, mg * 4:(mg + 1) * 4, QO + bl * 410:QO + bl * 410 + 410], dma=xb)
            for m in range(4):
                pm = psM[m]
                for k in range(32):
                    P.I("pe", "matmul", [w, cbl], [pm], pm[:, 0:410], lhsT=w[:, k, m * 128:(m + 1) * 128], rhs=cbl[:, k, :], start=(k == 0), stop=(k == 31))
                P.I("dve", "scalar_tensor_tensor", [pm, xb, mod], [x1], out=x1[:, m, :], in0=pm[:, 0:410], scalar=gate1[:, mg * 4 + m:mg * 4 + m + 1],
                    in1=xb[:, m, :], op0=ALU.mult, op1=ALU.add, join=True)
            P.I("act", "activation", [x1], [sqb], out=sqb[:, :, :], in_=x1[:, :, :], func=AF.Square)
            for m in range(4):
                P.I("pool", "tensor_tensor", [s2, sqb], [s2], out=s2[:, sl], in0=s2[:, sl], in1=sqb[:, m, :], op=ALU.add)
            P.I("pool", "dma_start", [x1], [X1T], out=X1Tv[:, mg * 4:(mg + 1) * 4, sl], in_=x1[:, :, :], dma=x1, join=True)
    eps6b = P.sbuf("eps6b", [128, 1], F32)
    P.I("dve", "memset", [], [eps6b], eps6b[:, :], EPS6)
    for bl in range(5):
        sl = slice(bl * 410, bl * 410 + 410)
        pm = psM[bl % 4]
        P.I("pe", "matmul", [s2, cf], [pm], pm[:, 0:410], lhsT=onesDf, rhs=s2[:, sl], start=True, stop=True)
        P.I("act", "activation", [pm, eps6b], [rstd2], out=rstd2[:, sl], in_=pm[:, 0:410], func=AF.Sqrt, bias=eps6b[:, 0:1], scale=1.0 / D, join=True)
    P.I("dve", "reciprocal", [rstd2], [rstd2], out=rstd2[:, :], in_=rstd2[:, :])
    P.phase_end()
    if stop_after <= 5:
        return _finish(P, nc, OUT, dbg, {})

    P.phase_begin()
    h2 = P.sbuf("h2", [128, 32, 412], BF16)
    x1s = [P.sbuf("x1s%d" % i, [128, 4, 412], F32) for i in range(1)]
    Ab = P.sbuf("Ab", [128, NJ, 410], BF16)
    wb = [P.sbuf("wb%d" % i, [128, 32, 512], BF16) for i in range(2)]
    psU = [P.psum("psU%d" % i, [128, 512]) for i in range(2)]
    psGt = [P.psum("psGt%d" % i, [128, 512]) for i in range(2)]
    psY = [P.psum("psY%d" % i, [128, 512]) for i in range(2)]
    psT = [P.psum("psT%d" % i, [128, 4, 128]) for i in range(2)]
    cu = [P.sbuf("cu%d" % i, [128, 410], F32) for i in range(2)]
    cg = [P.sbuf("cg%d" % i, [128, 410], F32) for i in range(2)]
    sgt = [P.sbuf("sgt%d" % i, [128, 410], F32) for i in range(2)]
    x1r = [P.sbuf("x1r%d" % i, [128, 410], F32) for i in range(2)]
    yT = [P.sbuf("yT%d" % i, [128, 410], F32) for i in range(2)]
    ost = [P.sbuf("ost%d" % i, [128, 4, 128], F32) for i in range(2)]
    b2v = mod.ap
    wcnt = 0
    def prep_steps(b):
        c0 = b * 410
        nin = min(412, NQ - c0)
        steps = []

        def mk(cg4):
            def f():
                xs_ = x1s[0]
                P.I("pool", "dma_start", [X1T], [xs_], out=xs_[:, :, 0:nin], in_=X1Tv[:, cg4 * 4:(cg4 + 1) * 4, c0:c0 + nin], dma=xs_)
                for c4 in range(4):
                    c = cg4 * 4 + c4
                    P.I("dve", "tensor_tensor", [xs_, rstd2], [xs_], out=xs_[:, c4, 0:nin], in0=xs_[:, c4, 0:nin], in1=rstd2[:, c0:c0 + nin],
                        op=ALU.mult, join=True)
                    P.I("dve", "tensor_scalar", [xs_, der, mod], [h2], out=h2[:, c, 0:nin], in0=xs_[:, c4, 0:nin], scalar1=A2[:, c:c + 1],
                        scalar2=B2[:, c:c + 1], op0=ALU.mult, op1=ALU.add, join=True)
            return f

        for cg4 in range(8):
            steps.append(mk(cg4))

        def masks():
            if b == 0:
                for c in range(32):
                    P.I("dve", "tensor_scalar", [h2, sml], [h2], out=h2[:, c, 0:1], in0=h2[:, c, 0:1], scalar1=sml[:, 8:9], scalar2=None,
                        op0=ALU.mult, join=True)
            if b == 4:
                for c in range(32):
                    P.I("dve", "tensor_scalar", [h2, sml], [h2], out=h2[:, c, nin - 1:nin], in0=h2[:, c, nin - 1:nin], scalar1=sml[:, 9:10],
                        scalar2=None, op0=ALU.mult, join=True)
        steps.append(masks)
        return steps

    for f_ in prep_steps(0):
        f_()
    for b in range(5):
        c0 = b * 410
        nin = min(412, NQ - c0)
        nout = nin - 2
        for jg in range(43):
            w = wb[wcnt % 2]
            wcnt += 1
            P.I("sp", "dma_start", [WUPB], [w], out=w[:, :, :], in_=WUPB[jg, :, :, :], dma=w)
            for jj in range(2):
                j = jg * 2 + jj
                pu, pg = psU[jj], psGt[jj]
                for k in range(32):
                    P.I("pe", "matmul", [w, h2], [pu], pu[:, 0:nin], lhsT=w[:, k, jj * 128:(jj + 1) * 128], rhs=h2[:, k, 0:nin], start=(k == 0), stop=(k == 31))
                for k in range(32):
                    P.I("pe", "matmul", [w, h2], [pg], pg[:, 0:nin], lhsT=w[:, k, 256 + jj * 128:256 + (jj + 1) * 128], rhs=h2[:, k, 0:nin],
                        start=(k == 0), stop=(k == 31))
                for (pp, dst, fo) in ((pu, cu[jj], j), (pg, cg[jj], NJ + j)):
                    P.I("dve", "tensor_scalar", [pp, vec], [dst], out=dst[:, 0:nout], in0=pp[:, 0:nout], scalar1=V("fw", 0 * 172 + fo), scalar2=V("fb", fo),
                        op0=ALU.mult, op1=ALU.add)
                    for kk in (1, 2):
                        P.I("dve", "scalar_tensor_tensor", [pp, vec, dst], [dst], out=dst[:, 0:nout], in0=pp[:, kk:kk + nout], scalar=V("fw", kk * 172 + fo),
                            in1=dst[:, 0:nout], op0=ALU.mult, op1=ALU.add)
                P.I("act", "activation", [cg[jj]], [sgt[jj]], out=sgt[jj][:, 0:nout], in_=cg[jj][:, 0:nout], func=AF.Silu)
                P.I("pool", "tensor_tensor", [sgt[jj], cu[jj]], [Ab], out=Ab[:, j, 0:nout], in0=sgt[jj][:, 0:nout], in1=cu[jj][:, 0:nout], op=ALU.mult, join=True)
        nxt = prep_steps(b + 1) if b + 1 < 5 else []
        ntl = (nout + 127) // 128
        pend_ep = None
        for m in range(32):
            w = wb[wcnt % 2]
            wcnt += 1
            wd = w.ap.rearrange("p a b -> p (a b)")[:, 0:NJ * 128].rearrange("p (k n) -> p k n", n=128)
            P.I("sp", "dma_start", [WDNB], [w], out=wd, in_=WDNB[m, :, :, :], dma=w)
            xr = x1r[m % 2]
            P.I("pool", "dma_start", [X1T], [xr], out=xr[:, 0:nout], in_=X1T[m, :, c0 + 1:c0 + 1 + nout], dma=xr)
            py = psY[m % 2]
            for k in range(NJ):
                P.I("pe", "matmul", [w, Ab], [py], py[:, 0:nout], lhsT=wd[:, k, :], rhs=Ab[:, k, 0:nout], start=(k == 0), stop=(k == NJ - 1))
            y = yT[m % 2]
            P.I("dve", "scalar_tensor_tensor", [py, xr, mod], [y], out=y[:, 0:nout], in0=py[:, 0:nout], scalar=gate2[:, m:m + 1], in1=xr[:, 0:nout],
                op0=ALU.mult, op1=ALU.add)
            def epilogue(m=m, y=y, c0=c0, nout=nout, ntl=ntl):
                pt_ = psT[m % 2]
                os_ = ost[m % 2]
                for tl in range(ntl):
                    nt_ = min(128, nout - tl * 128)
                    P.I("pe", "transpose", [y, cf], [pt_], out=pt_[0:nt_, tl, :], in_=y[:, tl * 128:tl * 128 + nt_], identity=identf)
                nfull = nout // 128
                P.I("act", "copy", [pt_], [os_], out=os_[:, 0:nfull, :], in_=pt_[:, 0:nfull, :])
                r0 = c0
                P.I("pool", "dma_start", [os_], [OUT], out=OUT.ap[r0:r0 + nfull * 128, m * 128:(m + 1) * 128].rearrange("(t p) f -> p t f", p=128),
                    in_=os_[:, 0:nfull, :], dma=os_, join=True)
                rem = nout - nfull * 128
                if rem:
                    P.I("dve", "tensor_copy", [pt_], [os_], out=os_[0:rem, nfull, :], in_=pt_[0:rem, nfull, :], join=True)
                    P.I("pool", "dma_start", [os_], [OUT], out=OUT.ap[r0 + nfull * 128:r0 + nout, m * 128:(m + 1) * 128], in_=os_[0:rem, nfull, :],
                        dma=os_, join=True)
            if pend_ep is not None:
                pend_ep()
            pend_ep = epilogue
            if m % 2 == 1 and nxt:
                nxt.pop(0)()
        pend_ep()
        pend_ep = None
        while nxt:
            nxt.pop(0)()
    P.wait_only("sp", [OUT])
    P.phase_end(final=True)
    P.ctx.close()
    return nc


def _finish(P, nc, OUT, dbg, taps):
    P.phase_begin()
    outs = [OUT]
    for name, buf in taps.items():
        shp = list(buf.ap.shape)
        t = Buf(nc.dram_tensor("tap_" + name, shp, buf.ap.dtype, kind="ExternalOutput"), "tap_" + name)
        P.I("sp", "dma_start", [buf], [t], out=t.full(), in_=buf.full(), dma=buf)
        outs.append(t)
    P.wait_only("sp", outs)
    P.phase_end(final=True)
    P.ctx.close()
    return nc


def _rope_tables():
    rows = SEQ // 64
    row = np.repeat(np.arange(rows), 64).astype(np.float32)
    col = np.tile(np.arange(64), rows).astype(np.float32)
    freqs = (np.float32(10000.0) ** (-np.arange(16, dtype=np.float32) / np.float32(16))).astype(np.float32)
    ang = np.concatenate([row[:, None] * freqs, col[:, None] * freqs], axis=-1).astype(np.float32)
    return np.cos(ang).astype(np.float32), np.sin(ang).astype(np.float32)


def _core_layout(qi):
    T0 = qi * 2048
    slot_tok = np.full(NS, -1, dtype=np.int64)
    ext = np.arange(T0 - 64, T0 + 2112)
    ok = (ext >= 0) & (ext < SEQ)
    slot_tok[0:NEXT] = np.where(ok, ext, -1)
    slot_tok[NEXT:NEXT + NCTX] = -2 - np.arange(NCTX)
    lo, hi = max(T0 - 64, 0), min(T0 + 2112, SEQ)
    others = np.concatenate([np.arange(0, lo), np.arange(hi, SEQ)])
    slot_tok[NEXT + NCTX:NEXT + NCTX + len(others)] = others
    return slot_tok


def _pm(v, n):
    return np.ascontiguousarray(np.asarray(v, dtype=np.float32).reshape(n, 128).T)


def make_in_maps(x, c, ctx, c_ctx, w_ada, b_ada, norm1_g, norm2_g, w_in, q_norm_g, k_norm_g,
                 lambda_q1, lambda_k1, lambda_q2, lambda_k2, subln_g, conv_dw_w, conv_dw_b,
                 conv_ln_g, conv_ln_b, w_out, w_up, ffn_dw_w, ffn_dw_b, w_down, cores=range(8)):
    x = np.asarray(x, dtype=np.float32)
    ctx = np.asarray(ctx, dtype=np.float32)
    cosT, sinT = _rope_tables()
    eye = np.eye(128, dtype=np.float32)
    sw = np.zeros((128, 128), np.float32)
    idx = np.arange(128)
    sw[idx, idx ^ 1] = 1.0
    o64 = np.zeros((128, 128), np.float32)
    o64[:64, :64] = 1.0 / 64
    o64[64:, 64:] = 1.0 / 64
    constb = np.concatenate([eye, sw, o64, np.ones((128, 128), np.float32)], axis=1).astype(ml_dtypes.bfloat16)
    constf = np.concatenate([eye, np.full((128, 128), 1.0 / 128, np.float32), np.ones((128, 128), np.float32)], axis=1)
    p = np.arange(128)
    pair = (p % 64) // 2
    sgn = np.where(p % 2 == 0, -1.0, 1.0).astype(np.float32)

    base = np.zeros((128, NV), np.float32)

    def put(name, arr):
        arr = np.asarray(arr, np.float32)
        base[:, _VOFF[name]:_VOFF[name] + arr.shape[1]] = arr

    put("g1", _pm(norm1_g[0], 32))
    put("g2", _pm(norm2_g[0], 32))
    put("bada", _pm(b_ada[0], 192))
    put("gq", np.asarray(q_norm_g[0], np.float32)[p % 64][:, None])
    put("gk", np.asarray(k_norm_g[0], np.float32)[p % 64][:, None])
    put("gs", np.asarray(subln_g[0], np.float32)[:, None])
    cw = np.asarray(conv_dw_w[0], np.float32)
    put("cw", cw.reshape(31, 16, 128).transpose(2, 0, 1).reshape(128, 31 * 16))
    put("cb", _pm(conv_dw_b[0], 16))
    put("lg", _pm(conv_ln_g[0], 16))
    put("lb", _pm(conv_ln_b[0], 16))
    fw = np.asarray(ffn_dw_w[0], np.float32)
    put("fw", fw.reshape(3, 172, 128).transpose(2, 0, 1).reshape(128, 3 * 172))
    put("fb", _pm(ffn_dw_b[0], 172))
    put("ccT", _pm(c_ctx, 32))
    lamv = np.concatenate([np.asarray(v[0], np.float32) for v in (lambda_q1, lambda_k1, lambda_q2, lambda_k2)])
    put("lam", np.broadcast_to(lamv[None, :], (128, 256)))

    wa2, wi2, wo2, wu2, wd2 = (np.asarray(w[0], dtype=np.float32) for w in (w_ada, w_in, w_out, w_up, w_down))
    in_maps = []
    for core in cores:
        b, qi = core // 4, core % 4
        st = _core_layout(qi)
        xs = np.zeros((NS, D), np.float32)
        mx = st >= 0
        xs[mx] = x[b][st[mx]]
        mc = st <= -2
        xs[mc] = ctx[b][-2 - st[mc]]
        validslot = (st != -1)
        keybias = np.where(validslot, 0.0, -30000.0).astype(np.float32).reshape(NT, 128).T
        valid = np.broadcast_to(validslot[:NEXT].astype(np.float32)[None, :], (128, NEXT))
        cs = np.ones((128, NS), np.float32)
        sn = np.zeros((128, NS), np.float32)
        cs[:, mx] = cosT[st[mx]][:, pair].T
        sn[:, mx] = sinT[st[mx]][:, pair].T * sgn[:, None]
        vecs = base.copy()
        vecs[:, _VOFF["cT"]:_VOFF["cT"] + 32] = _pm(c[b], 32)
        in_maps.append({
            "xs": xs, "vecs": vecs, "keybias": np.ascontiguousarray(keybias), "valid": np.ascontiguousarray(valid),
            "cosT": cs, "sinT": sn, "constb": constb, "constf": constf,
            "w_ada": wa2, "w_in": wi2, "w_out": wo2, "w_up": wu2, "w_down": wd2,
        })
    return in_maps


def kernel(**inputs):
    in_maps = make_in_maps(**inputs)
    nc = build_program()
    res = run_bass_kernel_spmd(nc, in_maps, core_ids=list(range(8)))
    out = np.empty((2, SEQ, D), np.float32)
    for core in range(8):
        b, qi = core // 4, core % 4
        out[b, qi * 2048:(qi + 1) * 2048] = res.results[core]["out"]
    return out
```

```python
from contextlib import ExitStack
import numpy as np
import ml_dtypes
import concourse.bass as bass
import concourse.mybir as mybir
from concourse.bass_utils import run_bass_kernel_spmd

F32 = mybir.dt.float32
BF16 = mybir.dt.bfloat16
ALU = mybir.AluOpType
AF = mybir.ActivationFunctionType

D = 4096
NCH = 32
SEQ = 8192
NCTX = 256
NT = 67
NS = NT * 128
NEXT = 2176
QO = 63
NQ = 2050
UO = 48
NU = 2080
DFF = 11008
NJ = 86
EPS6 = 1e-6
EPS5 = 1e-5
LAM_INIT = 0.2
ENGS = ("pe", "act", "dve", "pool", "sp")

_VOFF = {}
_o = 0
for _n, _w in (("g1", 32), ("g2", 32), ("bada", 192), ("gq", 1), ("gk", 1), ("gs", 1),
               ("cw", 31 * 16), ("cb", 16), ("lg", 16), ("lb", 16),
               ("fw", 3 * 172), ("fb", 172), ("cT", 32), ("ccT", 32), ("lam", 256)):
    _VOFF[_n] = _o
    _o += _w
NV = _o


class Buf:
    def __init__(self, ap, name):
        self.ap = ap
        self.name = name
        self.w_ev = {}
        self.r_ev = {}
        self.dsem = None
        self.dcount = 0

    def __getitem__(self, idx):
        return self.ap[idx]

    def full(self):
        return self.ap[(slice(None),) * len(self.ap.shape)]


class Op:
    __slots__ = ("eng", "fn", "waits", "signal", "semval", "dbuf", "seq")

    def __init__(self, eng, fn, seq):
        self.eng = eng
        self.fn = fn
        self.waits = {}
        self.signal = False
        self.semval = None
        self.dbuf = None
        self.seq = seq


class Prog:
    def __init__(self, nc):
        self.nc = nc
        self.ctx = ExitStack()
        self.pstack = None
        self.ops = {e: [] for e in ENGS}
        self.seq = {e: 0 for e in ENGS}
        self.ecount = {e: 0 for e in ENGS}
        self.waited = {e: {} for e in ENGS}
        self.esem = {e: self.ctx.enter_context(nc.semaphore("e_" + e)) for e in ENGS}
        self.nbuf = 0
        self.touched = {}
        self.bar_src = None
        self.bar_dst = None
        self.tok = None
        self.dbg = ()
        self.uid = 0

    def sbuf(self, name, shape, dtype, glob=False):
        st = self.ctx if (glob or self.pstack is None) else self.pstack
        self.uid += 1
        return Buf(st.enter_context(self.nc.sbuf_tensor("%s_%d" % (name, self.uid), list(shape), dtype)), name)

    def psum(self, name, shape, dtype=F32):
        st = self.ctx if self.pstack is None else self.pstack
        self.uid += 1
        return Buf(st.enter_context(self.nc.psum_tensor("%s_%d" % (name, self.uid), list(shape), dtype)), name)

    def dram(self, name, shape, dtype, kind="Internal"):
        if self.dbg and name in self.dbg:
            kind = "ExternalOutput"
        return Buf(self.nc.dram_tensor(name, list(shape), dtype, kind=kind), name)

    def op(self, eng, fn, reads=(), writes=(), dma=None, join=False):
        o = Op(eng, fn, self.seq[eng])
        self.seq[eng] += 1
        waits = o.waits

        def need(evd):
            for key, val in evd.items():
                if key[0] == "E":
                    if key[1] == eng and eng == "pe":
                        continue
                    old = waits.get(key)
                    if old is None or old.seq < val.seq:
                        waits[key] = val
                else:
                    waits[key] = key[1].dcount

        for r in reads:
            need(r.w_ev)
            self.touched[id(r)] = r
        for w in writes:
            if not join:
                need(w.w_ev)
            need(w.r_ev)
            self.touched[id(w)] = w
        for key, val in waits.items():
            if key[0] == "E":
                val.signal = True
        if dma is not None:
            if dma.dsem is None:
                dma.dsem = self.ctx.enter_context(self.nc.semaphore("d%d" % self.nbuf))
                self.nbuf += 1
            dma.dcount += 16
            key, val = ("D", dma), dma.dcount
            o.dbuf = dma
        else:
            key, val = ("E", eng), o
        for w in writes:
            if join:
                w.w_ev[key] = val
            else:
                w.w_ev = {key: val}
                w.r_ev = {}
        for r in reads:
            r.r_ev[key] = val
        self.ops[eng].append(o)
        return o

    def I(self, eng, meth, reads, writes, *a, dma=None, join=False, **kw):
        return self.op(eng, lambda e: getattr(e, meth)(*a, **kw), reads, writes, dma=dma, join=join)

    def wait_only(self, eng, bufs):
        o = Op(eng, None, self.seq[eng])
        self.seq[eng] += 1
        for b in bufs:
            for key, val in list(b.w_ev.items()) + list(b.r_ev.items()):
                if key[0] == "E":
                    if key[1] == eng:
                        continue
                    val.signal = True
                    old = o.waits.get(key)
                    if old is None or old.seq < val.seq:
                        o.waits[key] = val
                else:
                    o.waits[key] = key[1].dcount
        self.ops[eng].append(o)

    def barrier(self):
        bufs = list(self.touched.values())
        self.touched = {}
        src, dst, tok = self.bar_src, self.bar_dst, self.tok
        o = Op("sp", lambda e: e.dma_start(out=dst[0:1, 0:16], in_=src[0:1, 0:16]), self.seq["sp"])
        self.seq["sp"] += 1
        for b in bufs + [tok]:
            for key, val in list(b.w_ev.items()) + list(b.r_ev.items()):
                if key[0] == "E":
                    val.signal = True
                    old = o.waits.get(key)
                    if old is None or old.seq < val.seq:
                        o.waits[key] = val
                else:
                    o.waits[key] = key[1].dcount
        if tok.dsem is None:
            tok.dsem = self.ctx.enter_context(self.nc.semaphore("d%d" % self.nbuf))
            self.nbuf += 1
        tok.dcount += 16
        o.dbuf = tok
        self.ops["sp"].append(o)
        for b in bufs:
            b.w_ev = {}
            b.r_ev = {}
        tok.w_ev = {("D", tok): tok.dcount}
        tok.r_ev = {}
        for e in ENGS:
            if e != "sp":
                self.wait_only(e, [tok])

    def emit(self):
        nc = self.nc
        for e in ENGS:
            for o in self.ops[e]:
                if o.signal:
                    self.ecount[e] += 1
                    o.semval = self.ecount[e]
        prog = self

        def run(engname, engine):
            waited = prog.waited[engname]
            for o in prog.ops[engname]:
                for key, val in o.waits.items():
                    if key[0] == "E":
                        v = val.semval
                        sem = prog.esem[key[1]]
                    else:
                        v = val
                        sem = key[1].dsem
                    if waited.get(key, 0) >= v:
                        continue
                    waited[key] = v
                    engine.wait_ge(sem, v)
                if o.fn is None:
                    continue
                ins = o.fn(engine)
                if o.dbuf is not None:
                    ins.then_inc(o.dbuf.dsem, 16)
                elif o.signal:
                    ins.then_inc(prog.esem[engname], 1)

        with nc.Block() as block:
            @block.sync
            def _(e):
                run("sp", e)

            @block.tensor
            def _(e):
                run("pe", e)

            @block.scalar
            def _(e):
                run("act", e)

            @block.vector
            def _(e):
                run("dve", e)

            @block.gpsimd
            def _(e):
                run("pool", e)
        self.ops = {e: [] for e in ENGS}

    def phase_begin(self):
        self.pstack = ExitStack()

    def phase_end(self, final=False):
        if not final:
            self.barrier()
        self.emit()
        self.pstack.close()
        self.pstack = None


def build_program(stop_after=99, dbg=False):
    nc = bass.Bass("TRN2", target_bir_lowering=False)
    P = Prog(nc)
    P.dbg = dbg or ()
    print('sbuf bytes remaining', nc.sbuf_bytes_remaining)

    def din(name, shape, dt=F32):
        return Buf(nc.dram_tensor(name, list(shape), dt, kind="ExternalInput"), name)

    XS = din("xs", [NS, D])
    VEC = din("vecs", [128, NV])
    KB = din("keybias", [128, NT])
    VAL = din("valid", [128, NEXT])
    COS = din("cosT", [128, NS])
    SIN = din("sinT", [128, NS])
    CB = din("constb", [128, 4 * 128], BF16)
    CF = din("constf", [128, 3 * 128])
    WADA = din("w_ada", [D, 6 * D])
    WIN = din("w_in", [D, 10240])
    WOUT = din("w_out", [D, D])
    WUP = din("w_up", [D, 2 * DFF])
    WDN = din("w_down", [DFF, D])
    OUT = Buf(nc.dram_tensor("out", [2048, D], F32, kind="ExternalOutput"), "out")

    H1T = P.dram("h1t", [D, NS], BF16)
    XT = P.dram("xtT", [D, NEXT], F32)
    KT = P.dram("kt", [16, 128, NS], BF16)
    VS = P.dram("vs", [16, 128, NT, 128], BF16)
    QT = P.dram("qt", [16, 128, NQ], BF16)
    YS = P.dram("ys", [16, 128, NQ], F32)
    CAT = P.dram("cat", [32, 128, NQ], BF16)
    X1T = P.dram("x1t", [32, 128, NQ], F32)
    WUPB = P.dram("wupb", [43, 128, 32, 512], BF16)
    WDNB = P.dram("wdnb", [32, 128, NJ, 128], BF16)
    BARD = P.dram("bard", [1, 16], F32)
    P.bar_dst = BARD
    H1Tv = H1T.ap.rearrange("(c p) s -> p c s", p=128)
    XTv = XT.ap.rearrange("(c p) s -> p c s", p=128)
    WINv = WIN.ap.rearrange("(k p) n -> p k n", p=128)
    WADAv = WADA.ap.rearrange("(k p) n -> p k n", p=128)
    WOUTv = WOUT.ap.rearrange("(k p) n -> p k n", p=128)
    WUPv = WUP.ap.rearrange("(k p) n -> p k n", p=128)
    WDNv = WDN.ap.rearrange("(k p) n -> p k n", p=128)

    vec = P.sbuf("vec", [128, NV], F32, glob=True)
    cb = P.sbuf("cb", [128, 4 * 128], BF16, glob=True)
    cf = P.sbuf("cf", [128, 3 * 128], F32, glob=True)
    mod = P.sbuf("mod", [128, 192, 2], F32, glob=True)
    der = P.sbuf("der", [128, 8, 32], F32, glob=True)
    sml = P.sbuf("sml", [128, 16], F32, glob=True)
    kb = P.sbuf("kb", [128, NT], F32, glob=True)
    tok = P.sbuf("tok", [1, 16], F32, glob=True)
    rstd2 = P.sbuf("rstd2", [128, NQ], F32, glob=True)
    P.tok = tok
    P.bar_src = tok
    identb = cb.ap[:, 0:128]
    pswap = cb.ap[:, 128:256]
    ones64 = cb.ap[:, 256:384]
    onesb = cb.ap[:, 384:512]
    identf = cf.ap[:, 0:128]
    ones128f = cf.ap[:, 128:256]
    onesDf = cf.ap[:, 256:384]

    def V(name, i=0, w=1):
        o = _VOFF[name] + i
        return vec.ap[:, o:o + w]

    A1x, A1c, A2 = der.ap[:, 0, :], der.ap[:, 1, :], der.ap[:, 2, :]
    B1x, B1c = mod.ap[:, 0:32, 0], mod.ap[:, 0:32, 1]
    gate1, B2, gate2 = mod.ap[:, 64:96, 0], mod.ap[:, 96:128, 0], mod.ap[:, 160:192, 0]

    P.phase_begin()
    P.I("dve", "memset", [], [tok], tok[:, :], 0.0)
    for dst, src in ((vec, VEC), (cb, CB), (cf, CF), (kb, KB)):
        P.I("sp", "dma_start", [src], [dst], out=dst.full(), in_=src.full(), dma=dst)
    sc = P.sbuf("sc", [128, 32, 2], BF16)
    P.I("act", "activation", [vec], [sc], out=sc[:, :, 0], in_=V("cT", 0, 32), func=AF.Silu)
    P.I("act", "activation", [vec], [sc], out=sc[:, :, 1], in_=V("ccT", 0, 32), func=AF.Silu, join=True)
    wa = [P.sbuf("wa%d" % i, [128, 32, 1024], BF16) for i in range(2)]
    psmod = P.psum("psmod", [128, 256, 2])
    for nb in range(24):
        w = wa[nb % 2]
        P.I("pool", "dma_start", [WADA], [w], out=w[:, :, :], in_=WADAv[:, :, nb * 1024:(nb + 1) * 1024], dma=w)
        for m in range(8):
            for k in range(32):
                P.I("pe", "matmul", [w, sc], [psmod], psmod[:, nb * 8 + m, :], lhsT=w[:, k, m * 128:(m + 1) * 128],
                    rhs=sc[:, k, :], start=(k == 0), stop=(k == 31))
    for j in range(2):
        P.I("dve", "tensor_tensor", [psmod, vec], [mod], out=mod[:, :, j], in0=psmod[:, 0:192, j],
            in1=V("bada", 0, 192), op=ALU.add, join=True)
    for di, (sl, j, g) in enumerate(((slice(32, 64), 0, "g1"), (slice(32, 64), 1, "g1"), (slice(128, 160), 0, "g2"))):
        P.I("dve", "tensor_scalar", [mod], [der], out=der[:, 3, :], in0=mod[:, sl, j], scalar1=1.0, scalar2=None, op0=ALU.add)
        P.I("dve", "tensor_tensor", [der, vec], [der], out=der[:, di, :], in0=der[:, 3, :], in1=V(g, 0, 32), op=ALU.mult)
    P.I("dve", "tensor_scalar", [vec], [sml], out=sml[:, 0:1], in0=V("gq"), scalar1=0.125, scalar2=None, op0=ALU.mult)
    P.I("dve", "tensor_scalar", [vec], [sml], out=sml[:, 1:2], in0=V("gs"), scalar1=1.0 - LAM_INIT, scalar2=None, op0=ALU.mult)
    lamt = P.sbuf("lamt", [128, 128], F32)
    P.I("dve", "tensor_tensor", [vec], [lamt], out=lamt[:, 0:64], in0=V("lam", 0, 64), in1=V("lam", 64, 64), op=ALU.mult)
    P.I("dve", "tensor_tensor", [vec], [lamt], out=lamt[:, 64:128], in0=V("lam", 128, 64), in1=V("lam", 192, 64), op=ALU.mult)
    P.I("dve", "tensor_reduce", [lamt], [sml], out=sml[:, 3:4], in_=lamt[:, 0:64], axis=mybir.AxisListType.X, op=ALU.add)
    P.I("dve", "tensor_reduce", [lamt], [sml], out=sml[:, 4:5], in_=lamt[:, 64:128], axis=mybir.AxisListType.X, op=ALU.add)
    P.I("act", "activation", [sml], [sml], out=sml[:, 5:7], in_=sml[:, 3:5], func=AF.Exp)
    P.I("dve", "tensor_tensor", [sml], [sml], out=sml[:, 7:8], in0=sml[:, 6:7], in1=sml[:, 5:6], op=ALU.subtract)
    P.I("dve", "tensor_scalar", [sml], [sml], out=sml[:, 2:3], in0=sml[:, 7:8], scalar1=-LAM_INIT, scalar2=None, op0=ALU.add)
    P.phase_end()
    if stop_after <= 0:
        return _finish(P, nc, OUT, dbg, {"mod": mod, "sml": sml})

    P.phase_begin()
    xt = [P.sbuf("xt%d" % i, [128, D], F32) for i in range(2)]
    junk = P.sbuf("junk", [128, D], BF16)
    xsb = [P.sbuf("xsb%d" % i, [128, D], BF16) for i in range(2)]
    ss = [P.sbuf("ss%d" % i, [128, 4], F32) for i in range(2)]
    hst = [P.sbuf("hst%d" % i, [128, 32, 128], BF16) for i in range(2)]
    xst = [P.sbuf("xst%d" % i, [128, 32, 128], F32) for i in range(2)]
    ptb = [P.psum("ptb%d" % i, [128, 8, 128], BF16) for i in range(2)]
    ptf = [P.psum("ptf%d" % i, [128, 4, 128], F32) for i in range(2)]
    epsb = P.sbuf("epsb", [128, 1], F32)
    P.I("dve", "memset", [], [epsb], epsb[:, :], EPS6)
    P.I("sp", "dma_start", [XS], [xt[0]], out=xt[0][:, :], in_=XS[0:128, :], dma=xt[0])
    for t in range(NT):
        i = t % 2
        x = xt[i]
        if t + 1 < NT:
            xn = xt[(t + 1) % 2]
            P.I("sp", "dma_start", [XS], [xn], out=xn[:, :], in_=XS[(t + 1) * 128:(t + 2) * 128, :], dma=xn)
        P.I("dve", "memset", [], [ss[i]], ss[i][:, :], 0.0)
        P.I("act", "activation", [x, ss[i]], [junk, ss[i]], out=junk[:, :], in_=x[:, :], func=AF.Square, accum_out=ss[i][:, 0:1])
        P.I("act", "activation", [ss[i], epsb], [ss[i]], out=ss[i][:, 1:2], in_=ss[i][:, 0:1], func=AF.Sqrt, scale=1.0 / D, bias=epsb[:, 0:1])
        P.I("dve", "reciprocal", [ss[i]], [ss[i]], out=ss[i][:, 2:3], in_=ss[i][:, 1:2])
        P.I("dve", "tensor_scalar", [x, ss[i]], [xsb[i]], out=xsb[i][:, :], in0=x[:, :], scalar1=ss[i][:, 2:3], scalar2=None, op0=ALU.mult)
        isctx = 17 <= t < 19
        Aa, Bb = (A1c, B1c) if isctx else (A1x, B1x)
        for g in range(4):
            pb = ptb[g % 2]
            for c8 in range(8):
                c = g * 8 + c8
                P.I("pe", "transpose", [xsb[i], cb], [pb], out=pb[:, c8, :], in_=xsb[i][:, c * 128:(c + 1) * 128], identity=identb)
            for c8 in range(8):
                c = g * 8 + c8
                P.I("dve", "tensor_scalar", [pb, mod, der], [hst[i]], out=hst[i][:, c, :], in0=pb[:, c8, :], scalar1=Aa[:, c:c + 1],
                    scalar2=Bb[:, c:c + 1], op0=ALU.mult, op1=ALU.add, join=True)
        for q4 in range(4):
            P.I("sp", "dma_start", [hst[i]], [H1T], out=H1Tv[:, q4 * 8:(q4 + 1) * 8, t * 128:(t + 1) * 128], in_=hst[i][:, q4 * 8:(q4 + 1) * 8, :],
                dma=hst[i], join=True)
        if t < 17:
            for g in range(8):
                pf = ptf[g % 2]
                for c4 in range(4):
                    c = g * 4 + c4
                    P.I("pe", "transpose", [x, cf], [pf], out=pf[:, c4, :], in_=x[:, c * 128:(c + 1) * 128], identity=identf)
                eng = "act" if g % 2 == 0 else "dve"
                if eng == "act":
                    P.I("act", "copy", [pf], [xst[i]], out=xst[i][:, g * 4:(g + 1) * 4, :], in_=pf[:, :, :], join=True)
                else:
                    P.I("dve", "tensor_copy", [pf], [xst[i]], out=xst[i][:, g * 4:(g + 1) * 4, :], in_=pf[:, :, :], join=True)
            for q4 in range(4):
                P.I("sp", "dma_start", [xst[i]], [XT], out=XTv[:, q4 * 8:(q4 + 1) * 8, t * 128:(t + 1) * 128], in_=xst[i][:, q4 * 8:(q4 + 1) * 8, :],
                    dma=xst[i], join=True)
    P.phase_end()
    if stop_after <= 1:
        return _finish(P, nc, OUT, dbg, {})

    def qk_post_a(T, ps, n, gcol):
        sq, sd, kg, t1, t2, psN, psR = T[:7]
        P.I("act", "activation", [ps], [sq], out=sq[:, 0:n], in_=ps[:, 0:n], func=AF.Square)
        P.I("pe", "matmul", [sq, cb], [psN], psN[:, 0:n], lhsT=ones64, rhs=sq[:, 0:n], start=True, stop=True)
        P.I("act", "activation", [psN, T[7]], [sd], out=sd[:, 0:n], in_=psN[:, 0:n], func=AF.Sqrt, bias=T[7][:, 0:1], scale=1.0)
        P.I("dve", "reciprocal", [sd], [sd], out=sd[:, 0:n], in_=sd[:, 0:n])
        P.I("dve", "scalar_tensor_tensor", [ps, sd, vec, sml], [kg], out=kg[:, 0:n], in0=ps[:, 0:n], scalar=gcol, in1=sd[:, 0:n],
            op0=ALU.mult, op1=ALU.mult)

    def qk_post_b(T, n, cs, dst_ap, dst_buf):
        sq, sd, kg, t1, t2, psN, psR = T[:7]
        P.I("pe", "matmul", [kg, cb], [psR], psR[:, 0:n], lhsT=pswap, rhs=kg[:, 0:n], start=True, stop=True)
        P.I("pool", "tensor_tensor", [kg, cs], [t1], out=t1[:, 0:n], in0=kg[:, 0:n], in1=cs[:, 0, 0:n], op=ALU.mult)
        P.I("dve", "tensor_tensor", [psR, cs], [t2], out=t2[:, 0:n], in0=psR[:, 0:n], in1=cs[:, 1, 0:n], op=ALU.mult)
        P.I("pool", "tensor_tensor", [t1, t2], [dst_buf], out=dst_ap, in0=t1[:, 0:n], in1=t2[:, 0:n], op=ALU.add)

    P.phase_begin()
    hb = [P.sbuf("hb%d" % i, [128, 32, 512], BF16) for i in range(2)]
    wk = [P.sbuf("wk0", [128, 32, 512], BF16)]
    wv = [P.sbuf("wv0", [128, 32, 512], BF16)]
    csb = [P.sbuf("csb%d" % i, [128, 2, 512], F32) for i in range(2)]
    psK = [P.psum("psK%d" % i, [128, 512]) for i in range(4)]
    psV = [P.psum("psV%d" % i, [128, 512]) for i in range(2)]
    epsq = P.sbuf("epsq", [128, 1], F32)
    P.I("dve", "memset", [], [epsq], epsq[:, :], EPS6)
    psN_, psR_ = P.psum("psN", [128, 512]), P.psum("psR", [128, 512])
    t1_, t2_ = P.sbuf("t1", [128, 512], F32), P.sbuf("t2", [128, 512], F32)
    TT = [(P.sbuf("sq%d" % i, [128, 512], BF16), P.sbuf("sd%d" % i, [128, 512], F32), P.sbuf("kg%d" % i, [128, 512], BF16),
           t1_, t2_, psN_, psR_, epsq) for i in range(4)]
    kst = [P.sbuf("kst%d" % i, [128, 512], BF16) for i in range(4)]
    vst = [P.sbuf("vst%d" % i, [128, 4, 512], BF16) for i in range(2)]
    cnt = 0
    for g in range(4):
        P.I("pool", "dma_start", [WIN], [wk[0]], out=wk[0][:, :, :], in_=WINv[:, :, 2048 + g * 512:2048 + (g + 1) * 512], dma=wk[0])
        P.I("pool", "dma_start", [WIN], [wv[0]], out=wv[0][:, :, :], in_=WINv[:, :, 4096 + g * 512:4096 + (g + 1) * 512], dma=wv[0])
        for bl in range(17):
            s0 = bl * 512
            n = min(512, NS - s0)
            h = hb[cnt % 2]
            cs = csb[cnt % 2]
            cnt += 1
            P.I("sp", "dma_start", [H1T], [h], out=h[:, :, 0:n], in_=H1Tv[:, :, s0:s0 + n], dma=h)
            P.I("sp", "dma_start", [COS], [cs], out=cs[:, 0, 0:n], in_=COS[:, s0:s0 + n], dma=cs)
            P.I("sp", "dma_start", [SIN], [cs], out=cs[:, 1, 0:n], in_=SIN[:, s0:s0 + n], dma=cs, join=True)
            for hh in range(4):
                pk = psK[hh]
                for k in range(32):
                    P.I("pe", "matmul", [wk[0], h], [pk], pk[:, 0:n], lhsT=wk[0][:, k, hh * 128:(hh + 1) * 128], rhs=h[:, k, 0:n],
                        start=(k == 0), stop=(k == 31))
            vs_ = vst[bl % 2]
            ntile = n // 128

            def vtile(tt):
                if tt >= ntile:
                    return
                pv = psV[tt % 2]
                for k in range(32):
                    P.I("pe", "matmul", [wv[0], h], [pv], pv[:, :], lhsT=h[:, k, tt * 128:(tt + 1) * 128], rhs=wv[0][:, k, :],
                        start=(k == 0), stop=(k == 31))
                P.I("dve", "tensor_copy", [pv], [vs_], out=vs_[:, tt, :], in_=pv[:, :], join=True)

            for hh in range(4):
                qk_post_a(TT[hh], psK[hh], n, V("gk"))
                vtile(hh)
            for hh in range(4):
                ks = kst[hh]
                qk_post_b(TT[hh], n, cs, ks[:, 0:n], ks)
                P.I("pool", "dma_start", [ks], [KT], out=KT[g * 4 + hh, :, s0:s0 + n], in_=ks[:, 0:n], dma=ks, join=True)
            for hh in range(4):
                P.I("pool", "dma_start", [vs_], [VS], out=VS[g * 4 + hh, :, bl * 4:bl * 4 + ntile, :],
                    in_=vs_[:, 0:ntile, hh * 128:(hh + 1) * 128], dma=vs_, join=True)
    for g in range(4):
        P.I("pool", "dma_start", [WIN], [wk[0]], out=wk[0][:, :, :], in_=WINv[:, :, g * 512:(g + 1) * 512], dma=wk[0])
        for bl in range(5):
            s0 = QO + bl * 410
            n = 410
            h = hb[cnt % 2]
            cs = csb[cnt % 2]
            cnt += 1
            P.I("sp", "dma_start", [H1T], [h], out=h[:, :, 0:n], in_=H1Tv[:, :, s0:s0 + n], dma=h)
            P.I("sp", "dma_start", [COS], [cs], out=cs[:, 0, 0:n], in_=COS[:, s0:s0 + n], dma=cs)
            P.I("sp", "dma_start", [SIN], [cs], out=cs[:, 1, 0:n], in_=SIN[:, s0:s0 + n], dma=cs, join=True)
            for hh in range(4):
                pk = psK[hh]
                for k in range(32):
                    P.I("pe", "matmul", [wk[0], h], [pk], pk[:, 0:n], lhsT=wk[0][:, k, hh * 128:(hh + 1) * 128], rhs=h[:, k, 0:n],
                        start=(k == 0), stop=(k == 31))
            for hh in range(4):
                qk_post_a(TT[hh], psK[hh], n, sml[:, 0:1])
            for hh in range(4):
                ks = kst[hh]
                qk_post_b(TT[hh], n, cs, ks[:, 0:n], ks)
                P.I("pool", "dma_start", [ks], [QT], out=QT[g * 4 + hh, :, bl * 410:bl * 410 + n], in_=ks[:, 0:n], dma=ks, join=True)
    P.phase_end()
    if stop_after <= 2:
        return _finish(P, nc, OUT, dbg, {})

    P.phase_begin()
    val = P.sbuf("val", [128, NEXT], F32)
    P.I("sp", "dma_start", [VAL], [val], out=val.full(), in_=VAL.full(), dma=val)
    P.I("dve", "tensor_copy", [val], [sml], out=sml[:, 8:9], in_=val[:, 63:64])
    P.I("dve", "tensor_copy", [val], [sml], out=sml[:, 9:10], in_=val[:, 2112:2113])
    hb = [P.sbuf("hb%d" % i, [128, 32, 416], BF16) for i in range(2)]
    wc = [P.sbuf("wc%d" % i, [128, 32, 256], BF16) for i in range(2)]
    psA = [P.psum("psA%d" % i, [128, 512]) for i in range(2)]
    psG = [P.psum("psG%d" % i, [128, 512]) for i in range(2)]
    sg = [P.sbuf("sg%d" % i, [128, 416], F32) for i in range(2)]
    ub = [P.sbuf("ub%d" % i, [128, NU], BF16) for i in range(2)]
    dg = [P.sbuf("dg%d" % i, [128, 31, 128], BF16) for i in range(2)]
    psC = [P.psum("psC%d" % i, [128, 512]) for i in range(2)]
    acc = [P.sbuf("acc%d" % i, [128, NQ], F32) for i in range(2)]
    ysq = P.sbuf("ysq", [128, NQ], F32)
    s1 = P.sbuf("s1", [128, NQ], F32)
    s2 = P.sbuf("s2", [128, NQ], F32)
    P.I("dve", "memset", [], [s1], s1[:, :], 0.0)
    P.I("dve", "memset", [], [s2], s2[:, :], 0.0)
    cnt = 0
    for c in range(16):
        w = wc[c % 2]
        P.I("pool", "dma_start", [WIN], [w], out=w[:, :, 0:128], in_=WINv[:, :, 6144 + c * 128:6144 + (c + 1) * 128], dma=w)
        P.I("pool", "dma_start", [WIN], [w], out=w[:, :, 128:256], in_=WINv[:, :, 8192 + c * 128:8192 + (c + 1) * 128], dma=w, join=True)
        u = ub[c % 2]
        for bl in range(5):
            s0 = UO + bl * 416
            n = 416
            h = hb[cnt % 2]
            pa, pg, sgi = psA[cnt % 2], psG[cnt % 2], sg[cnt % 2]
            cnt += 1
            P.I("sp", "dma_start", [H1T], [h], out=h[:, :, 0:n], in_=H1Tv[:, :, s0:s0 + n], dma=h)
            for k in range(32):
                P.I("pe", "matmul", [w, h], [pa], pa[:, 0:n], lhsT=w[:, k, 0:128], rhs=h[:, k, 0:n], start=(k == 0), stop=(k == 31))
            for k in range(32):
                P.I("pe", "matmul", [w, h], [pg], pg[:, 0:n], lhsT=w[:, k, 128:256], rhs=h[:, k, 0:n], start=(k == 0), stop=(k == 31))
            P.I("act", "activation", [pg], [sgi], out=sgi[:, 0:n], in_=pg[:, 0:n], func=AF.Sigmoid)
            P.I("dve", "tensor_tensor", [pa, sgi], [u], out=u[:, bl * 416:bl * 416 + n], in0=pa[:, 0:n], in1=sgi[:, 0:n], op=ALU.mult, join=True)
        P.I("dve", "tensor_tensor", [u, val], [u], out=u[:, 0:16], in0=u[:, 0:16], in1=val[:, 48:64], op=ALU.mult)
        P.I("dve", "tensor_tensor", [u, val], [u], out=u[:, 2064:2080], in0=u[:, 2064:2080], in1=val[:, 2112:2128], op=ALU.mult)
        a = acc[c % 2]
        dgc = dg[c % 2]
        for k in range(31):
            P.I("dve", "tensor_scalar", [cb, vec], [dgc], out=dgc[:, k, :], in0=identb, scalar1=V("cw", k * 16 + c), scalar2=None,
                op0=ALU.mult, join=True)
        for bl in range(5):
            pc = psC[bl % 2]
            for k in range(31):
                P.I("pe", "matmul", [dgc, u], [pc], pc[:, 0:410], lhsT=dgc[:, k, :], rhs=u[:, bl * 410 + k:bl * 410 + k + 410],
                    start=(k == 0), stop=(k == 30))
            P.I("act", "activation", [pc, vec], [a], out=a[:, bl * 410:(bl + 1) * 410], in_=pc[:, 0:410], func=AF.Identity,
                bias=V("cb", c), scale=1.0, join=True)
        P.I("pool", "tensor_tensor", [s1, a], [s1], out=s1[:, :], in0=s1[:, :], in1=a[:, :], op=ALU.add)
        P.I("act", "activation", [a], [ysq], out=ysq[:, :], in_=a[:, :], func=AF.Square)
        P.I("pool", "tensor_tensor", [s2, ysq], [s2], out=s2[:, :], in0=s2[:, :], in1=ysq[:, :], op=ALU.add)
        P.I("pool", "dma_start", [a], [YS], out=YS[c, :, :], in_=a[:, :], dma=a, join=True)
    mean = P.sbuf("mean", [128, NQ], F32)
    rstd = P.sbuf("rstdc", [128, NQ], F32)
    eps5 = P.sbuf("eps5", [128, 1], F32)
    P.I("dve", "memset", [], [eps5], eps5[:, :], EPS5)
    for bl in range(5):
        sl = slice(bl * 410, bl * 410 + 410)
        pa, pg = psA[bl % 2], psG[bl % 2]
        P.I("pe", "matmul", [s1, cf], [pa], pa[:, 0:410], lhsT=onesDf, rhs=s1[:, sl], start=True, stop=True)
        P.I("pe", "matmul", [s2, cf], [pg], pg[:, 0:410], lhsT=onesDf, rhs=s2[:, sl], start=True, stop=True)
        P.I("act", "activation", [pa], [mean], out=mean[:, sl], in_=pa[:, 0:410], func=AF.Identity, scale=1.0 / 2048, join=True)
        P.I("act", "activation", [pg], [rstd], out=rstd[:, sl], in_=pg[:, 0:410], func=AF.Identity, scale=1.0 / 2048, join=True)
    P.I("dve", "tensor_tensor", [mean], [ysq], out=ysq[:, :], in0=mean[:, :], in1=mean[:, :], op=ALU.mult)
    P.I("dve", "tensor_tensor", [rstd, ysq], [rstd], out=rstd[:, :], in0=rstd[:, :], in1=ysq[:, :], op=ALU.subtract)
    P.I("act", "activation", [rstd, eps5], [rstd], out=rstd[:, :], in_=rstd[:, :], func=AF.Sqrt, bias=eps5[:, 0:1], scale=1.0)
    P.I("dve", "reciprocal", [rstd], [rstd], out=rstd[:, :], in_=rstd[:, :])
    P.I("dve", "tensor_tensor", [mean, rstd], [mean], out=mean[:, :], in0=mean[:, :], in1=rstd[:, :], op=ALU.mult)
    cst = [P.sbuf("cst%d" % i, [128, NQ], BF16) for i in range(2)]
    for c in range(16):
        a = acc[c % 2]
        P.I("sp", "dma_start", [YS], [a], out=a[:, :], in_=YS[c, :, :], dma=a)
        P.I("dve", "tensor_tensor", [a, rstd], [a], out=a[:, :], in0=a[:, :], in1=rstd[:, :], op=ALU.mult)
        P.I("pool", "tensor_tensor", [a, mean], [a], out=a[:, :], in0=a[:, :], in1=mean[:, :], op=ALU.subtract)
        o = cst[c % 2]
        P.I("dve", "tensor_scalar", [a, vec], [a], out=a[:, :], in0=a[:, :], scalar1=V("lg", c), scalar2=V("lb", c), op0=ALU.mult, op1=ALU.add)
        P.I("act", "activation", [a], [o], out=o[:, :], in_=a[:, :], func=AF.Silu)
        P.I("pool", "dma_start", [o], [CAT], out=CAT[16 + c, :, :], in_=o[:, :], dma=o, join=True)
    P.phase_end()
    if stop_after <= 3:
        return _finish(P, nc, OUT, dbg, {})

    P.phase_begin()
    kt = [P.sbuf("kt%d" % i, [128, NS], BF16) for i in range(2)]
    vt = [P.sbuf("vt%d" % i, [128, NT, 128], BF16) for i in range(2)]
    qt = [P.sbuf("qt%d" % i, [128, NQ], BF16) for i in range(2)]
    psS = [P.psum("psS%d" % i, [128, 2, 512]) for i in range(2)]
    psO = [P.psum("psO%d" % i, [128, 512]) for i in range(2)]
    psL = [P.psum("psL%d" % i, [128, 512]) for i in range(2)]
    pT = [P.sbuf("pT%d" % i, [128, 2, 410], BF16) for i in range(3)]
    rr = [P.sbuf("rr%d" % i, [128, 410], F32) for i in range(2)]
    o_ = P.sbuf("o_", [128, 410], F32)
    r32 = P.sbuf("r32", [128, 410], F32)
    osq = P.sbuf("osq", [128, 410], F32)
    ast = [P.sbuf("ast%d" % i, [128, 410], BF16) for i in range(2)]
    eps6 = P.sbuf("eps6a", [128, 1], F32)
    P.I("dve", "memset", [], [eps6], eps6[:, :], EPS6)
    stf = [P.sbuf("stf%d" % i, [128, 2048], F32) for i in range(3)]
    stb = [P.sbuf("stb%d" % i, [128, 2048], BF16) for i in range(3)]
    csteps = []
    for kk in range(32):
        for half in range(2):
            for c6 in range(6):
                jg0 = c6 * 8
                nj = min(8, 43 - jg0)
                src = WUP[kk * 128:(kk + 1) * 128, half * DFF + jg0 * 256:half * DFF + (jg0 + nj) * 256]
                dst = WUPB.ap[jg0:jg0 + nj, :, kk, half * 256:(half + 1) * 256].rearrange("j p c -> p j c")
                csteps.append((WUP, src, WUPB, dst, nj, 256))
    for kk in range(NJ):
        for hh in range(2):
            src = WDN[kk * 128:(kk + 1) * 128, hh * 2048:(hh + 1) * 2048]
            dst = WDNB.ap[hh * 16:(hh + 1) * 16, :, kk, :].rearrange("m p c -> p m c")
            csteps.append((WDN, src, WDNB, dst, 16, 128))
    cstate = [0]

    def conv_store(i):
        SB, src, DB, dst, nj, w = csteps[i]
        b_ = stb[i % 3]
        P.I("sp", "dma_start", [b_], [DB], out=dst, in_=b_[:, 0:nj * w].rearrange("p (j c) -> p j c", c=w), dma=b_, join=True)

    def conv_advance():
        i = cstate[0]
        if i > len(csteps):
            return
        if i < len(csteps):
            SB, src, DB, dst, nj, w = csteps[i]
            f_, b_ = stf[i % 3], stb[i % 3]
            P.I("sp", "dma_start", [SB], [f_], out=f_[:, 0:nj * w], in_=src, dma=f_)
            P.I("pool", "tensor_copy", [f_], [b_], out=b_[:, 0:nj * w], in_=f_[:, 0:nj * w])
        if i >= 1:
            conv_store(i - 1)
        cstate[0] = i + 1

    ucnt = 0
    for head in range(16):
        hi = head % 2
        k_, v_, q_ = kt[hi], vt[hi], qt[hi]
        P.I("sp", "dma_start", [KT], [k_], out=k_[:, :], in_=KT[head, :, :], dma=k_)
        P.I("sp", "dma_start", [VS], [v_], out=v_[:, :, :], in_=VS[head, :, :, :], dma=v_)
        P.I("sp", "dma_start", [QT], [q_], out=q_[:, :], in_=QT[head, :, :], dma=q_)
        for qb in range(5):
            qc = qb * 410
            n = 410
            pend = None
            for t in range(NT + 1):
                if t < NT:
                    ps = psS[ucnt % 2]
                    pt_ = pT[ucnt % 3]
                    ucnt += 1
                    P.I("pe", "matmul", [k_, q_], [ps], ps[:, 0, 0:n], lhsT=k_[0:64, t * 128:(t + 1) * 128], rhs=q_[0:64, qc:qc + n],
                        start=True, stop=True)
                    P.I("pe", "matmul", [k_, q_], [ps], ps[:, 1, 0:n], lhsT=k_[64:128, t * 128:(t + 1) * 128], rhs=q_[64:128, qc:qc + n],
                        start=True, stop=True)
                    P.I("act", "activation", [ps, kb], [pt_], out=pt_[:, :, :], in_=ps[:, :, 0:n], func=AF.Exp, bias=kb[:, t:t + 1], scale=1.0)
                    cur = (t, pt_)
                    if ucnt % 9 == 0:
                        conv_advance()
                else:
                    cur = None
                if pend is not None:
                    tp, pp = pend
                    st, sp_ = (tp == 0), (tp == NT - 1)
                    for s in range(2):
                        P.I("pe", "matmul", [v_, pp], [psO[s]], psO[s][:, 0:n], lhsT=v_[:, tp, :], rhs=pp[:, s, :], start=st, stop=sp_)
                    for s in range(2):
                        P.I("pe", "matmul", [cb, pp], [psL[s]], psL[s][32 * s:32 * s + 32, 0:n], lhsT=onesb[:, 32 * s:32 * s + 32], rhs=pp[:, s, :],
                            start=st, stop=sp_, tile_position=(0, 32 * s))
                pend = cur
            for s in range(2):
                P.I("dve", "reciprocal", [psL[s]], [r32], out=r32[32 * s:32 * s + 32, :], in_=psL[s][32 * s:32 * s + 32, 0:n], join=True)
            for s in range(2):
                P.I("pe", "matmul", [r32, cf], [psL[s]], psL[s][:, 0:n], lhsT=onesDf[32 * s:32 * s + 32, :], rhs=r32[32 * s:32 * s + 32, :],
                    start=True, stop=True)
            for s in range(2):
                P.I("act", "activation", [psL[s]], [rr[s]], out=rr[s][:, :], in_=psL[s][:, 0:n], func=AF.Identity, scale=1.0 / 32)
                P.I("dve", "tensor_tensor", [psO[s], rr[s]], [rr[s]], out=rr[s][:, :], in0=psO[s][:, 0:n], in1=rr[s][:, :], op=ALU.mult)
            P.I("dve", "scalar_tensor_tensor", [rr[0], rr[1], sml], [o_], out=o_[:, :], in0=rr[1][:, :], scalar=sml[:, 2:3], in1=rr[0][:, :],
                op0=ALU.mult, op1=ALU.add)
            P.I("act", "activation", [o_], [osq], out=osq[:, :], in_=o_[:, :], func=AF.Square)
            P.I("pe", "matmul", [osq, cf], [psL[0]], psL[0][:, 0:n], lhsT=ones128f, rhs=osq[:, :], start=True, stop=True)
            P.I("act", "activation", [psL[0], eps6], [osq], out=osq[:, :], in_=psL[0][:, 0:n], func=AF.Sqrt, bias=eps6[:, 0:1], scale=1.0)
            P.I("dve", "reciprocal", [osq], [osq], out=osq[:, :], in_=osq[:, :])
            a_ = ast[(head * 5 + qb) % 2]
            P.I("dve", "scalar_tensor_tensor", [o_, osq, sml], [a_], out=a_[:, :], in0=o_[:, :], scalar=sml[:, 1:2], in1=osq[:, :],
                op0=ALU.mult, op1=ALU.mult)
            P.I("sp", "dma_start", [a_], [CAT], out=CAT[head, :, qc:qc + n], in_=a_[:, :], dma=a_, join=True)
    while cstate[0] <= len(csteps):
        conv_advance()
    P.phase_end()
    if stop_after <= 4:
        return _finish(P, nc, OUT, dbg, {})

    P.phase_begin()
    cbk = [P.sbuf("cbk%d" % i, [128, 32, 410], BF16) for i in range(2)]
    wo = [P.sbuf("wo%d" % i, [128, 32, 512], BF16) for i in range(2)]
    xtb = [P.sbuf("xtb%d" % i, [128, 4, 410], F32) for i in range(2)]
    x1b = [P.sbuf("x1b%d" % i, [128, 4, 410], F32) for i in range(2)]
    psM = [P.psum("psM%d" % i, [128, 512]) for i in range(4)]
    sqb = P.sbuf("sqb", [128, 4, 410], F32)
    s2 = P.sbuf("s2n", [128, NQ], F32)
    P.I("dve", "memset", [], [s2], s2[:, :], 0.0)
    CATv = CAT.ap.rearrange("c p s -> p c s")
    X1Tv = X1T.ap.rearrange("c p s -> p c s")
    cnt = 0
    for mg in range(8):
        w = wo[mg % 2]
        P.I("pool", "dma_start", [WOUT], [w], out=w[:, :, :], in_=WOUTv[:, :, mg * 512:(mg + 1) * 512], dma=w)
        for bl in range(5):
            sl = slice(bl * 410, bl * 410 + 410)
            cbl, xb, x1 = cbk[cnt % 2], xtb[cnt % 2], x1b[cnt % 2]
            cnt += 1
            P.I("sp", "dma_start", [CAT], [cbl], out=cbl[:, :, :], in_=CATv[:, :, sl], dma=cbl)
            P.I("sp", "dma_start", [XT], [xb], out=xb[:, :, :], in_=XTv[:, mg * 4:(mg + 1) * 4, QO + bl * 410:QO + bl * 410 + 410], dma=xb)
            for m in range(4):
                pm = psM[m]
                for k in range(32):
                    P.I("pe", "matmul", [w, cbl], [pm], pm[:, 0:410], lhsT=w[:, k, m * 128:(m + 1) * 128], rhs=cbl[:, k, :], start=(k == 0), stop=(k == 31))
                P.I("dve", "scalar_tensor_tensor", [pm, xb, mod], [x1], out=x1[:, m, :], in0=pm[:, 0:410], scalar=gate1[:, mg * 4 + m:mg * 4 + m + 1],
                    in1=xb[:, m, :], op0=ALU.mult, op1=ALU.add, join=True)
            P.I("act", "activation", [x1], [sqb], out=sqb[:, :, :], in_=x1[:, :, :], func=AF.Square)
            for m in range(4):
                P.I("pool", "tensor_tensor", [s2, sqb], [s2], out=s2[:, sl], in0=s2[:, sl], in1=sqb[:, m, :], op=ALU.add)
            P.I("pool", "dma_start", [x1], [X1T], out=X1Tv[:, mg * 4:(mg + 1) * 4, sl], in_=x1[:, :, :], dma=x1, join=True)
    eps6b = P.sbuf("eps6b", [128, 1], F32)
    P.I("dve", "memset", [], [eps6b], eps6b[:, :], EPS6)
    for bl in range(5):
        sl = slice(bl * 410, bl * 410 + 410)
        pm = psM[bl % 4]
        P.I("pe", "matmul", [s2, cf], [pm], pm[:, 0:410], lhsT=onesDf, rhs=s2[:, sl], start=True, stop=True)
        P.I("act", "activation", [pm, eps6b], [rstd2], out=rstd2[:, sl], in_=pm[:, 0:410], func=AF.Sqrt, bias=eps6b[:, 0:1], scale=1.0 / D, join=True)
    P.I("dve", "reciprocal", [rstd2], [rstd2], out=rstd2[:, :], in_=rstd2[:, :])
    P.phase_end()
    if stop_after <= 5:
        return _finish(P, nc, OUT, dbg, {})

    P.phase_begin()
    h2 = P.sbuf("h2", [128, 32, 412], BF16)
    x1s = [P.sbuf("x1s%d" % i, [128, 4, 412], F32) for i in range(1)]
    Ab = P.sbuf("Ab", [128, NJ, 410], BF16)
    wb = [P.sbuf("wb%d" % i, [128, 32, 512], BF16) for i in range(2)]
    psU = [P.psum("psU%d" % i, [128, 512]) for i in range(2)]
    psGt = [P.psum("psGt%d" % i, [128, 512]) for i in range(2)]
    psY = [P.psum("psY%d" % i, [128, 512]) for i in range(2)]
    psT = [P.psum("psT%d" % i, [128, 4, 128]) for i in range(2)]
    cu = [P.sbuf("cu%d" % i, [128, 410], F32) for i in range(2)]
    cg = [P.sbuf("cg%d" % i, [128, 410], F32) for i in range(2)]
    sgt = [P.sbuf("sgt%d" % i, [128, 410], F32) for i in range(2)]
    x1r = [P.sbuf("x1r%d" % i, [128, 410], F32) for i in range(2)]
    yT = [P.sbuf("yT%d" % i, [128, 410], F32) for i in range(2)]
    ost = [P.sbuf("ost%d" % i, [128, 4, 128], F32) for i in range(2)]
    b2v = mod.ap
    wcnt = 0
    def prep_steps(b):
        c0 = b * 410
        nin = min(412, NQ - c0)
        steps = []

        def mk(cg4):
            def f():
                xs_ = x1s[0]
                P.I("pool", "dma_start", [X1T], [xs_], out=xs_[:, :, 0:nin], in_=X1Tv[:, cg4 * 4:(cg4 + 1) * 4, c0:c0 + nin], dma=xs_)
                for c4 in range(4):
                    c = cg4 * 4 + c4
                    P.I("dve", "tensor_tensor", [xs_, rstd2], [xs_], out=xs_[:, c4, 0:nin], in0=xs_[:, c4, 0:nin], in1=rstd2[:, c0:c0 + nin],
                        op=ALU.mult, join=True)
                    P.I("dve", "tensor_scalar", [xs_, der, mod], [h2], out=h2[:, c, 0:nin], in0=xs_[:, c4, 0:nin], scalar1=A2[:, c:c + 1],
                        scalar2=B2[:, c:c + 1], op0=ALU.mult, op1=ALU.add, join=True)
            return f

        for cg4 in range(8):
            steps.append(mk(cg4))

        def masks():
            if b == 0:
                for c in range(32):
                    P.I("dve", "tensor_scalar", [h2, sml], [h2], out=h2[:, c, 0:1], in0=h2[:, c, 0:1], scalar1=sml[:, 8:9], scalar2=None,
                        op0=ALU.mult, join=True)
            if b == 4:
                for c in range(32):
                    P.I("dve", "tensor_scalar", [h2, sml], [h2], out=h2[:, c, nin - 1:nin], in0=h2[:, c, nin - 1:nin], scalar1=sml[:, 9:10],
                        scalar2=None, op0=ALU.mult, join=True)
        steps.append(masks)
        return steps

    for f_ in prep_steps(0):
        f_()
    for b in range(5):
        c0 = b * 410
        nin = min(412, NQ - c0)
        nout = nin - 2
        for jg in range(43):
            w = wb[wcnt % 2]
            wcnt += 1
            P.I("sp", "dma_start", [WUPB], [w], out=w[:, :, :], in_=WUPB[jg, :, :, :], dma=w)
            for jj in range(2):
                j = jg * 2 + jj
                pu, pg = psU[jj], psGt[jj]
                for k in range(32):
                    P.I("pe", "matmul", [w, h2], [pu], pu[:, 0:nin], lhsT=w[:, k, jj * 128:(jj + 1) * 128], rhs=h2[:, k, 0:nin], start=(k == 0), stop=(k == 31))
                for k in range(32):
                    P.I("pe", "matmul", [w, h2], [pg], pg[:, 0:nin], lhsT=w[:, k, 256 + jj * 128:256 + (jj + 1) * 128], rhs=h2[:, k, 0:nin],
                        start=(k == 0), stop=(k == 31))
                for (pp, dst, fo) in ((pu, cu[jj], j), (pg, cg[jj], NJ + j)):
                    P.I("dve", "tensor_scalar", [pp, vec], [dst], out=dst[:, 0:nout], in0=pp[:, 0:nout], scalar1=V("fw", 0 * 172 + fo), scalar2=V("fb", fo),
                        op0=ALU.mult, op1=ALU.add)
                    for kk in (1, 2):
                        P.I("dve", "scalar_tensor_tensor", [pp, vec, dst], [dst], out=dst[:, 0:nout], in0=pp[:, kk:kk + nout], scalar=V("fw", kk * 172 + fo),
                            in1=dst[:, 0:nout], op0=ALU.mult, op1=ALU.add)
                P.I("act", "activation", [cg[jj]], [sgt[jj]], out=sgt[jj][:, 0:nout], in_=cg[jj][:, 0:nout], func=AF.Silu)
                P.I("pool", "tensor_tensor", [sgt[jj], cu[jj]], [Ab], out=Ab[:, j, 0:nout], in0=sgt[jj][:, 0:nout], in1=cu[jj][:, 0:nout], op=ALU.mult, join=True)
        nxt = prep_steps(b + 1) if b + 1 < 5 else []
        ntl = (nout + 127) // 128
        pend_ep = None
        for m in range(32):
            w = wb[wcnt % 2]
            wcnt += 1
            wd = w.ap.rearrange("p a b -> p (a b)")[:, 0:NJ * 128].rearrange("p (k n) -> p k n", n=128)
            P.I("sp", "dma_start", [WDNB], [w], out=wd, in_=WDNB[m, :, :, :], dma=w)
            xr = x1r[m % 2]
            P.I("pool", "dma_start", [X1T], [xr], out=xr[:, 0:nout], in_=X1T[m, :, c0 + 1:c0 + 1 + nout], dma=xr)
            py = psY[m % 2]
            for k in range(NJ):
                P.I("pe", "matmul", [w, Ab], [py], py[:, 0:nout], lhsT=wd[:, k, :], rhs=Ab[:, k, 0:nout], start=(k == 0), stop=(k == NJ - 1))
            y = yT[m % 2]
            P.I("dve", "scalar_tensor_tensor", [py, xr, mod], [y], out=y[:, 0:nout], in0=py[:, 0:nout], scalar=gate2[:, m:m + 1], in1=xr[:, 0:nout],
                op0=ALU.mult, op1=ALU.add)
            def epilogue(m=m, y=y, c0=c0, nout=nout, ntl=ntl):
                pt_ = psT[m % 2]
                os_ = ost[m % 2]
                for tl in range(ntl):
                    nt_ = min(128, nout - tl * 128)
                    P.I("pe", "transpose", [y, cf], [pt_], out=pt_[0:nt_, tl, :], in_=y[:, tl * 128:tl * 128 + nt_], identity=identf)
                nfull = nout // 128
                P.I("act", "copy", [pt_], [os_], out=os_[:, 0:nfull, :], in_=pt_[:, 0:nfull, :])
                r0 = c0
                P.I("pool", "dma_start", [os_], [OUT], out=OUT.ap[r0:r0 + nfull * 128, m * 128:(m + 1) * 128].rearrange("(t p) f -> p t f", p=128),
                    in_=os_[:, 0:nfull, :], dma=os_, join=True)
                rem = nout - nfull * 128
                if rem:
                    P.I("dve", "tensor_copy", [pt_], [os_], out=os_[0:rem, nfull, :], in_=pt_[0:rem, nfull, :], join=True)
                    P.I("pool", "dma_start", [os_], [OUT], out=OUT.ap[r0 + nfull * 128:r0 + nout, m * 128:(m + 1) * 128], in_=os_[0:rem, nfull, :],
                        dma=os_, join=True)
            if pend_ep is not None:
                pend_ep()
            pend_ep = epilogue
            if m % 2 == 1 and nxt:
                nxt.pop(0)()
        pend_ep()
        pend_ep = None
        while nxt:
            nxt.pop(0)()
    P.wait_only("sp", [OUT])
    P.phase_end(final=True)
    P.ctx.close()
    return nc


def _finish(P, nc, OUT, dbg, taps):
    P.phase_begin()
    outs = [OUT]
    for name, buf in taps.items():
        shp = list(buf.ap.shape)
        t = Buf(nc.dram_tensor("tap_" + name, shp, buf.ap.dtype, kind="ExternalOutput"), "tap_" + name)
        P.I("sp", "dma_start", [buf], [t], out=t.full(), in_=buf.full(), dma=buf)
        outs.append(t)
    P.wait_only("sp", outs)
    P.phase_end(final=True)
    P.ctx.close()
    return nc


def _rope_tables():
    rows = SEQ // 64
    row = np.repeat(np.arange(rows), 64).astype(np.float32)
    col = np.tile(np.arange(64), rows).astype(np.float32)
    freqs = (np.float32(10000.0) ** (-np.arange(16, dtype=np.float32) / np.float32(16))).astype(np.float32)
    ang = np.concatenate([row[:, None] * freqs, col[:, None] * freqs], axis=-1).astype(np.float32)
    return np.cos(ang).astype(np.float32), np.sin(ang).astype(np.float32)


def _core_layout(qi):
    T0 = qi * 2048
    slot_tok = np.full(NS, -1, dtype=np.int64)
    ext = np.arange(T0 - 64, T0 + 2112)
    ok = (ext >= 0) & (ext < SEQ)
    slot_tok[0:NEXT] = np.where(ok, ext, -1)
    slot_tok[NEXT:NEXT + NCTX] = -2 - np.arange(NCTX)
    lo, hi = max(T0 - 64, 0), min(T0 + 2112, SEQ)
    others = np.concatenate([np.arange(0, lo), np.arange(hi, SEQ)])
    slot_tok[NEXT + NCTX:NEXT + NCTX + len(others)] = others
    return slot_tok


def _pm(v, n):
    return np.ascontiguousarray(np.asarray(v, dtype=np.float32).reshape(n, 128).T)


def make_in_maps(x, c, ctx, c_ctx, w_ada, b_ada, norm1_g, norm2_g, w_in, q_norm_g, k_norm_g,
                 lambda_q1, lambda_k1, lambda_q2, lambda_k2, subln_g, conv_dw_w, conv_dw_b,
                 conv_ln_g, conv_ln_b, w_out, w_up, ffn_dw_w, ffn_dw_b, w_down, cores=range(8)):
    x = np.asarray(x, dtype=np.float32)
    ctx = np.asarray(ctx, dtype=np.float32)
    cosT, sinT = _rope_tables()
    eye = np.eye(128, dtype=np.float32)
    sw = np.zeros((128, 128), np.float32)
    idx = np.arange(128)
    sw[idx, idx ^ 1] = 1.0
    o64 = np.zeros((128, 128), np.float32)
    o64[:64, :64] = 1.0 / 64
    o64[64:, 64:] = 1.0 / 64
    constb = np.concatenate([eye, sw, o64, np.ones((128, 128), np.float32)], axis=1).astype(ml_dtypes.bfloat16)
    constf = np.concatenate([eye, np.full((128, 128), 1.0 / 128, np.float32), np.ones((128, 128), np.float32)], axis=1)
    p = np.arange(128)
    pair = (p % 64) // 2
    sgn = np.where(p % 2 == 0, -1.0, 1.0).astype(np.float32)

    base = np.zeros((128, NV), np.float32)

    def put(name, arr):
        arr = np.asarray(arr, np.float32)
        base[:, _VOFF[name]:_VOFF[name] + arr.shape[1]] = arr

    put("g1", _pm(norm1_g[0], 32))
    put("g2", _pm(norm2_g[0], 32))
    put("bada", _pm(b_ada[0], 192))
    put("gq", np.asarray(q_norm_g[0], np.float32)[p % 64][:, None])
    put("gk", np.asarray(k_norm_g[0], np.float32)[p % 64][:, None])
    put("gs", np.asarray(subln_g[0], np.float32)[:, None])
    cw = np.asarray(conv_dw_w[0], np.float32)
    put("cw", cw.reshape(31, 16, 128).transpose(2, 0, 1).reshape(128, 31 * 16))
    put("cb", _pm(conv_dw_b[0], 16))
    put("lg", _pm(conv_ln_g[0], 16))
    put("lb", _pm(conv_ln_b[0], 16))
    fw = np.asarray(ffn_dw_w[0], np.float32)
    put("fw", fw.reshape(3, 172, 128).transpose(2, 0, 1).reshape(128, 3 * 172))
    put("fb", _pm(ffn_dw_b[0], 172))
    put("ccT", _pm(c_ctx, 32))
    lamv = np.concatenate([np.asarray(v[0], np.float32) for v in (lambda_q1, lambda_k1, lambda_q2, lambda_k2)])
    put("lam", np.broadcast_to(lamv[None, :], (128, 256)))

    wa2, wi2, wo2, wu2, wd2 = (np.asarray(w[0], dtype=np.float32) for w in (w_ada, w_in, w_out, w_up, w_down))
    in_maps = []
    for core in cores:
        b, qi = core // 4, core % 4
        st = _core_layout(qi)
        xs = np.zeros((NS, D), np.float32)
        mx = st >= 0
        xs[mx] = x[b][st[mx]]
        mc = st <= -2
        xs[mc] = ctx[b][-2 - st[mc]]
        validslot = (st != -1)
        keybias = np.where(validslot, 0.0, -30000.0).astype(np.float32).reshape(NT, 128).T
        valid = np.broadcast_to(validslot[:NEXT].astype(np.float32)[None, :], (128, NEXT))
        cs = np.ones((128, NS), np.float32)
        sn = np.zeros((128, NS), np.float32)
        cs[:, mx] = cosT[st[mx]][:, pair].T
        sn[:, mx] = sinT[st[mx]][:, pair].T * sgn[:, None]
        vecs = base.copy()
        vecs[:, _VOFF["cT"]:_VOFF["cT"] + 32] = _pm(c[b], 32)
        in_maps.append({
            "xs": xs, "vecs": vecs, "keybias": np.ascontiguousarray(keybias), "valid": np.ascontiguousarray(valid),
            "cosT": cs, "sinT": sn, "constb": constb, "constf": constf,
            "w_ada": wa2, "w_in": wi2, "w_out": wo2, "w_up": wu2, "w_down": wd2,
        })
    return in_maps


def kernel(**inputs):
    in_maps = make_in_maps(**inputs)
    nc = build_program()
    res = run_bass_kernel_spmd(nc, in_maps, core_ids=list(range(8)))
    out = np.empty((2, SEQ, D), np.float32)
    for core in range(8):
        b, qi = core // 4, core % 4
        out[b, qi * 2048:(qi + 1) * 2048] = res.results[core]["out"]
    return out
```
